# Optimizing a Trainium2 kernel written in Bass

```python
import math
import jax
import jax.numpy as jnp
from jax import lax
import numpy as np

D_MODEL = 1024
BATCH = 32
SEQ = 256
DEPTH = 2
DEC_BATCH = 8
DEC_SEQ = 1024
PAST_LEN = 256

GRID_W = 64
ROPE_BASE = 10000.0
RMS_EPS = 1e-6
Q_BLOCK = 128

H_A = 8
QK_NOPE = 64
QK_ROPE = 32
V_HD_A = 64
Q_LORA = 256
KV_LORA = 128
H_B = 4
DIFF_HD = 64
DIFF_VD = 2 * DIFF_HD
H_C = 8
DK_C = 64
DV_C = D_MODEL // H_C
MLSTM_CHUNK = 64
I_BIAS_INIT = -2.0
F_BIAS_INIT = 3.0
D_FF = -(-8 * D_MODEL // (3 * 256)) * 256

IN_AB = Q_LORA + KV_LORA + QK_ROPE + 3 * H_B * 2 * DIFF_HD
OUT_AB = H_A * V_HD_A + H_B * DIFF_VD
IN_C = 2 * H_C * DK_C + 2 * H_C * DV_C + 4 * H_C
N_EVEN = (DEPTH + 1) // 2
N_ODD = DEPTH // 2

kernel_name = 'hybrid_mla_diff_mlstm_dit_step'


def rms_norm(x, g):
    xf = x.astype(jnp.float32)
    y = xf * lax.rsqrt(jnp.mean(xf * xf, axis=-1, keepdims=True) + RMS_EPS)
    return (y * g.astype(jnp.float32)).astype(x.dtype)


def rope1d(x, pos):
    half = x.shape[-1] // 2
    freqs = jnp.power(ROPE_BASE, -jnp.arange(half, dtype=jnp.float32) / half)
    ang = pos[:, None] * freqs[None, :]
    cos = jnp.cos(ang).astype(x.dtype)
    sin = jnp.sin(ang).astype(x.dtype)
    x1, x2 = x[..., :half], x[..., half:]
    return jnp.concatenate([x1 * cos - x2 * sin, x1 * sin + x2 * cos], axis=-1)


def rope2d(x, rows, cols):
    h = x.shape[-1] // 2
    return jnp.concatenate([rope1d(x[..., :h], rows), rope1d(x[..., h:], cols)], axis=-1)


def rope2d_pair(x, rows, cols):
    return jnp.concatenate([rope2d(x[..., :DIFF_HD], rows, cols), rope2d(x[..., DIFF_HD:], rows, cols)], axis=-1)


def grid_positions(n_tok):
    n_rows = n_tok // GRID_W
    t = jnp.arange(n_rows * GRID_W)
    return (t // GRID_W).astype(jnp.float32), (t % GRID_W).astype(jnp.float32)


def softmax_probs(q, k, scale):
    s = jnp.einsum('bhqd,bhkd->bhqk', q, k).astype(jnp.float32) * scale
    return jax.nn.softmax(s, axis=-1)


def map_query_blocks(fn, *qs):
    B, H, T, _ = qs[0].shape
    nb = T // Q_BLOCK
    blocks = tuple(q.reshape(B, H, nb, Q_BLOCK, q.shape[-1]).transpose(2, 0, 1, 3, 4) for q in qs)
    out = lax.map(lambda qb: fn(*qb), blocks)
    return out.transpose(1, 2, 0, 3, 4).reshape(B, H, T, out.shape[-1])


def adaln(cond, w, b):
    return jnp.split((jax.nn.silu(cond) @ w + b)[:, None, :], 6, axis=-1)


def modulate(x, g, shift, scale):
    return rms_norm(x, g) * (1.0 + scale) + shift


def swiglu(h, w_in, w_out):
    a, b = jnp.split(h @ w_in, 2, axis=-1)
    return (jax.nn.silu(a) * b) @ w_out


def even_mixer(h, w_in, g_ql, g_kvl, w_uq, w_ukv, lam_vec, g_sub, w_out, lam_init, pos=None, ctx=None):
    B, T, _ = h.shape
    c0 = Q_LORA
    c1 = c0 + KV_LORA
    c2 = c1 + QK_ROPE
    c3 = c2 + H_B * 2 * DIFF_HD
    c4 = c3 + H_B * 2 * DIFF_HD
    cq, ckv, kr, dq, dk, dv = jnp.split(h @ w_in, [c0, c1, c2, c3, c4], axis=-1)
    ckv = rms_norm(ckv, g_kvl)
    qa = (rms_norm(cq, g_ql) @ w_uq).reshape(B, T, H_A, QK_NOPE + QK_ROPE).transpose(0, 2, 1, 3)
    q_nope, q_rope = qa[..., :QK_NOPE], qa[..., QK_NOPE:]
    dq = dq.reshape(B, T, H_B, 2 * DIFF_HD).transpose(0, 2, 1, 3)
    dk = dk.reshape(B, T, H_B, 2 * DIFF_HD).transpose(0, 2, 1, 3)
    dv = dv.reshape(B, T, H_B, DIFF_VD).transpose(0, 2, 1, 3)
    own = (ckv, kr, dk, dv)
    if pos is None:
        ckv_all, kr_all, dk_all, dv_all = own
    else:
        rows, cols = pos
        q_rope = rope2d(q_rope, rows, cols)
        dq = rope2d_pair(dq, rows, cols)
        ckv_c, kr_c, dk_c, dv_c = ctx
        ckv_all = jnp.concatenate([ckv, ckv_c], axis=1)
        kr_all = jnp.concatenate([rope2d(kr, rows, cols), kr_c], axis=1)
        dk_all = jnp.concatenate([rope2d_pair(dk, rows, cols), dk_c], axis=2)
        dv_all = jnp.concatenate([dv, dv_c], axis=2)
    K = ckv_all.shape[1]
    kv = (ckv_all @ w_ukv).reshape(B, K, H_A, QK_NOPE + V_HD_A).transpose(0, 2, 1, 3)
    k_a = jnp.concatenate([kv[..., :QK_NOPE], jnp.broadcast_to(kr_all[:, None], (B, H_A, K, QK_ROPE))], axis=-1)
    v_a = kv[..., QK_NOPE:]
    q_a = jnp.concatenate([q_nope, q_rope], axis=-1)
    scale_a = (QK_NOPE + QK_ROPE) ** -0.5

    def mla_block(qb):
        p = softmax_probs(qb, k_a, scale_a)
        return jnp.einsum('bhqk,bhkd->bhqd', p.astype(v_a.dtype), v_a)

    out_a = map_query_blocks(mla_block, q_a)
    lv = lam_vec.astype(jnp.float32)
    lam = jnp.exp(jnp.sum(lv[0] * lv[1])) - jnp.exp(jnp.sum(lv[2] * lv[3])) + lam_init
    k1, k2 = dk_all[..., :DIFF_HD], dk_all[..., DIFF_HD:]
    scale_b = DIFF_HD ** -0.5

    def diff_block(q1, q2):
        p = softmax_probs(q1, k1, scale_b) - lam * softmax_probs(q2, k2, scale_b)
        return jnp.einsum('bhqk,bhkd->bhqd', p.astype(dv_all.dtype), dv_all)

    out_b = map_query_blocks(diff_block, dq[..., :DIFF_HD], dq[..., DIFF_HD:])
    out_b = rms_norm(out_b, g_sub) * (1.0 - lam_init)
    merged = jnp.concatenate([out_a.transpose(0, 2, 1, 3).reshape(B, T, H_A * V_HD_A),
                              out_b.transpose(0, 2, 1, 3).reshape(B, T, H_B * DIFF_VD)], axis=-1)
    return merged @ w_out, own


def mlstm_chunked(q, k, v, li, lf, C0, n0, m0):
    B, H, T, dk = q.shape
    dv = v.shape[-1]
    L = MLSTM_CHUNK
    nc = T // L
    f32 = jnp.float32
    qc = q.astype(f32).reshape(B, H, nc, L, dk)
    kc = k.astype(f32).reshape(B, H, nc, L, dk)
    vc = v.astype(f32).reshape(B, H, nc, L, dv)
    lic = li.reshape(B, H, nc, L)
    b = jnp.cumsum(lf.reshape(B, H, nc, L), axis=-1)
    g = b[..., -1]
    a = g[..., None] - b + lic
    a_max = jnp.max(a, axis=-1)
    w = jnp.exp(a - a_max[..., None])
    kv_chunk = jnp.einsum('bhcl,bhcld,bhcle->bhcde', w, kc, vc)
    kn_chunk = jnp.einsum('bhcl,bhcld->bhcd', w, kc)

    def step(carry, inp):
        C, n, m = carry
        g_j, am_j, kv_j, kn_j = inp
        m_new = jnp.maximum(g_j + m, am_j)
        s_old = jnp.exp(g_j + m - m_new)
        s_new = jnp.exp(am_j - m_new)
        C_new = s_old[..., None, None] * C + s_new[..., None, None] * kv_j
        n_new = s_old[..., None] * n + s_new[..., None] * kn_j
        return (C_new, n_new, m_new), (C, n, m)

    xs = (jnp.moveaxis(g, 2, 0), jnp.moveaxis(a_max, 2, 0), jnp.moveaxis(kv_chunk, 2, 0), jnp.moveaxis(kn_chunk, 2, 0))
    init = (C0.astype(f32), n0.astype(f32), m0.astype(f32))
    (Cf, nf, mf), (Cp, np_, mp) = lax.scan(step, init, xs)
    Cp = jnp.moveaxis(Cp, 0, 2)
    np_ = jnp.moveaxis(np_, 0, 2)
    mp = jnp.moveaxis(mp, 0, 2)
    mask = jnp.tril(jnp.ones((L, L), dtype=bool))
    dlog = jnp.where(mask, b[..., :, None] - b[..., None, :] + lic[..., None, :], -jnp.inf)
    m_inter = b + mp[..., None]
    m_t = jnp.maximum(m_inter, jnp.max(dlog, axis=-1))
    s = jnp.einsum('bhcld,bhcsd->bhcls', qc, kc) * jnp.exp(dlog - m_t[..., None])
    inter = jnp.exp(m_inter - m_t)
    num = jnp.einsum('bhcls,bhcse->bhcle', s, vc) + inter[..., None] * jnp.einsum('bhcld,bhcde->bhcle', qc, Cp)
    den = jnp.sum(s, axis=-1) + inter * jnp.einsum('bhcld,bhcd->bhcl', qc, np_)
    h = num / jnp.maximum(jnp.abs(den), jnp.exp(-m_t))[..., None]
    return h.reshape(B, H, T, dv), Cf, nf, mf


def odd_mixer(h, w_in, b_gate, g_norm, w_out, C0, n0, m0):
    B, T, _ = h.shape
    s0 = H_C * DK_C
    q, k, v, o, gates = jnp.split(h @ w_in, [s0, 2 * s0, 2 * s0 + H_C * DV_C, 2 * s0 + 2 * H_C * DV_C], axis=-1)

    def heads(x, d):
        return x.reshape(B, T, H_C, d).transpose(0, 2, 1, 3)

    q = heads(q, DK_C)
    k = heads(k, DK_C) * (DK_C ** -0.5)
    v = heads(v, DV_C)
    gts = (gates.astype(jnp.float32) + b_gate.astype(jnp.float32)).reshape(B, T, 4, H_C).transpose(2, 0, 3, 1)
    li_f, lf_f = gts[0], jax.nn.log_sigmoid(gts[1])
    li_b, lf_b = gts[2], jax.nn.log_sigmoid(gts[3])
    h_f, Cf, nf, mf = mlstm_chunked(q, k, v, li_f, lf_f, C0[:, 0], n0[:, 0], m0[:, 0])

    def flip(x):
        return jnp.flip(x, axis=2)

    h_b, Cb, nb, mb = mlstm_chunked(flip(q), flip(k), flip(v), flip(li_b), flip(lf_b), C0[:, 1], n0[:, 1], m0[:, 1])
    hs = rms_norm(h_f + flip(h_b), g_norm).transpose(0, 2, 1, 3).reshape(B, T, H_C * DV_C)
    y = (hs.astype(h.dtype) * jax.nn.sigmoid(o)) @ w_out
    return y, (jnp.stack([Cf, Cb], axis=1), jnp.stack([nf, nb], axis=1), jnp.stack([mf, mb], axis=1))


def setup_inputs(seed: int = 0) -> dict:
    key = jax.random.key(seed)
    ks = iter(jax.random.split(key, 40))

    def nrm(shape, scale=1.0):
        return scale * jax.random.normal(next(ks), shape, jnp.float32)

    def gain(shape):
        return 1.0 + nrm(shape, 0.05)

    gate_offsets = jnp.repeat(jnp.array([I_BIAS_INIT, F_BIAS_INIT, I_BIAS_INIT, F_BIAS_INIT], jnp.float32), H_C)
    return {
        'x_prompt': nrm((BATCH, SEQ, D_MODEL)),
        'x_sample': nrm((DEC_BATCH, DEC_SEQ, D_MODEL)),
        'cache_mla_ckv': nrm((DEC_BATCH, N_EVEN, PAST_LEN, KV_LORA)),
        'cache_mla_krope': nrm((DEC_BATCH, N_EVEN, PAST_LEN, QK_ROPE)),
        'cache_diff_k': nrm((DEC_BATCH, N_EVEN, H_B, PAST_LEN, 2 * DIFF_HD)),
        'cache_diff_v': nrm((DEC_BATCH, N_EVEN, H_B, PAST_LEN, DIFF_VD)),
        'state_mlstm_C': nrm((DEC_BATCH, N_ODD, 2, H_C, DK_C, DV_C), 0.3),
        'state_mlstm_n': nrm((DEC_BATCH, N_ODD, 2, H_C, DK_C), 0.3),
        'state_mlstm_m': nrm((DEC_BATCH, N_ODD, 2, H_C)),
        'c': nrm((DEC_BATCH, D_MODEL)),
        'c_ctx': nrm((D_MODEL,)),
        'w_ada': nrm((DEPTH, D_MODEL, 6 * D_MODEL), 0.5 * D_MODEL ** -0.5),
        'b_ada': nrm((DEPTH, 6 * D_MODEL), 0.1),
        'g_mix': gain((DEPTH, D_MODEL)),
        'g_ffn': gain((DEPTH, D_MODEL)),
        'w_ffn_in': nrm((DEPTH, D_MODEL, 2 * D_FF), D_MODEL ** -0.5),
        'w_ffn_out': nrm((DEPTH, D_FF, D_MODEL), D_FF ** -0.5),
        'w_in_ab': nrm((N_EVEN, D_MODEL, IN_AB), D_MODEL ** -0.5),
        'g_q_lora': gain((N_EVEN, Q_LORA)),
        'g_kv_lora': gain((N_EVEN, KV_LORA)),
        'w_uq': nrm((N_EVEN, Q_LORA, H_A * (QK_NOPE + QK_ROPE)), Q_LORA ** -0.5),
        'w_ukv': nrm((N_EVEN, KV_LORA, H_A * (QK_NOPE + V_HD_A)), KV_LORA ** -0.5),
        'diff_lambda': nrm((N_EVEN, 4, DIFF_HD), 0.1),
        'g_diff_subln': gain((N_EVEN, DIFF_VD)),
        'w_out_ab': nrm((N_EVEN, OUT_AB, D_MODEL), OUT_AB ** -0.5),
        'w_in_c': nrm((N_ODD, D_MODEL, IN_C), D_MODEL ** -0.5),
        'b_gate_c': gate_offsets[None, :] + nrm((N_ODD, 4 * H_C), 0.1),
        'g_mlstm': gain((N_ODD, DV_C)),
        'w_out_c': nrm((N_ODD, H_C * DV_C, D_MODEL), (H_C * DV_C) ** -0.5),
        'g_final': gain((D_MODEL,)),
    }


def reference(x_prompt, x_sample, cache_mla_ckv, cache_mla_krope, cache_diff_k, cache_diff_v,
              state_mlstm_C, state_mlstm_n, state_mlstm_m, c, c_ctx,
              w_ada, b_ada, g_mix, g_ffn, w_ffn_in, w_ffn_out,
              w_in_ab, g_q_lora, g_kv_lora, w_uq, w_ukv, diff_lambda, g_diff_subln, w_out_ab,
              w_in_c, b_gate_c, g_mlstm, w_out_c, g_final):
    pos = grid_positions(x_sample.shape[1])
    xp, xs = x_prompt, x_sample
    ckv_l, kr_l, dk_l, dv_l, C_l, n_l, m_l = [], [], [], [], [], [], []
    for l in range(DEPTH):
        j = l // 2
        p_sh1, p_sc1, p_g1, p_sh2, p_sc2, p_g2 = adaln(c_ctx[None], w_ada[l], b_ada[l])
        s_sh1, s_sc1, s_g1, s_sh2, s_sc2, s_g2 = adaln(c, w_ada[l], b_ada[l])
        hp = modulate(xp, g_mix[l], p_sh1, p_sc1)
        hs = modulate(xs, g_mix[l], s_sh1, s_sc1)
        if l % 2 == 0:
            ep = (w_in_ab[j], g_q_lora[j], g_kv_lora[j], w_uq[j], w_ukv[j], diff_lambda[j], g_diff_subln[j], w_out_ab[j])
            lam_init = 0.8 - 0.6 * math.exp(-0.3 * l)
            yp, (ckv_p, kr_p, dk_p, dv_p) = even_mixer(hp, *ep, lam_init)
            ctx = (cache_mla_ckv[:, j], cache_mla_krope[:, j], cache_diff_k[:, j], cache_diff_v[:, j])
            ys, _ = even_mixer(hs, *ep, lam_init, pos, ctx)
            ckv_l.append(ckv_p)
            kr_l.append(kr_p)
            dk_l.append(dk_p)
            dv_l.append(dv_p)
        else:
            op = (w_in_c[j], b_gate_c[j], g_mlstm[j], w_out_c[j])
            Bp = xp.shape[0]
            z_C = jnp.zeros((Bp, 2, H_C, DK_C, DV_C), jnp.float32)
            z_n = jnp.zeros((Bp, 2, H_C, DK_C), jnp.float32)
            z_m = jnp.zeros((Bp, 2, H_C), jnp.float32)
            yp, (C_p, n_p, m_p) = odd_mixer(hp, *op, z_C, z_n, z_m)
            ys, _ = odd_mixer(hs, *op, state_mlstm_C[:, j], state_mlstm_n[:, j], state_mlstm_m[:, j])
            C_l.append(C_p)
            n_l.append(n_p)
            m_l.append(m_p)
        xp = xp + p_g1 * yp
        xs = xs + s_g1 * ys
        xp = xp + p_g2 * swiglu(modulate(xp, g_ffn[l], p_sh2, p_sc2), w_ffn_in[l], w_ffn_out[l])
        xs = xs + s_g2 * swiglu(modulate(xs, g_ffn[l], s_sh2, s_sc2), w_ffn_in[l], w_ffn_out[l])
    y_prompt = rms_norm(xp, g_final)
    y_sample = rms_norm(xs, g_final)
    new_mla_ckv = jnp.stack(ckv_l, axis=1)
    new_mla_krope = jnp.stack(kr_l, axis=1)
    new_diff_k = jnp.stack(dk_l, axis=1)
    new_diff_v = jnp.stack(dv_l, axis=1)
    new_mlstm_C = jnp.stack(C_l, axis=1)
    new_mlstm_n = jnp.stack(n_l, axis=1)
    new_mlstm_m = jnp.stack(m_l, axis=1)
    return (y_prompt, y_sample, new_mla_ckv, new_mla_krope, new_diff_k, new_diff_v, new_mlstm_C, new_mlstm_n, new_mlstm_m)
```

```python
import math
import numpy as np
import concourse.bass as bass
import concourse.mybir as mybir
from concourse.bass_utils import run_bass_kernel_spmd

F32 = mybir.dt.float32
BF16 = mybir.dt.bfloat16
ALU = mybir.AluOpType
AF = mybir.ActivationFunctionType
AX = mybir.AxisListType

ENGS = ['pe', 'act', 'dve', 'pool', 'sp']
D = 1024
NT = 1024
DFF = 2816
EPS = 1e-6
LAM_INIT0 = 0.8 - 0.6 * math.exp(-0.3 * 0)


class Op:
    __slots__ = ('eng', 'fn', 'deps', 'needed', 'sig', 'chan', 'key', 'seq')
    _n = [0]

    def __init__(self, eng, fn, chan):
        self.eng = eng
        self.fn = fn
        self.chan = chan
        self.deps = set()
        self.needed = chan is not None
        self.sig = 0
        self.key = None
        Op._n[0] += 1
        self.seq = Op._n[0]


class Prog:
    def __init__(self, nc):
        self.nc = nc
        self.ops = {e: [] for e in ENGS}
        self.res_w = {}
        self.res_r = {}

    def add(self, eng, fn, r=(), w=(), chan=None):
        op = Op(eng, fn, chan)
        deps = set()
        if eng != 'pe':
            w = list(w) + [n for n in r if n.startswith('pb') and n not in w]
        for n in r:
            o = self.res_w.get(n)
            if o is not None:
                deps.add(o)
        for n in w:
            o = self.res_w.get(n)
            if o is not None:
                deps.add(o)
            for o in self.res_r.get(n, {}).values():
                deps.add(o)
        for n in r:
            self.res_r.setdefault(n, {})[(eng, chan)] = op
        for n in w:
            self.res_w[n] = op
            self.res_r[n] = {}
        deps.discard(op)
        if eng == 'pe':
            deps = {d for d in deps if not (d.eng == 'pe' and d.chan is None)}
        op.deps = deps
        self.ops[eng].append(op)
        return op

    def barrier(self):
        lasts = []
        for e in ENGS:
            for o in reversed(self.ops[e]):
                if o.fn is not None and o.chan is None:
                    lasts.append(o)
                    break
        chans = {}
        for e in ENGS:
            for o in self.ops[e]:
                if o.chan is not None:
                    chans[o.chan] = o
        lasts += list(chans.values())
        for e in ENGS:
            op = Op(e, None, None)
            op.deps = {d for d in lasts if not (e == 'pe' and d.eng == 'pe' and d.chan is None)}
            self.ops[e].append(op)

    def emit(self, final_wait_prefix='out'):
        nc = self.nc
        chans = {}
        for e in ENGS:
            for o in self.ops[e]:
                if o.chan is not None and o.chan.startswith(final_wait_prefix):
                    chans[o.chan] = o
        fin = Op('sp', None, None)
        fin.deps = set(chans.values())
        self.ops['sp'].append(fin)
        for e in ENGS:
            for op in self.ops[e]:
                for d in op.deps:
                    d.needed = True
        counters = {}
        for e in ENGS:
            for op in self.ops[e]:
                if op.needed:
                    key = ('ch', op.chan) if op.chan else ('eng', e)
                    counters[key] = counters.get(key, 0) + (16 if op.chan else 1)
                    op.sig = counters[key]
                    op.key = key
        sems = {}
        for i, key in enumerate(counters):
            sems[key] = nc.alloc_semaphore("s%d" % i)
        self.counters = counters
        engobj = {'pe': 'tensor', 'act': 'scalar', 'dve': 'vector', 'pool': 'gpsimd', 'sp': 'sync'}
        stats = {}

        chan_ops = {}
        for e in ENGS:
            for op in self.ops[e]:
                if op.chan is not None:
                    chan_ops.setdefault(op.key, []).append((op.seq, op.sig))
        for k in chan_ops:
            chan_ops[k].sort()

        def emit_engine(eo, ops):
            waited = {}
            nw = 0
            for op in ops:
                need = {}
                for d in op.deps:
                    v = d.sig
                    if d.chan is not None:
                        for (sq, sg) in chan_ops[d.key]:
                            if sq < op.seq:
                                v = max(v, sg)
                            else:
                                break
                    if need.get(d.key, 0) < v:
                        need[d.key] = v
                for key, val in need.items():
                    if waited.get(key, 0) < val:
                        eo.wait_ge(sems[key], val)
                        waited[key] = val
                        nw += 1
                if op.fn is not None:
                    ins = op.fn(eo)
                    if op.needed:
                        ins.then_inc(sems[op.key], 16 if op.chan else 1)
            return nw

        with nc.Block() as block:
            for e in ENGS:
                ops = self.ops[e]

                def f(eo, ops=ops, e=e):
                    stats[e] = (len(ops), emit_engine(eo, ops))
                getattr(block, engobj[e])(f)
        self.stats = stats
        return stats


def mm(out, lhsT, rhs, start=True, stop=True, sgc=False):
    if sgc:
        return lambda e: e.matmul(out, lhsT, rhs, start=start, stop=stop, skip_group_check=True)
    return lambda e: e.matmul(out, lhsT, rhs, start=start, stop=stop)


def tr(out, in_, ident):
    return lambda e: e.transpose(out, in_, ident)


def act(out, in_, func, **kw):
    return lambda e: e.activation(out=out, in_=in_, func=func, **kw)


def tt(out, in0, in1, op):
    return lambda e: e.tensor_tensor(out=out, in0=in0, in1=in1, op=op)


def ts(out, in0, s1, op0, s2=None, op1=None):
    if op1 is None:
        return lambda e: e.tensor_scalar(out=out, in0=in0, scalar1=s1, scalar2=None, op0=op0)
    return lambda e: e.tensor_scalar(out=out, in0=in0, scalar1=s1, scalar2=s2, op0=op0, op1=op1)


def stt(out, in0, scalar, in1, op0, op1):
    return lambda e: e.scalar_tensor_tensor(out=out, in0=in0, scalar=scalar, in1=in1, op0=op0, op1=op1)


def cp(out, in_):
    return lambda e: e.tensor_copy(out=out, in_=in_)


def acp(out, in_):
    return lambda e: e.copy(out=out, in_=in_)


def rcp(out, in_):
    return lambda e: e.reciprocal(out=out, in_=in_)


def dma(out, in_, slow=False):
    if slow:
        return lambda e: e.dma_start(out=out, in_=in_, allow_slow_non_contiguous=True)
    return lambda e: e.dma_start(out=out, in_=in_)


def mset(ap, v):
    return lambda e: e.memset(ap, v)


def _rope_tables():
    t = np.arange(1024)
    rows = (t // 64).astype(np.float64)
    cols = (t % 64).astype(np.float64)

    def blk(pos, half):
        fr = np.power(10000.0, -np.arange(half, dtype=np.float64) / half)
        ang = pos[None, :] * fr[:, None]
        c = np.concatenate([np.cos(ang), np.cos(ang)], 0)
        s = np.concatenate([-np.sin(ang), np.sin(ang)], 0)
        return c, s

    c1, s1 = blk(rows, 8)
    c2, s2 = blk(cols, 8)
    c32 = np.concatenate([c1, c2], 0)
    s32 = np.concatenate([s1, s2], 0)
    d1, e1 = blk(rows, 16)
    d2, e2 = blk(cols, 16)
    c64 = np.concatenate([d1, d2], 0)
    s64 = np.concatenate([e1, e2], 0)
    c128 = np.concatenate([c64, c64], 0)
    s128 = np.concatenate([s64, s64], 0)
    return (c32.astype(np.float32), s32.astype(np.float32), c128.astype(np.float32), s128.astype(np.float32))


def _swap_perm(n, half):
    p = np.arange(n)
    for b in range(0, n, 2 * half):
        p[b:b + half] = np.arange(b + half, b + 2 * half)
        p[b + half:b + 2 * half] = np.arange(b, b + half)
    return p


C_ID = 0
C_TUI = 128
C_TLS = 256
C_TLI = 384
C_TUS = 512
C_ONE = 640
C_SELF = 768
C_SELB = 769
NCST = 772

V_COND = 0
V_BADA = 16
V_G = 112
V_GQL = 152
V_GKV = 154
V_GSUB = 155
V_GKVB = 156
V_GMLB = 284
V_BGB = 412
V_LAMB = 444
V_N0 = 700
V_M0 = 708
NVEC = 724


def _consts():
    c = np.zeros((128, NCST), np.float32)
    s = np.arange(128)[:, None]
    l = np.arange(128)[None, :]
    c[:, C_ID:C_ID + 128] = (s == l)
    c[:, C_TUI:C_TUI + 128] = (s <= l)
    c[:, C_TLS:C_TLS + 128] = (s > l)
    c[:, C_TLI:C_TLI + 128] = (s >= l)
    c[:, C_TUS:C_TUS + 128] = (s < l)
    c[:, C_ONE:C_ONE + 128] = 1.0
    c[0:8, C_SELF] = 1.0
    c[8:16, C_SELB] = 1.0
    return c


def build(parts=('m0', 'f0', 'm1', 'f1'), dbg=False, groups=(0, 1), m0_stop=99, skip=(), m1_stop=99):
    nc = bass.Bass("TRN2", target_bir_lowering=False)
    P = Prog(nc)

    def din(name, shape):
        return nc.dram_tensor(name, list(shape), F32, kind="ExternalInput").ap()

    def dout(name, shape):
        return nc.dram_tensor(name, list(shape), F32, kind="ExternalOutput").ap()

    xg = din("xg", [2, NT, D])
    cst_d = din("cst", [128, NCST])
    vec_d = din("vec", [128, NVEC])
    rope_d = din("rope", [4, 128, 1024])
    w_ada = din("w_ada", [2, D, 6 * D])
    w_ffn_in = din("w_ffn_in", [2, D, 2 * DFF])
    w_ffn_out = din("w_ffn_out", [2, DFF, D])
    w_in_ab = din("w_in_ab", [D, 1952])
    w_in_sw = din("w_in_sw", [D, 1056])
    w_uq = din("w_uq", [256, 1024])
    w_ukv = din("w_ukv", [128, 1024])
    w_out_ab = din("w_out_ab", [D, D])
    w_in_c = din("w_in_c", [D, 3104])
    w_out_c = din("w_out_c", [D, D])
    ckv_c = din("ckv_c", [256, 128])
    kr_c = din("kr_c", [256, 32])
    dk_c = din("dk_c", [4, 256, 128])
    dv_c = din("dv_c", [4, 256, 128])
    c0_d = din("c0", [128, 2, 4, 128])
    y_d = dout("y", [2, NT, D])
    o_ckv = dout("o_ckv", [NT, 128])
    o_kr = dout("o_kr", [NT, 32])
    o_dk = dout("o_dk", [4, 4, 256, 128])
    o_dv = dout("o_dv", [4, 4, 256, 128])
    o_C = dout("o_C", [4, 2, 8, 64, 128])
    o_n = dout("o_n", [4, 2, 8, 64])
    o_m = dout("o_m", [4, 2, 8])
    dbg_out = {}

    def sb(name, shape, dt=F32):
        return nc.alloc_sbuf_tensor(name, list(shape), dt)

    cst = sb("cstS", [128, NCST])
    vec = sb("vecS", [128, NVEC])
    cstb = sb("cstB", [128, NCST], BF16)
    xT = sb("xT", [128, 8, NT])
    modraw = [sb("modraw%d" % l, [128, 48, 2]) for l in range(2)]
    modA = sb("modA", [128, 2, 2, 2, 8])
    condb = sb("condb", [128, 2, 8], BF16)
    stage = [sb("stage%d" % i, [128, 2048]) for i in range(2)]
    wsl = [sb("wsl%d" % i, [128, 2048], BF16) for i in range(3)]
    sqb = [sb("sqb%d" % i, [128, 512], BF16) for i in range(2)]
    tmpf = [sb("tmpf%d" % i, [128, 512]) for i in range(4)]
    rstdb = [sb("rstd%d" % i, [128, 512]) for i in range(2)]
    joinb = sb("joinb", [128, 8])
    pb = [nc.alloc_psum_tensor("pb%d" % i, [128, 512], F32) for i in range(8)]
    pbb = [p[:].bitcast(BF16) for p in pb]
    ARENA = 118 * 1024
    arena = sb("arena", [128, ARENA // 4])

    idf = cst[:, C_ID:C_ID + 128]
    idb = cstb[:, C_ID:C_ID + 128]
    onesb = cstb[:, C_ONE:C_ONE + 128]
    onesf = cst[:, C_ONE:C_ONE + 128]

    st = {'bank': 0, 'stage': 0, 'wsl': 0, 'ev': 0, 'tmpf': 0, 'sqb': 0, 'aoff': 0, 'castq': 0, 'rstd': 0, 'reserved': set()}

    def nb():
        while True:
            b = st['bank']
            st['bank'] = (b + 1) % 8
            if b not in st['reserved']:
                return b

    def ev_eng():
        st['ev'] ^= 1
        return 'act' if st['ev'] else 'dve'

    def evac(out, in_, r, w):
        e = ev_eng()
        if e == 'act':
            P.add('act', acp(out, in_), r=r, w=w)
        else:
            P.add('dve', cp(out, in_), r=r, w=w)

    def ntmp():
        i = st['tmpf']
        st['tmpf'] = (i + 1) % 4
        return i

    def carve(shape, dt=F32, parts=128):
        n = 1
        for s_ in shape[1:]:
            n *= s_
        esz = 2 if dt == BF16 else 4
        nbytes = (n * esz + 31) // 32 * 32
        off = st['aoff']
        assert off + nbytes <= ARENA, ("arena overflow", off, nbytes)
        st['aoff'] = off + nbytes
        a = arena[0:shape[0], off // 4: off // 4 + nbytes // 4]
        if dt == BF16:
            a = a.bitcast(BF16)
        a = a[:, 0:n]
        if len(shape) == 3:
            a = a.rearrange("p (a b) -> p a b", b=shape[2])
        elif len(shape) == 4:
            a = a.rearrange("p (a b c) -> p a b c", b=shape[2], c=shape[3])
        return a

    def arena_reset():
        P.barrier()
        st['aoff'] = 0

    def dump(name, ap, shape, r):
        if not dbg or name in dbg_out:
            return
        d = nc.dram_tensor("dbg_" + name, list(shape), F32, kind="ExternalOutput").ap()
        dbg_out[name] = d
        P.add('pool', dma(d, ap), r=r, chan='out_dbg_' + name)

    st['stage_pool'] = [(stage[i][:], 'stage%d' % i) for i in range(2)]
    st['wsl_pool'] = [(wsl[i][:], 'wsl%d' % i) for i in range(3)]
    base_pools = (st['stage_pool'], st['wsl_pool'])

    def sload(src, shape):
        n = 1
        for s_ in shape[1:]:
            n *= s_
        assert n <= 2048
        pool_ = st['stage_pool']
        si = st['stage'] % len(pool_)
        st['stage'] = si + 1
        stg_ap, stg_name = pool_[si]
        sv = stg_ap[0:shape[0], 0:n]
        if len(shape) == 3:
            sv = sv.rearrange("p (a b) -> p a b", b=shape[2])
        elif len(shape) == 4:
            sv = sv.rearrange("p (a b c) -> p a b c", b=shape[2], c=shape[3])
        P.add('sp', dma(sv, src), w=[stg_name], chan=stg_name)
        return sv, stg_name

    def wload(src, shape, dst=None, dstname=None):
        sv, sn = sload(src, shape)
        if dst is None:
            n = 1
            for s_ in shape[1:]:
                n *= s_
            wpool = st['wsl_pool']
            wi = st['wsl'] % len(wpool)
            st['wsl'] = wi + 1
            w_ap, dstname = wpool[wi]
            wv = w_ap[0:shape[0], 0:n]
            if len(shape) == 3:
                wv = wv.rearrange("p (a b) -> p a b", b=shape[2])
        else:
            wv = dst
        P.add('act', acp(wv, sv), r=[sn], w=[dstname])
        return wv, dstname

    P.add('sp', dma(cst[:], cst_d), w=['cst'], chan='ld_cst')
    P.add('sp', dma(vec[:], vec_d), w=['vec'], chan='ld_vec')
    P.add('dve', cp(cstb[:], cst[:]), r=['cst'], w=['cstb'])

    P.add('act', act(condb[:], vec[:, V_COND:V_COND + 16].rearrange("p (g c) -> p g c", c=8), AF.Silu), r=['vec'], w=['condb'])
    def adaln_closures(l):
        def make(gi):
            def run(slots):
                bk = nb()
                src = w_ada[l][:, gi * 256:(gi + 1) * 256].rearrange("(kc p) n -> p kc n", p=128)
                wv, wn = wload(src, [128, 8, 256], dst=slots[gi % 2], dstname='adas%d' % (gi % 2))
                for c2 in range(2):
                    for kc in range(8):
                        P.add('pe', mm(pb[bk][:, c2 * 2:c2 * 2 + 2], wv[:, kc, c2 * 128:(c2 + 1) * 128], condb[:, :, kc],
                                       start=(kc == 0), stop=(kc == 7)), r=[wn, 'condb'], w=['pb%d' % bk])
                badaT = vec[:, V_BADA + l * 48 + gi * 2:V_BADA + l * 48 + gi * 2 + 2]
                P.add('dve', tt(modraw[l][:, gi * 2:gi * 2 + 2, :], pb[bk][:, 0:4].rearrange("p (o g) -> p o g", g=2),
                                badaT.unsqueeze(2).to_broadcast([128, 2, 2]), ALU.add), r=['pb%d' % bk, 'vec'], w=['modraw%d.%d' % (l, gi)])
            return run

        def fin(slots):
            names = ['modraw%d.%d' % (l, gi) for gi in range(24)]
            first = True
            for g in range(2):
                for wh in range(2):
                    sc = modraw[l][:, (1 + 3 * wh) * 8:(2 + 3 * wh) * 8, g]
                    gv = vec[:, V_G + (2 * l + wh) * 8:V_G + (2 * l + wh + 1) * 8]
                    P.add('dve', stt(modA[:, l, g, wh, :], sc, 1.0, gv, ALU.add, ALU.mult), r=names + ['vec'],
                          w=['modA'] + (['modraw%d' % l] if first else []))
                    first = False
        return [make(gi) for gi in range(24)] + [fin]

    pending_ada = []

    dump('modraw0', modraw[0][:], [128, 48, 2], ['modraw0'])
    dump('modA', modA[:], [128, 2, 2, 2, 8], ['modA'])

    def mcol(l, g, j, c):
        return modraw[l][:, j * 8 + c, g:g + 1]

    def norm_mod(src, dst, dstname, l, g, wh, n_tt=2, final=False):
        for t_ in range(n_tt):
            tok = slice(t_ * 512, (t_ + 1) * 512)
            bk = nb()
            for c in range(8):
                si = st['sqb']
                st['sqb'] ^= 1
                if c % 3 == 0:
                    P.add('act', act(sqb[si][:], src[:, c, tok], AF.Square), r=['xT'], w=['sqb%d' % si])
                elif c % 3 == 1:
                    P.add('dve', tt(sqb[si][:], src[:, c, tok], src[:, c, tok], ALU.mult), r=['xT'], w=['sqb%d' % si])
                else:
                    P.add('pool', tt(sqb[si][:], src[:, c, tok], src[:, c, tok], ALU.mult), r=['xT'], w=['sqb%d' % si])
                P.add('pe', mm(pb[bk][:], onesb, sqb[si][:], start=(c == 0), stop=(c == 7)), r=['sqb%d' % si, 'cstb'], w=['pb%d' % bk])
            ri = st['rstd']
            st['rstd'] ^= 1
            rstd = rstdb[ri]
            rn = 'rstd%d' % ri
            P.add('act', act(rstd[:], pb[bk][:], AF.Ln, scale=1.0 / D, bias=EPS), r=['pb%d' % bk], w=[rn])
            P.add('act', act(rstd[:], rstd[:], AF.Exp, scale=-0.5), r=[rn], w=[rn])
            for c in range(8):
                ti = ntmp()
                if final:
                    gv = vec[:, V_G + 4 * 8 + c:V_G + 4 * 8 + c + 1]
                    P.add('dve', stt(dst[:, c, tok], src[:, c, tok], gv, rstd[:], ALU.mult, ALU.mult),
                          r=['xT', rn, 'vec'], w=[dstname + '.%d' % t_])
                else:
                    P.add('dve', tt(tmpf[ti][:], src[:, c, tok], rstd[:], ALU.mult), r=['xT', rn], w=['tmpf%d' % ti])
                    if c % 8 in (2, 5, 7):
                        P.add('dve', ts(dst[:, c, tok], tmpf[ti][:], modA[:, l, g, wh, c:c + 1], ALU.mult, mcol(l, g, 3 * wh, c), ALU.add),
                              r=['tmpf%d' % ti, 'modA', 'modraw%d' % l], w=[dstname + '.%d.%d' % (t_, c)])
                    else:
                        P.add('act', act(dst[:, c, tok], tmpf[ti][:], AF.Identity, scale=modA[:, l, g, wh, c:c + 1],
                                         bias=mcol(l, g, 3 * wh, c)), r=['tmpf%d' % ti, 'modA', 'modraw%d' % l], w=[dstname + '.%d.%d' % (t_, c)])
            if not final:
                P.add('dve', mset(joinb[:, t_:t_ + 1], 0.0), r=[dstname + '.%d.%d' % (t_, c) for c in range(8)], w=[dstname + '.%d' % t_])

    def ffn(l, g):
        arena_reset()
        hm = carve([128, 8, NT], BF16)
        u = carve([128, 22, NT], BF16)
        ada_sl = [carve([128, 8, 256], BF16) for _ in range(2)]
        st['stage_pool'] = base_pools[0] + [(carve([128, 2048]), 'stageF%d' % i) for i in range(2)]
        st['wsl_pool'] = base_pools[1] + [(carve([128, 2048], BF16), 'wslF%d' % i) for i in range(3)]

        def bg():
            if pending_ada:
                pending_ada.pop(0)(ada_sl)
        norm_mod(xT, hm, 'hm', l, g, 1)
        if l == 0 and g == 0:
            dump('xT0', xT[:], [128, 8, NT], ['xT'])
            dump('hm0', hm, [128, 8, NT], ['hm.0', 'hm.1'])
        for j2 in range(11):
            wa, wan = wload(w_ffn_in[l][:, j2 * 256:(j2 + 1) * 256].rearrange("(kc p) n -> p kc n", p=128), [128, 8, 256])
            wb_, wbn = wload(w_ffn_in[l][:, DFF + j2 * 256:DFF + (j2 + 1) * 256].rearrange("(kc p) n -> p kc n", p=128), [128, 8, 256])
            bg()
            for jj in range(2):
                j = j2 * 2 + jj
                for t_ in range(2):
                    tok = slice(t_ * 512, (t_ + 1) * 512)
                    ba = nb()
                    for kc in range(8):
                        P.add('pe', mm(pb[ba][:], wa[:, kc, jj * 128:(jj + 1) * 128], hm[:, kc, tok], start=(kc == 0), stop=(kc == 7)),
                              r=[wan, 'hm.%d' % t_], w=['pb%d' % ba])
                    bb = nb()
                    for kc in range(8):
                        P.add('pe', mm(pb[bb][:], wb_[:, kc, jj * 128:(jj + 1) * 128], hm[:, kc, tok], start=(kc == 0), stop=(kc == 7)),
                              r=[wbn, 'hm.%d' % t_], w=['pb%d' % bb])
                    ti = ntmp()
                    P.add('act', act(tmpf[ti][:], pb[ba][:], AF.Silu), r=['pb%d' % ba], w=['tmpf%d' % ti])
                    P.add('dve', tt(u[:, j, tok], tmpf[ti][:], pb[bb][:], ALU.mult), r=['tmpf%d' % ti, 'pb%d' % bb], w=['u.%d.%d' % (j, t_)])
        if l == 0 and g == 0:
            dump('u0', u, [128, 22, NT], ['u.%d.%d' % (j, t_) for j in range(22) for t_ in range(2)])
        for oc in range(8):
            w1, w1n = wload(w_ffn_out[l][0:11 * 128, oc * 128:(oc + 1) * 128].rearrange("(j p) n -> p j n", p=128), [128, 11, 128])
            w2, w2n = wload(w_ffn_out[l][11 * 128:22 * 128, oc * 128:(oc + 1) * 128].rearrange("(j p) n -> p j n", p=128), [128, 11, 128])
            bg()
            bg()
            for t_ in range(2):
                tok = slice(t_ * 512, (t_ + 1) * 512)
                bk = nb()
                for j in range(22):
                    wv = w1 if j < 11 else w2
                    P.add('pe', mm(pb[bk][:], wv[:, j % 11, :], u[:, j, tok], start=(j == 0), stop=(j == 21)),
                          r=[w1n if j < 11 else w2n, 'u.%d.%d' % (j, t_)], w=['pb%d' % bk])
                P.add('dve', stt(xT[:, oc, tok], pb[bk][:], mcol(l, g, 5, oc), xT[:, oc, tok], ALU.mult, ALU.add),
                      r=['pb%d' % bk, 'modraw%d' % l, 'xT'], w=['xT'])
        while pending_ada:
            bg()
        st['stage_pool'], st['wsl_pool'] = base_pools


    lamc = sb("lamc", [128, 4])
    lamt = sb("lamt", [128, 2, 64])
    lv = vec[:, V_LAMB:V_LAMB + 256].rearrange("p (a b) -> p a b", b=64)
    P.add('dve', tt(lamt[:], lv[:, 0::2, :], lv[:, 1::2, :], ALU.mult), r=['vec'], w=['lamt'])
    P.add('dve', lambda e: e.tensor_reduce(out=lamc[:, 2:4], in_=lamt[:], axis=AX.X, op=ALU.add), r=['lamt'], w=['lamc'])
    P.add('act', act(lamc[:, 2:4], lamc[:, 2:4], AF.Exp), r=['lamc'], w=['lamc'])
    P.add('dve', tt(lamc[:, 0:1], lamc[:, 2:3], lamc[:, 3:4], ALU.subtract), r=['lamc'], w=['lamc'])
    P.add('dve', ts(lamc[:, 0:1], lamc[:, 0:1], LAM_INIT0, ALU.add), r=['lamc'], w=['lamc'])
    P.add('dve', ts(lamc[:, 1:2], lamc[:, 0:1], -1.0, ALU.mult), r=['lamc'], w=['lamc'])
    gsubc = sb("gsubc", [128, 1])
    P.add('dve', ts(gsubc[:], vec[:, V_GSUB:V_GSUB + 1], 1.0 - LAM_INIT0, ALU.mult), r=['vec'], w=['gsubc'])

    SCALE_A = 96.0 ** -0.5
    SCALE_B = 64.0 ** -0.5

    def mixer0(g):
        arena_reset()
        l = 0
        NK = 1024 if g == 0 else 1280
        NKT = NK // 128
        hm = carve([128, 8, NT], BF16)
        kN = carve([128, 4, NK], BF16)
        krT = carve([128, NK], BF16)
        ckvn = carve([128, NK], BF16)
        Vm = carve([128, NKT, 512], BF16)
        dkT = carve([128, 4, NK], BF16)
        dvt = carve([128, NKT, 512], BF16)
        wukv_b = carve([128, 1024], BF16)
        wuq_b = carve([128, 2, 1024], BF16)
        cqn = carve([128, 2, 512], BF16)
        qNz = carve([128, 8, 512], BF16)
        qR = carve([128, 8, 512], BF16)
        dqT = carve([128, 4, 512], BF16)
        outA = carve([128, 4, 512], BF16)
        outB = carve([128, 4, 512], BF16)
        PT = [carve([128, 512], BF16) for _ in range(4)]
        ES = [carve([128, 512]) for _ in range(2)]
        ESb = [carve([128, 512], BF16) for _ in range(2)]
        if g == 0:
            stg = [carve([128, 512]) for _ in range(4)]
        if g == 1:
            rc32 = carve([32, 512])
            rs32 = carve([32, 512])
            rc128 = carve([128, 512])
            rs128 = carve([128, 512])

            def load_rope(t_):
                tk = slice(t_ * 512, (t_ + 1) * 512)
                P.add('sp', dma(rc32, rope_d[0][0:32, tk]), w=['rc32'], chan='ld_rope0')
                P.add('sp', dma(rs32, rope_d[1][0:32, tk]), w=['rs32'], chan='ld_rope1')
                P.add('sp', dma(rc128, rope_d[2][:, tk]), w=['rc128'], chan='ld_rope2')
                P.add('sp', dma(rs128, rope_d[3][:, tk]), w=['rs128'], chan='ld_rope3')
        cnt = {'pt': 0, 'stg': 0, 'sb': 0, 'ab': 0, 'es': 0, 'eseng': 0}

        def nsb():
            cnt['sb'] = (cnt['sb'] + 1) % 4
            return cnt['sb']

        def nab():
            cnt['ab'] = (cnt['ab'] + 1) % 4
            return 4 + cnt['ab']

        def nstg():
            cnt['stg'] = (cnt['stg'] + 1) % 4
            return cnt['stg']

        ada_sl = [carve([128, 8, 256], BF16) for _ in range(2)] if pending_ada else None

        def bg():
            if pending_ada:
                pending_ada.pop(0)(ada_sl)

        P.add('pool', mset(qNz, 0.0), w=['qNz'])
        P.add('pool', mset(qR, 0.0), w=['qRz'])
        P.add('pool', mset(krT, 0.0), w=['krTz'])
        norm_mod(xT, hm, 'hm', 0, g, 0)

        def fm_mm(bk, wv, wn, c0, M, t_):
            tok = slice(t_ * 512, (t_ + 1) * 512)
            for kc in range(8):
                P.add('pe', mm(pb[bk][0:M, :], wv[:, kc, c0:c0 + M], hm[:, kc, tok], start=(kc == 0), stop=(kc == 7)),
                      r=[wn, 'hm.%d' % t_], w=['pb%d' % bk])

        def tm_mm(bk, col0, wv, wn, n, j):
            for kc in range(8):
                P.add('pe', mm(pb[bk][:, col0:col0 + n], hm[:, kc, j * 128:(j + 1) * 128], wv[:, kc, 0:n], start=(kc == 0), stop=(kc == 7)),
                      r=[wn, 'hm.%d' % (j // 4)], w=['pb%d' % bk])

        def rope_evac(dst, dstname, M, bk, bks, ctab, stab, tok, cn, sn_, dst2=None, extra_r=()):
            t1 = ntmp()
            P.add('dve', tt(tmpf[t1][0:M, :], pb[bk][0:M, :], ctab[0:M, :], ALU.mult), r=['pb%d' % bk, cn], w=['tmpf%d' % t1])
            t2 = ntmp()
            P.add('dve', tt(tmpf[t2][0:M, :], pb[bks][0:M, :], stab[0:M, :], ALU.mult), r=['pb%d' % bks, sn_], w=['tmpf%d' % t2])
            P.add('pool', tt(dst, tmpf[t1][0:M, :], tmpf[t2][0:M, :], ALU.add), r=['tmpf%d' % t1, 'tmpf%d' % t2] + list(extra_r), w=[dstname])
            if dst2 is not None:
                P.add('pool', tt(dst2, tmpf[t1][0:M, :], tmpf[t2][0:M, :], ALU.add), r=['tmpf%d' % t1, 'tmpf%d' % t2], w=[dstname])

        wukv_b, _ = wload(w_ukv, [128, 1024], dst=wukv_b, dstname='wukv')
        wuq_b, _ = wload(w_uq.rearrange("(kc p) n -> p kc n", p=128), [128, 2, 1024], dst=wuq_b, dstname='wuq')
        wA, wAn = wload(w_in_ab[:, 256:416].rearrange("(kc p) n -> p kc n", p=128), [128, 8, 160])
        bg()
        bg()
        if g == 1:
            wAs, wAsn = wload(w_in_sw[:, 0:32].rearrange("(kc p) n -> p kc n", p=128), [128, 8, 32])
        for t_ in range(2):
            tok = slice(t_ * 512, (t_ + 1) * 512)
            gtok = slice(t_ * 512, (t_ + 1) * 512)
            if g == 1:
                load_rope(t_)
            bk = nb()
            fm_mm(bk, wA, wAn, 0, 128, t_)
            si = st['sqb']
            st['sqb'] ^= 1
            P.add('act', act(sqb[si][:], pb[bk][:], AF.Square), r=['pb%d' % bk], w=['sqb%d' % si])
            b2 = nb()
            P.add('pe', mm(pb[b2][:], onesb, sqb[si][:]), r=['sqb%d' % si, 'cstb'], w=['pb%d' % b2])
            ri = st['rstd']
            st['rstd'] ^= 1
            rn = 'rstd%d' % ri
            P.add('act', act(rstdb[ri][:], pb[b2][:], AF.Ln, scale=1.0 / 128, bias=EPS), r=['pb%d' % b2], w=[rn])
            P.add('act', act(rstdb[ri][:], rstdb[ri][:], AF.Exp, scale=-0.5), r=[rn], w=[rn])
            P.add('dve', stt(ckvn[:, gtok], pb[bk][:], vec[:, V_GKV:V_GKV + 1], rstdb[ri][:], ALU.mult, ALU.mult),
                  r=['pb%d' % bk, rn, 'vec'], w=['ckvn.%d' % t_])
            bk = nb()
            fm_mm(bk, wA, wAn, 128, 32, t_)
            if g == 0:
                evac(krT[0:32, gtok], pb[bk][0:32, :], r=['pb%d' % bk, 'krTz'], w=['krT.%d' % t_])
            else:
                bks = nb()
                fm_mm(bks, wAs, wAsn, 0, 32, t_)
                rope_evac(krT[0:32, gtok], 'krT.%d' % t_, 32, bk, bks, rc32, rs32, tok, 'rc32', 'rs32', extra_r=['krTz'])
        if g == 0:
            for j in range(8):
                bk = nb()
                tm_mm(bk, 0, wA, wAn, 160, j)
                si_ = nstg()
                sg = stg[si_]
                ti = ntmp()
                P.add('act', act(tmpf[ti][:, 0:128], pb[bk][:, 0:128], AF.Square, accum_out=tmpf[ti][:, 128:129]), r=['pb%d' % bk], w=['tmpf%d' % ti])
                P.add('act', act(tmpf[ti][:, 129:130], tmpf[ti][:, 128:129], AF.Ln, scale=1.0 / 128, bias=EPS), r=['tmpf%d' % ti], w=['tmpf%d' % ti])
                P.add('act', act(tmpf[ti][:, 130:131], tmpf[ti][:, 129:130], AF.Exp, scale=-0.5), r=['tmpf%d' % ti], w=['tmpf%d' % ti])
                P.add('dve', stt(sg[:, 0:128], pb[bk][:, 0:128], tmpf[ti][:, 130:131], vec[:, V_GKVB:V_GKVB + 128], ALU.mult, ALU.mult),
                      r=['pb%d' % bk, 'tmpf%d' % ti, 'vec'], w=['stg%d' % si_])
                P.add('act', acp(sg[:, 128:160], pb[bk][:, 128:160]), r=['pb%d' % bk], w=['stg%d' % si_])
                P.add('pool', dma(o_ckv[j * 128:(j + 1) * 128, :], sg[:, 0:128]), r=['stg%d' % si_], chan='out_stg%d' % si_)
                P.add('pool', dma(o_kr[j * 128:(j + 1) * 128, :], sg[:, 128:160]), r=['stg%d' % si_], chan='out_stg%d' % si_)
        if m0_stop <= 1:
            return
        for hh in range(0 if 'dk' in skip else 2):
            wK, wKn = wload(w_in_ab[:, 928 + hh * 256:928 + (hh + 1) * 256].rearrange("(kc p) n -> p kc n", p=128), [128, 8, 256])
            bg()
            if g == 1:
                wKs, wKsn = wload(w_in_sw[:, 544 + hh * 256:544 + (hh + 1) * 256].rearrange("(kc p) n -> p kc n", p=128), [128, 8, 256])
            for t_ in range(2):
                if g == 1:
                    load_rope(t_)
                for h2 in range(2):
                    h = hh * 2 + h2
                    tok = slice(t_ * 512, (t_ + 1) * 512)
                    bk = nb()
                    fm_mm(bk, wK, wKn, h2 * 128, 128, t_)
                    if g == 0:
                        evac(dkT[:, h, tok], pb[bk][:], r=['pb%d' % bk], w=['dkT.%d.%d' % (h, t_)])
                    else:
                        bks = nb()
                        fm_mm(bks, wKs, wKsn, h2 * 128, 128, t_)
                        rope_evac(dkT[:, h, tok], 'dkT.%d.%d' % (h, t_), 128, bk, bks, rc128, rs128, tok, 'rc128', 'rs128')
            if g == 0:
                for j in range(8):
                    bk = nb()
                    tm_mm(bk, 0, wK, wKn, 256, j)
                    si_ = nstg()
                    evac(stg[si_][:, 0:256], pb[bk][:, 0:256], r=['pb%d' % bk], w=['stg%d' % si_])
                    if 'odk' not in skip:
                      P.add('pool', dma(o_dk[j // 2, 2 * hh:2 * hh + 2, (j % 2) * 128:(j % 2 + 1) * 128, :].rearrange("h t d -> t h d"),
                                    stg[si_][:, 0:256].rearrange("t (h d) -> t h d", d=128)), r=['stg%d' % si_], chan='out_stg%d' % si_)
        wV0, wV0n = wload(w_in_ab[:, 1440:1696].rearrange("(kc p) n -> p kc n", p=128), [128, 8, 256])
        wV1, wV1n = wload(w_in_ab[:, 1696:1952].rearrange("(kc p) n -> p kc n", p=128), [128, 8, 256])
        bg()
        bg()
        for j in range(0 if 'dv' in skip else 8):
            bk = nb()
            tm_mm(bk, 0, wV0, wV0n, 256, j)
            tm_mm(bk, 256, wV1, wV1n, 256, j)
            P.add('act', acp(dvt[:, j, :], pb[bk][:]), r=['pb%d' % bk], w=['dvt.%d' % j])
            if g == 0:
                si_ = nstg()
                P.add('dve', cp(stg[si_][:], pb[bk][:]), r=['pb%d' % bk], w=['stg%d' % si_])
                if 'odv' not in skip:
                  P.add('pool', dma(o_dv[j // 2, :, (j % 2) * 128:(j % 2 + 1) * 128, :].rearrange("h t d -> t h d"),
                                stg[si_][:].rearrange("t (h d) -> t h d", d=128)), r=['stg%d' % si_], chan='out_stg%d' % si_)
        if g == 1:
            sv, sn = sload(ckv_c.rearrange("(a p) f -> p a f", p=128), [128, 2, 128])
            bk = nb()
            for a in range(2):
                P.add('pe', tr(pb[bk][:, a * 128:(a + 1) * 128], sv[:, a, :], idf), r=[sn, 'cst'], w=['pb%d' % bk])
            evac(ckvn[:, 1024:1280], pb[bk][:, 0:256], r=['pb%d' % bk], w=['ckvn.c'])
            sv, sn = sload(kr_c.rearrange("(a p) f -> p a f", p=128), [128, 2, 32])
            bk = nb()
            for a in range(2):
                P.add('pe', tr(pb[bk][0:32, a * 128:(a + 1) * 128], sv[:, a, :], idf), r=[sn, 'cst'], w=['pb%d' % bk])
            evac(krT[0:32, 1024:1280], pb[bk][0:32, 0:256], r=['pb%d' % bk, 'krTz'], w=['krT.c'])
            sv, sn = sload(dk_c.rearrange("h (a p) f -> p h a f", p=128), [128, 4, 2, 128])
            for h in range(4):
                bk = nb()
                for a in range(2):
                    P.add('pe', tr(pb[bk][:, a * 128:(a + 1) * 128], sv[:, h, a, :], idf), r=[sn, 'cst'], w=['pb%d' % bk])
                evac(dkT[:, h, 1024:1280], pb[bk][:, 0:256], r=['pb%d' % bk], w=['dkT.%d.c' % h])
            sv, sn = sload(dv_c.rearrange("h (a p) f -> p h a f", p=128), [128, 4, 2, 128])
            for a in range(2):
                P.add('pool', cp(dvt[:, 8 + a, :].rearrange("p (h f) -> p h f", f=128), sv[:, :, a, :]), r=[sn], w=['dvt.%d' % (8 + a)])
        if m0_stop <= 2:
            return
        ckv_names = ['ckvn.0', 'ckvn.1'] + (['ckvn.c'] if g == 1 else [])
        kblocks = [(0, 512), (512, 512)] + ([(1024, 256)] if g == 1 else [])
        for c in range(4):
            for (k0, kn_) in kblocks:
                bk = nb()
                P.add('pe', mm(pb[bk][:, 0:kn_], wukv_b[:, c * 128:(c + 1) * 128], ckvn[:, k0:k0 + kn_]), r=['wukv'] + ckv_names, w=['pb%d' % bk])
                evac(kN[:, c, k0:k0 + kn_], pb[bk][:, 0:kn_], r=['pb%d' % bk], w=['kN.%d.%d' % (c, k0)])
        for kt in range(NKT):
            bk = nb()
            P.add('pe', mm(pb[bk][:], ckvn[:, kt * 128:(kt + 1) * 128], wukv_b[:, 512:1024]), r=['wukv'] + ckv_names, w=['pb%d' % bk])
            evac(Vm[:, kt, :], pb[bk][:], r=['pb%d' % bk], w=['Vm.%d' % kt])
        kN_names = ['kN.%d.%d' % (c, k0) for c in range(4) for (k0, _) in kblocks]
        krT_names = ['krT.0', 'krT.1'] + (['krT.c'] if g == 1 else [])
        Vm_names = ['Vm.%d' % kt for kt in range(NKT)]
        dvt_names = ['dvt.%d' % kt for kt in range(NKT)]

        if m0_stop <= 3:
            return
        for t_ in range(2):
            tok = slice(t_ * 512, (t_ + 1) * 512)
            if g == 1:
                load_rope(t_)
            wQ, wQn = wload(w_in_ab[:, 0:256].rearrange("(kc p) n -> p kc n", p=128), [128, 8, 256])
            bg()
            b2 = nb()
            cqi = {}
            for c in range(2):
                bk = nb()
                fm_mm(bk, wQ, wQn, c * 128, 128, t_)
                cqi[c] = ntmp()
                P.add('dve', cp(tmpf[cqi[c]][:], pb[bk][:]), r=['pb%d' % bk], w=['tmpf%d' % cqi[c]])
                si = st['sqb']
                st['sqb'] ^= 1
                P.add('act', act(sqb[si][:], pb[bk][:], AF.Square), r=['pb%d' % bk], w=['sqb%d' % si])
                P.add('pe', mm(pb[b2][:], onesb, sqb[si][:], start=(c == 0), stop=(c == 1)), r=['sqb%d' % si, 'cstb'], w=['pb%d' % b2])
            ri = st['rstd']
            st['rstd'] ^= 1
            rn = 'rstd%d' % ri
            P.add('act', act(rstdb[ri][:], pb[b2][:], AF.Ln, scale=1.0 / 256, bias=EPS), r=['pb%d' % b2], w=[rn])
            P.add('act', act(rstdb[ri][:], rstdb[ri][:], AF.Exp, scale=-0.5), r=[rn], w=[rn])
            for c in range(2):
                P.add('dve', stt(cqn[:, c, :], tmpf[cqi[c]][:], vec[:, V_GQL + c:V_GQL + c + 1], rstdb[ri][:], ALU.mult, ALU.mult),
                      r=['tmpf%d' % cqi[c], rn, 'vec'], w=['cqn'])
            for c in range(4):
                bk = nb()
                for kc in range(2):
                    P.add('pe', mm(pb[bk][:], wuq_b[:, kc, c * 128:(c + 1) * 128], cqn[:, kc, :], start=(kc == 0), stop=(kc == 1)),
                          r=['wuq', 'cqn'], w=['pb%d' % bk])
                P.add('act', acp(qNz[0:64, 2 * c, :], pb[bk][0:64, :]), r=['pb%d' % bk, 'qNz'], w=['qN.%d' % c])
                P.add('dve', cp(qNz[64:128, 2 * c + 1, :], pb[bk][64:128, :]), r=['pb%d' % bk, 'qNz'], w=['qN.%d' % c])
            for h in range(8):
                bk = nb()
                for kc in range(2):
                    P.add('pe', mm(pb[bk][0:32, :], wuq_b[:, kc, 512 + h * 32:512 + (h + 1) * 32], cqn[:, kc, :], start=(kc == 0), stop=(kc == 1)),
                          r=['wuq', 'cqn'], w=['pb%d' % bk])
                if g == 0:
                    evac(qR[0:32, h, :], pb[bk][0:32, :], r=['pb%d' % bk, 'qRz'], w=['qR.%d' % h])
                else:
                    bks = nb()
                    for kc in range(2):
                        P.add('pe', mm(pb[bks][0:32, :], wuq_b[:, kc, 768 + h * 32:768 + (h + 1) * 32], cqn[:, kc, :], start=(kc == 0), stop=(kc == 1)),
                              r=['wuq', 'cqn'], w=['pb%d' % bks])
                    rope_evac(qR[0:32, h, :], 'qR.%d' % h, 32, bk, bks, rc32, rs32, tok, 'rc32', 'rs32', extra_r=['qRz'])
            for hh in range(2):
                wD, wDn = wload(w_in_ab[:, 416 + hh * 256:416 + (hh + 1) * 256].rearrange("(kc p) n -> p kc n", p=128), [128, 8, 256])
                bg()
                if g == 1:
                    wDs, wDsn = wload(w_in_sw[:, 32 + hh * 256:32 + (hh + 1) * 256].rearrange("(kc p) n -> p kc n", p=128), [128, 8, 256])
                for h2 in range(2):
                    h = hh * 2 + h2
                    bk = nb()
                    fm_mm(bk, wD, wDn, h2 * 128, 128, t_)
                    if g == 0:
                        evac(dqT[:, h, :], pb[bk][:], r=['pb%d' % bk], w=['dqT.%d' % h])
                    else:
                        bks = nb()
                        fm_mm(bks, wDs, wDsn, h2 * 128, 128, t_)
                        rope_evac(dqT[:, h, :], 'dqT.%d' % h, 128, bk, bks, rc128, rs128, tok, 'rc128', 'rs128')
            LOOK = 3
            NQ = 256 if g == 0 else 512
            for qb in range(512 // NQ):
                qc = slice(qb * NQ, (qb + 1) * NQ)
                if g == 0:
                    seq = t_ * 2 + qb
                    kts = [2 * seq, 2 * seq + 1]
                else:
                    kts = list(range(NKT))
                nk = len(kts)
                steps = []

                def esum_step(hd_, i_, nk_, pi, force_eng=None):
                    if i_ == 0:
                        hd_['es'] = cnt['es']
                        cnt['es'] ^= 1
                    k_ = hd_['es']
                    eng_ = 'pool' if cnt['eseng'] % 3 == 0 else 'dve'
                    cnt['eseng'] += 1
                    if force_eng is not None:
                        eng_ = force_eng
                    if nk_ == 1:
                        P.add(eng_, cp(ESb[k_][:, 0:NQ], PT[pi][:, 0:NQ]), r=['PT%d' % pi], w=['ESb%d' % k_])
                    elif i_ == 0:
                        P.add(eng_, cp(ES[k_][:, 0:NQ], PT[pi][:, 0:NQ]), r=['PT%d' % pi], w=['ES%d' % k_])
                    elif i_ < nk_ - 1:
                        P.add(eng_, tt(ES[k_][:, 0:NQ], ES[k_][:, 0:NQ], PT[pi][:, 0:NQ], ALU.add), r=['PT%d' % pi, 'ES%d' % k_], w=['ES%d' % k_])
                    else:
                        P.add(eng_, tt(ESb[k_][:, 0:NQ], ES[k_][:, 0:NQ], PT[pi][:, 0:NQ], ALU.add), r=['PT%d' % pi, 'ES%d' % k_], w=['ESb%d' % k_])
                for h in range(8):
                    hd = {}
                    for i_, kt in enumerate(kts):
                        def s1(h=h, kt=kt, i_=i_, hd=hd):
                            c, e = h // 2, h % 2
                            pr = slice(e * 64, (e + 1) * 64)
                            ks = slice(kt * 128, (kt + 1) * 128)
                            if i_ == 0:
                                hd['ba'] = nab()
                                hd['bd'] = nab()
                            bs = nsb()
                            P.add('pe', mm(pb[bs][:, 0:NQ], kN[:, c, ks], qNz[:, h, qc], start=True, stop=False),
                                  r=kN_names + ['qN.%d' % c], w=['pb%d' % bs])
                            P.add('pe', mm(pb[bs][:, 0:NQ], krT[:, ks], qR[:, h, qc], start=False, stop=True),
                                  r=krT_names + ['qR.%d' % h], w=['pb%d' % bs])
                            return bs

                        def s2(bs, h=h, kt=kt, i_=i_, hd=hd):
                            c, e = h // 2, h % 2
                            pr = slice(e * 64, (e + 1) * 64)
                            ba, bd = hd['ba'], hd['bd']
                            pi = cnt['pt']
                            cnt['pt'] = (pi + 1) % 4
                            P.add('act', act(PT[pi][:, 0:NQ], pb[bs][:, 0:NQ], AF.Exp, scale=SCALE_A), r=['pb%d' % bs], w=['PT%d' % pi])
                            P.add('pe', mm(pb[ba][0:64, 0:NQ], Vm[:, kt, h * 64:(h + 1) * 64], PT[pi][:, 0:NQ], start=(i_ == 0), stop=(i_ == nk - 1)),
                                  r=Vm_names + ['PT%d' % pi], w=['pb%d' % ba])
                            esum_step(hd, i_, nk, pi)
                            if i_ == nk - 1:
                                P.add('pe', mm(pb[bd][0:64, 0:NQ], onesb[:, 0:64], ESb[hd['es']][:, 0:NQ]), r=['cstb', 'ESb%d' % hd['es']], w=['pb%d' % bd])
                            if i_ == nk - 1:
                                ti = ntmp()
                                P.add('act', act(tmpf[ti][0:64, 0:NQ], pb[bd][0:64, 0:NQ], AF.Ln), r=['pb%d' % bd], w=['tmpf%d' % ti])
                                P.add('act', act(tmpf[ti][0:64, 0:NQ], tmpf[ti][0:64, 0:NQ], AF.Exp, scale=-1.0), r=['tmpf%d' % ti], w=['tmpf%d' % ti])
                                P.add('dve', tt(outA[pr, c, qc], pb[ba][0:64, 0:NQ], tmpf[ti][0:64, 0:NQ], ALU.mult),
                                      r=['pb%d' % ba, 'tmpf%d' % ti], w=['outA.%d' % c])
                        steps.append((s1, s2))
                pend = []
                for idx in range(len(steps) + LOOK):
                    if idx < len(steps):
                        pend.append(steps[idx][0]())
                    if idx >= LOOK:
                        steps[idx - LOOK][1](pend[idx - LOOK])
                steps = []
                LOOKD = 1
                for h in range(4):
                    hd = {}
                    for i_, kt in enumerate(kts):
                        def s1(h=h, kt=kt, i_=i_, hd=hd):
                            ks = slice(kt * 128, (kt + 1) * 128)
                            if i_ == 0:
                                hd[0] = (nab(), nab())
                                hd[1] = (nab(), nab())
                            bss = []
                            for sub in range(2):
                                pr = slice(sub * 64, (sub + 1) * 64)
                                bs = nsb()
                                bss.append(bs)
                                P.add('pe', mm(pb[bs][:, 0:NQ], dkT[pr, h, ks], dqT[pr, h, qc]),
                                      r=['dkT.%d.0' % h, 'dkT.%d.1' % h] + (['dkT.%d.c' % h] if g == 1 else []) + ['dqT.%d' % h], w=['pb%d' % bs])
                            return bss

                        def s2(bss, h=h, kt=kt, i_=i_, hd=hd):
                            pis = []
                            for sub in range(2):
                                pi = cnt['pt']
                                cnt['pt'] = (pi + 1) % 4
                                pis.append(pi)
                                P.add('act', act(PT[pi][:, 0:NQ], pb[bss[sub]][:, 0:NQ], AF.Exp, scale=SCALE_B), r=['pb%d' % bss[sub]], w=['PT%d' % pi])
                            for sub in range(2):
                                bnum, bden = hd[sub]
                                pi = pis[sub]
                                P.add('pe', mm(pb[bnum][:, 0:NQ], dvt[:, kt, h * 128:(h + 1) * 128], PT[pi][:, 0:NQ], start=(i_ == 0), stop=(i_ == nk - 1)),
                                      r=dvt_names + ['PT%d' % pi], w=['pb%d' % bnum])
                                hs_ = hd.setdefault(('es', sub), {})
                                esum_step(hs_, i_, nk, pi, force_eng=('pool' if sub == 0 else 'dve'))
                                if i_ == nk - 1:
                                    P.add('pe', mm(pb[bden][:, 0:NQ], onesb, ESb[hs_['es']][:, 0:NQ]), r=['cstb', 'ESb%d' % hs_['es']], w=['pb%d' % bden])
                            if i_ == nk - 1:
                                (n1, d1), (n2, d2) = hd[0], hd[1]
                                t1 = ntmp()
                                A1 = tmpf[t1][:, 0:NQ]
                                P.add('act', act(A1, pb[d1][:, 0:NQ], AF.Ln), r=['pb%d' % d1], w=['tmpf%d' % t1])
                                P.add('act', act(A1, A1, AF.Exp, scale=-1.0), r=['tmpf%d' % t1], w=['tmpf%d' % t1])
                                P.add('dve', tt(A1, pb[n1][:, 0:NQ], A1, ALU.mult), r=['pb%d' % n1, 'tmpf%d' % t1], w=['tmpf%d' % t1])
                                t2 = ntmp()
                                A2 = tmpf[t2][:, 0:NQ]
                                P.add('act', act(A2, pb[d2][:, 0:NQ], AF.Ln), r=['pb%d' % d2], w=['tmpf%d' % t2])
                                P.add('act', act(A2, A2, AF.Exp, scale=-1.0), r=['tmpf%d' % t2], w=['tmpf%d' % t2])
                                P.add('dve', tt(A2, pb[n2][:, 0:NQ], A2, ALU.mult), r=['pb%d' % n2, 'tmpf%d' % t2], w=['tmpf%d' % t2])
                                P.add('dve', stt(A1, A2, lamc[:, 1:2], A1, ALU.mult, ALU.add), r=['tmpf%d' % t1, 'tmpf%d' % t2, 'lamc'], w=['tmpf%d' % t1])
                                si = st['sqb']
                                st['sqb'] ^= 1
                                P.add('act', act(sqb[si][:, 0:NQ], A1, AF.Square), r=['tmpf%d' % t1], w=['sqb%d' % si])
                                bq = d1
                                P.add('pe', mm(pb[bq][:, 0:NQ], onesb, sqb[si][:, 0:NQ]), r=['sqb%d' % si, 'cstb'], w=['pb%d' % bq])
                                P.add('act', act(A2, pb[bq][:, 0:NQ], AF.Ln, scale=1.0 / 128, bias=EPS), r=['pb%d' % bq], w=['tmpf%d' % t2])
                                P.add('act', act(A2, A2, AF.Exp, scale=-0.5), r=['tmpf%d' % t2], w=['tmpf%d' % t2])
                                P.add('dve', stt(outB[:, h, qc], A1, gsubc[:, 0:1], A2, ALU.mult, ALU.mult),
                                      r=['tmpf%d' % t1, 'tmpf%d' % t2, 'gsubc'], w=['outB.%d' % h])
                        steps.append((s1, s2))
                LOOK_SAVE = LOOK
                LOOK = LOOKD
                pend = []
                for idx in range(len(steps) + LOOK):
                    if idx < len(steps):
                        pend.append(steps[idx][0]())
                    if idx >= LOOK:
                        steps[idx - LOOK][1](pend[idx - LOOK])
                LOOK = LOOK_SAVE
            if dbg and t_ == 1:
                dump('outA_%d' % g, outA, [128, 4, 512], ['outA.%d' % c for c in range(4)])
                dump('outB_%d' % g, outB, [128, 4, 512], ['outB.%d' % c for c in range(4)])
                dump('qN_%d' % g, qNz, [128, 8, 512], ['qN.%d' % c for c in range(4)])
                dump('qR_%d' % g, qR[0:32], [32, 8, 512], ['qR.%d' % c for c in range(8)])
                dump('dqT_%d' % g, dqT, [128, 4, 512], ['dqT.%d' % c for c in range(4)])
                dump('kN_%d' % g, kN, [128, 4, NK], kN_names)
                dump('krT_%d' % g, krT[0:32], [32, NK], krT_names)
                dump('ckvn_%d' % g, ckvn, [128, NK], ckv_names)
                dump('Vm_%d' % g, Vm, [128, NKT, 512], Vm_names)
                dump('dkT_%d' % g, dkT, [128, 4, NK], ['dkT.%d.%d' % (h, t) for h in range(4) for t in range(2)])
                dump('dvt_%d' % g, dvt, [128, NKT, 512], dvt_names)
                dump('cqn_%d' % g, cqn, [128, 2, 512], ['cqn'])
                dump('lamc', lamc[:], [128, 4], ['lamc'])
            for o2 in range(4):
                wO, wOn = wload(w_out_ab[:, o2 * 256:(o2 + 1) * 256].rearrange("(kc p) n -> p kc n", p=128), [128, 8, 256])
                bg()
                bg()
                for oo in range(2):
                    oc = o2 * 2 + oo
                    bk = nb()
                    for kc in range(8):
                        rhs = outA[:, kc, :] if kc < 4 else outB[:, kc - 4, :]
                        rn_ = 'outA.%d' % kc if kc < 4 else 'outB.%d' % (kc - 4)
                        P.add('pe', mm(pb[bk][:], wO[:, kc, oo * 128:(oo + 1) * 128], rhs, start=(kc == 0), stop=(kc == 7)),
                              r=[wOn, rn_], w=['pb%d' % bk])
                    P.add('dve', stt(xT[:, oc, tok], pb[bk][:], mcol(0, g, 2, oc), xT[:, oc, tok], ALU.mult, ALU.add),
                          r=['pb%d' % bk, 'modraw0', 'xT'], w=['xT'])
        while pending_ada:
            bg()


    LN8 = math.log(0.125)

    def mixer1(g):
        arena_reset()
        hm = carve([128, 8, NT], BF16)
        qk = carve([128, 8, 1024], BF16)
        vv = carve([128, 8, 1024], BF16)
        so = carve([128, 8, 1024], BF16)
        hf = carve([128, 8, 1024], BF16)
        gt = carve([128, 8, 32])
        nlf = carve([128, 8, 16])
        argA = carve([128, 8, 16])
        argK = carve([128, 16])
        fac = carve([128, 8, 48])
        eG = carve([128, 8, 16])
        eGs = carve([128, 8, 8])
        State = carve([128, 2, 4, 128])
        Nst = carve([128, 2, 4])
        Stb = carve([128, 2, 4, 128], BF16)
        Nstb = carve([128, 2, 4], BF16)
        qp = [carve([128, 512], BF16) for _ in range(2)]
        kp = [carve([128, 512], BF16) for _ in range(2)]
        kpp = [carve([128, 512], BF16) for _ in range(2)]
        qz = [carve([128, 8, 128], BF16) for _ in range(2)]
        kT = [carve([128, 4, 128], BF16) for _ in range(2)]
        ST = [carve([128, 8, 128], BF16) for _ in range(2)]
        dn = carve([128, 2, 8])
        hs32 = carve([128, 1024])
        ssq = carve([128, 16])
        hsn = carve([128, 1024], BF16)
        hsq = hsn
        if g == 0:
            amaxT = carve([16, 8])
            GtT = carve([16, 8])
            mw = carve([16, 12, 4])
            emfbc = carve([128, 16, 4])
            rhsD = carve([16, 16, 4])
            Cst = carve([128, 4, 128])
            nst_o = carve([128, 4])
        else:
            em0 = carve([128, 16])
        hsT = hm
        cnt = {'i': 0}

        norm_mod(xT, hm, 'hm', 1, g, 0)

        def tm_mm(bk, col0, wv, wn, n, j):
            for kc in range(8):
                P.add('pe', mm(pb[bk][:, col0:col0 + n], hm[:, kc, j * 128:(j + 1) * 128], wv[:, kc, 0:n], start=(kc == 0), stop=(kc == 7)),
                      r=[wn, 'hm.%d' % (j // 4)], w=['pb%d' % bk])

        for ld in range(12):
            c0 = ld * 256
            wv, wn = wload(w_in_c[:, c0:c0 + 256].rearrange("(kc p) n -> p kc n", p=128), [128, 8, 256])
            for j in range(8):
                bk = nb()
                tm_mm(bk, 0, wv, wn, 256, j)
                if c0 < 1024:
                    evac(qk[:, j, c0:c0 + 256], pb[bk][:, 0:256], r=['pb%d' % bk], w=['qk.%d.%d' % (j, ld)])
                elif c0 < 2048:
                    evac(vv[:, j, c0 - 1024:c0 - 1024 + 256], pb[bk][:, 0:256], r=['pb%d' % bk], w=['vv.%d.%d' % (j, ld)])
                else:
                    sov = so[:, j, c0 - 2048:c0 - 2048 + 256]
                    P.add('act', act(sov, pb[bk][:, 0:256], AF.Sigmoid), r=['pb%d' % bk], w=['so.%d.%d' % (j, ld)])
                    P.add('pool', tt(sov.rearrange("p (h v) -> p h v", v=128), sov.rearrange("p (h v) -> p h v", v=128),
                                     vec[:, V_GMLB:V_GMLB + 128].unsqueeze(1).to_broadcast([128, 2, 128]), ALU.mult),
                          r=['so.%d.%d' % (j, ld), 'vec'], w=['so.%d.%d' % (j, ld)])
        qk_names = lambda j: ['qk.%d.%d' % (j, ld) for ld in range(4)]
        vv_names = lambda j: ['vv.%d.%d' % (j, ld) for ld in range(4, 8)]
        so_names = lambda j: ['so.%d.%d' % (j, ld) for ld in range(8, 12)]
        if m1_stop <= 1:
            return
        wv, wn = wload(w_in_c[:, 3072:3104].rearrange("(kc p) n -> p kc n", p=128), [128, 8, 32])
        for j in range(8):
            bk = nb()
            tm_mm(bk, 0, wv, wn, 32, j)
            P.add('dve', tt(gt[:, j, :], pb[bk][:, 0:32], vec[:, V_BGB:V_BGB + 32], ALU.add), r=['pb%d' % bk, 'vec'], w=['gt.%d' % j])
            g4 = gt[:, j, :].rearrange("p (a b) -> p a b", b=8)
            nl = nlf[:, j, :].rearrange("p (a b) -> p a b", b=8)
            P.add('act', act(nl, g4[:, 1::2, :], AF.Exp, scale=-1.0), r=['gt.%d' % j], w=['nlf.%d' % j])
            P.add('act', act(nl, nl, AF.Ln, bias=1.0), r=['nlf.%d' % j], w=['nlf.%d' % j])
            bc = nb()
            for k_, (co, d_) in enumerate(((C_TUI, 0), (C_TLS, 0), (C_TLI, 1), (C_TUS, 1))):
                P.add('pe', mm(pb[bc][:, k_ * 8:(k_ + 1) * 8], cst[:, co:co + 128], nlf[:, j, d_ * 8:(d_ + 1) * 8]), r=['cst', 'nlf.%d' % j], w=['pb%d' % bc])
            P.add('pe', mm(pb[bc][:, 32:48], onesf, nlf[:, j, :]), r=['cst', 'nlf.%d' % j], w=['pb%d' % bc])
            p4 = pb[bc][:, 0:32].rearrange("p (a b) -> p a b", b=8)
            f6 = fac[:, j, :].rearrange("p (a b) -> p a b", b=8)
            aK = argK[:, :].rearrange("p (a b) -> p a b", b=8)
            aA = argA[:, j, :].rearrange("p (a b) -> p a b", b=8)
            P.add('dve', tt(aK, g4[:, 0::2, :], p4[:, 0::2, :], ALU.add), r=['gt.%d' % j, 'pb%d' % bc], w=['argK'])
            P.add('dve', tt(aA, g4[:, 0::2, :], p4[:, 1::2, :], ALU.subtract), r=['gt.%d' % j, 'pb%d' % bc], w=['argA.%d' % j])
            P.add('act', act(f6[:, 0::3, :], p4[:, 0::2, :], AF.Exp, scale=-1.0), r=['pb%d' % bc], w=['fac.%d' % j])
            P.add('act', act(eG[:, j, :], pb[bc][:, 32:48], AF.Exp, scale=-1.0), r=['pb%d' % bc], w=['eG.%d' % j])
            P.add('act', act(f6[:, 1::3, :], aK, AF.Exp, bias=LN8), r=['argK'], w=['fac.%d' % j])
            P.add('act', act(f6[:, 2::3, :], aA, AF.Exp, bias=LN8), r=['argA.%d' % j], w=['fac.%d' % j])
            eg4 = eG[:, j, :].rearrange("p (d c e) -> p d c e", c=4, e=2)
            es = eGs[:, j, :].rearrange("p (d c) -> p d c", c=4)
            for e_ in range(2):
                pe_ = slice(e_ * 64, (e_ + 1) * 64)
                P.add('dve', cp(es[pe_], eg4[pe_, :, :, e_]), r=['eG.%d' % j], w=['eGs.%d' % j])

        if m1_stop <= 2:
            return
        if g == 0:
            for half in range(2):
                bk = nb()
                for jj in range(4):
                    j = half * 4 + jj
                    P.add('pe', tr(pb[bk][0:16, jj * 128:(jj + 1) * 128], argA[:, j, :], idf), r=['argA.%d' % j, 'cst'], w=['pb%d' % bk])
                P.add('dve', lambda e, bk=bk, half=half: e.tensor_reduce(out=amaxT[0:16, half * 4:(half + 1) * 4],
                                                                        in_=pb[bk][0:16, :].rearrange("p (a b) -> p a b", b=128), axis=AX.X, op=ALU.max),
                      r=['pb%d' % bk], w=['amaxT'])
            bk = nb()
            for j in range(8):
                P.add('pe', mm(pb[bk][0:16, j:j + 1], nlf[:, j, :], onesf[:, 0:1]), r=['nlf.%d' % j, 'cst'], w=['pb%d' % bk])
            P.add('dve', cp(GtT[0:16, :], pb[bk][0:16, 0:8]), r=['pb%d' % bk], w=['GtT'])
            am = amaxT[0:16, :].rearrange("p (s t) -> p s t", t=2)
            gm = GtT[0:16, :].rearrange("p (s t) -> p s t", t=2)
            a0, a1, G0, G1 = am[:, :, 0], am[:, :, 1], gm[:, :, 0], gm[:, :, 1]
            W = lambda i: mw[0:16, i, :]
            mops = [
                ts(W(0), G0, -1.0, ALU.mult), tt(W(1), a0, W(0), ALU.max), tt(W(2), W(1), G1, ALU.subtract), tt(W(3), a1, W(2), ALU.max),
                ts(W(4), G1, -1.0, ALU.mult), tt(W(5), a1, W(4), ALU.max), tt(W(6), W(5), G0, ALU.subtract), tt(W(7), a0, W(6), ALU.max),
                ts(W(8), W(3), cst[0:16, C_SELF:C_SELF + 1], ALU.mult),
                stt(W(9), W(7), cst[0:16, C_SELB:C_SELB + 1], W(8), ALU.mult, ALU.add),
            ]
            for f_ in mops:
                P.add('dve', f_, r=['amaxT', 'GtT', 'mw', 'cst'], w=['mw'])
            P.add('act', act(W(10), W(9), AF.Exp, scale=-1.0), r=['mw'], w=['mw'])
            P.add('sp', dma(o_m.rearrange("s d h -> (d h) s"), W(9), slow=True), r=['mw'], chan='out_m')
            P.add('dve', tt(rhsD, cst[0:16, C_ID:C_ID + 16].unsqueeze(2).to_broadcast([16, 16, 4]),
                            W(10).unsqueeze(1).to_broadcast([16, 16, 4]), ALU.mult), r=['mw', 'cst'], w=['rhsD'])
            bk = nb()
            P.add('pe', mm(pb[bk][:, 0:64], onesf[0:16, :], rhsD.rearrange("p a b -> p (a b)")), r=['rhsD', 'cst'], w=['pb%d' % bk])
            P.add('dve', cp(emfbc.rearrange("p a b -> p (a b)"), pb[bk][:, 0:64]), r=['pb%d' % bk], w=['emfbc'])
        else:
            P.add('act', act(em0, vec[:, V_M0:V_M0 + 16], AF.Exp), r=['vec'], w=['em0'])

        if m1_stop <= 3:
            return
        seqs = [[2 * s_, 2 * s_ + 1] for s_ in range(4)] if g == 0 else [list(range(8))]
        mask4 = carve([128, 2, 4, 128], BF16)
        for i__ in range(2):
            P.add('pool', mset(qz[i__], 0.0), w=['qzz'])
        for d_ in range(2):
            co = C_TUI if d_ == 0 else C_TLI
            for hh in range(4):
                P.add('pool', cp(mask4[:, d_, hh, :], cst[:, co:co + 128]), r=['cst'], w=['mask4'])
        for si_, tiles in enumerate(seqs):
            ntl = len(tiles)
            for d in range(2):
                sn_ = 'State.%d' % d
                if g == 0:
                    P.add('pool', mset(State[:, d], 0.0), w=[sn_])
                    P.add('pool', mset(Nst[:, d, :], 0.0), w=['Nst.%d' % d])
                else:
                    P.add('sp', dma(State[:, d], c0_d[:, d]), w=[sn_], chan='ld_c0_%d' % d)
                    e4 = em0[:, :].rearrange("p (d c e) -> p d c e", c=4, e=2)
                    for e_ in range(2):
                        pe_ = slice(e_ * 64, (e_ + 1) * 64)
                        P.add('dve', tt(State[pe_, d], State[pe_, d], e4[pe_, d, :, e_].unsqueeze(2).to_broadcast([64, 4, 128]), ALU.mult),
                              r=[sn_, 'em0'], w=[sn_])
                        P.add('dve', tt(Nst[pe_, d, :], vec[pe_, V_N0 + d * 4:V_N0 + d * 4 + 4], e4[pe_, d, :, e_], ALU.mult),
                              r=['vec', 'em0'], w=['Nst.%d' % d])
                P.add('act', acp(Stb[:, d], State[:, d]), r=[sn_], w=['Stb.%d' % d])
                P.add('act', acp(Nstb[:, d, :], Nst[:, d, :]), r=['Nst.%d' % d], w=['Nstb.%d' % d])
            def tile_step(d, j, k_, ntl=ntl, si_=si_):
                        i_ = d
                        store_h = (k_ < ntl // 2)
                        sn_ = 'State.%d' % d
                        f6 = fac[:, j, :].rearrange("p (a b) -> p a b", b=8)
                        q3 = qk[:, j, 0:512].rearrange("p (h k) -> p h k", k=64)
                        k3 = qk[:, j, 512:1024].rearrange("p (h k) -> p h k", k=64)
                        bc3 = lambda a: f6[:, a, :].unsqueeze(2).to_broadcast([128, 8, 64])
                        P.add('dve', tt(qp[i_].rearrange("p (h k) -> p h k", k=64), q3, bc3(3 * d), ALU.mult), r=qk_names(j) + ['fac.%d' % j], w=['qp%d' % i_])
                        P.add('pool', tt(kp[i_].rearrange("p (h k) -> p h k", k=64), k3, bc3(3 * d + 1), ALU.mult), r=qk_names(j) + ['fac.%d' % j], w=['kp%d' % i_])
                        P.add('dve', tt(kpp[i_].rearrange("p (h k) -> p h k", k=64), k3, bc3(3 * d + 2), ALU.mult), r=qk_names(j) + ['fac.%d' % j], w=['kpp%d' % i_])
                        bt = nb()
                        for c in range(4):
                            P.add('pe', tr(pbb[bt][:, c * 128:(c + 1) * 128], qp[i_][:, c * 128:(c + 1) * 128], idb), r=['qp%d' % i_, 'cstb'], w=['pb%d' % bt])
                        for c in range(4):
                            P.add('pe', tr(pbb[bt][:, (4 + c) * 128:(5 + c) * 128], kp[i_][:, c * 128:(c + 1) * 128], idb), r=['kp%d' % i_, 'cstb'], w=['pb%d' % bt])
                        pt3 = pbb[bt][:, :].rearrange("p (a b) -> p a b", b=128)
                        P.add('act', acp(qz[i_][0:64, 0::2, :], pt3[0:64, 0:4, :]), r=['pb%d' % bt, 'qzz'], w=['qkT%d' % i_])
                        P.add('act', acp(qz[i_][64:128, 1::2, :], pt3[64:128, 0:4, :]), r=['pb%d' % bt, 'qzz'], w=['qkT%d' % i_])
                        P.add('act', acp(kT[i_][:, :, :], pt3[:, 4:8, :]), r=['pb%d' % bt], w=['qkT%d' % i_])
                        yield
                        if m1_stop <= 4:
                            return
                        bsp = [nb(), nb()]
                        for h in range(8):
                            c, e_ = h // 2, h % 2
                            pe_ = slice(e_ * 64, (e_ + 1) * 64)
                            P.add('pe', mm(pb[bsp[e_]][:, c * 128:(c + 1) * 128], kT[i_][:, c, :], qz[i_][:, h, :]), r=['qkT%d' % i_], w=['pb%d' % bsp[e_]])
                        for e_ in range(2):
                            P.add('dve', tt(ST[i_][:, e_::2, :], pb[bsp[e_]][:].rearrange("p (h l) -> p h l", l=128),
                                            mask4[:, d], ALU.mult), r=['pb%d' % bsp[e_], 'mask4'], w=['ST%d.%d' % (i_, e_)])
                        yield
                        if m1_stop <= 5:
                            return
                        bd = nb()
                        bnum = []
                        for h4 in range(2):
                            bn_ = nb()
                            bnum.append(bn_)
                            for hh in range(4):
                                h = h4 * 4 + hh
                                c, e_ = h // 2, h % 2
                                pe_ = slice(e_ * 64, (e_ + 1) * 64)
                                P.add('pe', mm(pb[bn_][:, hh * 128:(hh + 1) * 128], ST[i_][:, h, :], vv[:, j, h * 128:(h + 1) * 128], start=True, stop=False),
                                      r=['ST%d.%d' % (i_, h % 2)] + vv_names(j), w=['pb%d' % bn_])
                                P.add('pe', mm(pb[bn_][:, hh * 128:(hh + 1) * 128], qz[i_][:, h, :], Stb[:, d, c, :], start=False, stop=True),
                                      r=['qkT%d' % i_, 'Stb.%d' % d], w=['pb%d' % bn_])
                        for h in range(8):
                            c, e_ = h // 2, h % 2
                            pe_ = slice(e_ * 64, (e_ + 1) * 64)
                            P.add('pe', mm(pb[bd][:, h:h + 1], ST[i_][:, h, :], onesb[:, 0:1], start=True, stop=False),
                                  r=['ST%d.%d' % (i_, h % 2), 'cstb'], w=['pb%d' % bd])
                            P.add('pe', mm(pb[bd][:, h:h + 1], qz[i_][:, h, :], Nstb[:, d, c:c + 1], start=False, stop=True),
                                  r=['qkT%d' % i_, 'Nstb.%d' % d], w=['pb%d' % bd])
                        yield
                        P.add('act', act(dn[:, i_, :], pb[bd][:, 0:8], AF.Abs), r=['pb%d' % bd], w=['dn%d' % i_])
                        P.add('dve', ts(dn[:, i_, :], dn[:, i_, :], 1.0, ALU.max), r=['dn%d' % i_], w=['dn%d' % i_])
                        P.add('dve', rcp(dn[:, i_, :], dn[:, i_, :]), r=['dn%d' % i_], w=['dn%d' % i_])
                        for h4 in range(2):
                            num3 = pb[bnum[h4]][:].rearrange("p (h v) -> p h v", v=128)
                            rd = dn[:, i_, h4 * 4:(h4 + 1) * 4].unsqueeze(2).to_broadcast([128, 4, 128])
                            if store_h:
                                P.add('dve', tt(hf[:, j, h4 * 512:(h4 + 1) * 512].rearrange("p (h v) -> p h v", v=128), num3, rd, ALU.mult),
                                      r=['pb%d' % bnum[h4], 'dn%d' % i_], w=['hf.%d.%d' % (j, h4)])
                            else:
                                P.add('dve', tt(hs32[:, h4 * 512:(h4 + 1) * 512].rearrange("p (h v) -> p h v", v=128), num3, rd, ALU.mult),
                                      r=['pb%d' % bnum[h4], 'dn%d' % i_], w=['hs32.%d' % h4])
                                P.add('pool' if h4 == 0 else 'dve', tt(hs32[:, h4 * 512:(h4 + 1) * 512], hs32[:, h4 * 512:(h4 + 1) * 512], hf[:, j, h4 * 512:(h4 + 1) * 512], ALU.add),
                                      r=['hs32.%d' % h4, 'hf.%d.%d' % (j, h4)], w=['hs32.%d' % h4])
                        if m1_stop <= 6:
                            return
                        bkv = []
                        for b_ in range(2):
                            bk = nb()
                            bkv.append(bk)
                            for cc in range(2):
                                c = b_ * 2 + cc
                                P.add('pe', mm(pb[bk][:, cc * 256:(cc + 1) * 256], kpp[i_][:, c * 128:(c + 1) * 128], vv[:, j, c * 256:(c + 1) * 256]),
                                      r=['kpp%d' % i_] + vv_names(j), w=['pb%d' % bk])
                        bkn = nb()
                        for c in range(4):
                            P.add('pe', mm(pb[bkn][:, c:c + 1], kpp[i_][:, c * 128:(c + 1) * 128], onesb[:, 0:1]), r=['kpp%d' % i_, 'cstb'], w=['pb%d' % bkn])
                        es = eGs[:, j, :].rearrange("p (d c) -> p d c", c=4)
                        P.add('dve', tt(State[:, d], State[:, d], es[:, d, :].unsqueeze(2).to_broadcast([128, 4, 128]), ALU.mult),
                              r=[sn_, 'eGs.%d' % j, 'Stb.%d' % d], w=[sn_])
                        for b_ in range(2):
                            for e_ in range(2):
                                pe_ = slice(e_ * 64, (e_ + 1) * 64)
                                kvv = pb[bkv[b_]][pe_, :].rearrange("p (c e v) -> p c e v", e=2, v=128)[:, :, e_, :]
                                P.add('dve', tt(State[pe_, d, b_ * 2:b_ * 2 + 2, :], State[pe_, d, b_ * 2:b_ * 2 + 2, :], kvv, ALU.add),
                                      r=[sn_, 'pb%d' % bkv[b_]], w=[sn_])
                        P.add('pool', tt(Nst[:, d, :], Nst[:, d, :], es[:, d, :], ALU.mult), r=['Nst.%d' % d, 'eGs.%d' % j], w=['Nst.%d' % d])
                        P.add('dve', tt(Nst[:, d, :], Nst[:, d, :], pb[bkn][:, 0:4], ALU.add), r=['Nst.%d' % d, 'pb%d' % bkn], w=['Nst.%d' % d])
                        P.add('act', acp(Stb[:, d], State[:, d]), r=[sn_], w=['Stb.%d' % d])
                        P.add('act', acp(Nstb[:, d, :], Nst[:, d, :]), r=['Nst.%d' % d], w=['Nstb.%d' % d])
                        if m1_stop <= 7:
                            return
                        if not store_h:
                            P.add('act', act(hsq[:], hs32[:], AF.Square), r=['hs32.0', 'hs32.1'], w=['hsn'] + ['hsn.%d' % h_ for h_ in range(8)])
                            P.add('dve', lambda e: e.tensor_reduce(out=ssq[:, 0:8], in_=hsq[:].rearrange("p (h v) -> p h v", v=128), axis=AX.X, op=ALU.add),
                                  r=['hsn'], w=['ssq'])
                            P.add('act', act(ssq[:, 8:16], ssq[:, 0:8], AF.Ln, scale=1.0 / 128, bias=EPS), r=['ssq'], w=['ssq'])
                            P.add('act', act(ssq[:, 8:16], ssq[:, 8:16], AF.Exp, scale=-0.5), r=['ssq'], w=['ssq'])
                            h3 = hs32[:].rearrange("p (h v) -> p h v", v=128)
                            for h_ in range(8):
                                P.add('dve',
                                      stt(hsn[:, h_ * 128:(h_ + 1) * 128], hs32[:, h_ * 128:(h_ + 1) * 128], ssq[:, 8 + h_:9 + h_],
                                          so[:, j, h_ * 128:(h_ + 1) * 128], ALU.mult, ALU.mult),
                                      r=['hs32.0', 'hs32.1', 'ssq'] + so_names(j), w=['hsn.%d' % h_, 'hsnr.%d' % h_])
                            bt = nb()
                            for c in range(8):
                                P.add('pe', tr(pbb[bt][:, c * 128:(c + 1) * 128], hsn[:, c * 128:(c + 1) * 128], idb), r=['hsn.%d' % c, 'cstb'], w=['pb%d' % bt])
                            P.add('act', acp(hsT[:, :, j * 128:(j + 1) * 128], pbb[bt][:, :].rearrange("p (c t) -> p c t", t=128)),
                                  r=['pb%d' % bt], w=['hm.%d' % (j // 4)])
            for k_ in range(ntl):
                gens = [tile_step(0, tiles[k_], k_), tile_step(1, tiles[ntl - 1 - k_], k_)]
                while gens:
                    for gn in list(gens):
                        try:
                            next(gn)
                        except StopIteration:
                            gens.remove(gn)
            if g == 0 and m1_stop > 8:
                for d in range(2):
                    for e_ in range(2):
                        pe_ = slice(e_ * 64, (e_ + 1) * 64)
                        emsel = emfbc[pe_, d * 8 + e_:d * 8 + 8:2, si_]
                        P.add('dve', tt(Cst[pe_], State[pe_, d], emsel.unsqueeze(2).to_broadcast([64, 4, 128]), ALU.mult),
                              r=['State.%d' % d, 'emfbc'], w=['Cst'])
                        P.add('dve', tt(nst_o[pe_, :], Nst[pe_, d, :], emsel, ALU.mult), r=['Nst.%d' % d, 'emfbc'], w=['nst_o'])
                        P.add('sp', dma(o_C[si_, d].rearrange("(c e) k v -> e k c v", e=2)[e_], Cst[pe_]), r=['Cst'], chan='out_C')
                        P.add('sp', dma(o_n[si_, d].rearrange("(c e) k -> e k c", e=2)[e_], nst_o[pe_, :], slow=True), r=['nst_o'], chan='out_n')

        for o2 in range(4):
            wO, wOn = wload(w_out_c[:, o2 * 256:(o2 + 1) * 256].rearrange("(kc p) n -> p kc n", p=128), [128, 8, 256])
            for oo in range(2):
                oc = o2 * 2 + oo
                for t_ in range(2):
                    tok = slice(t_ * 512, (t_ + 1) * 512)
                    bk = nb()
                    for kc in range(8):
                        P.add('pe', mm(pb[bk][:], wO[:, kc, oo * 128:(oo + 1) * 128], hsT[:, kc, tok], start=(kc == 0), stop=(kc == 7)),
                              r=[wOn, 'hm.%d' % t_], w=['pb%d' % bk])
                    P.add('dve', stt(xT[:, oc, tok], pb[bk][:], mcol(1, g, 2, oc), xT[:, oc, tok], ALU.mult, ALU.add),
                          r=['pb%d' % bk, 'modraw1', 'xT'], w=['xT'])

    def load_x(g, reset=True):
        if reset:
            arena_reset()
        xin = [carve([128, D]) for _ in range(2)]
        for j in range(8):
            s_ = j % 2
            P.add('sp', dma(xin[s_], xg[g][j * 128:(j + 1) * 128, :]), w=['xin%d' % s_], chan='xin%d' % s_)
            for half in range(2):
                bk = nb()
                for c4 in range(4):
                    c = half * 4 + c4
                    P.add('pe', tr(pb[bk][:, c4 * 128:(c4 + 1) * 128], xin[s_][:, c * 128:(c + 1) * 128], idf),
                          r=['xin%d' % s_, 'cst'], w=['pb%d' % bk])
                evac(xT[:, half * 4:(half + 1) * 4, j * 128:(j + 1) * 128], pb[bk][:].rearrange("p (c t) -> p c t", c=4),
                     r=['pb%d' % bk], w=['xT'])

    def store_y(g):
        arena_reset()
        yT = carve([128, 8, NT])
        yo = [carve([128, D]) for _ in range(2)]
        norm_mod(xT, yT, 'yT', 0, g, 0, final=True)
        for j in range(8):
            s_ = j % 2
            for half in range(2):
                bk = nb()
                for c4 in range(4):
                    c = half * 4 + c4
                    P.add('pe', tr(pb[bk][:, c4 * 128:(c4 + 1) * 128], yT[:, c, j * 128:(j + 1) * 128], idf),
                          r=['yT.%d' % (j // 4), 'cst'], w=['pb%d' % bk])
                evac(yo[s_][:, half * 512:(half + 1) * 512], pb[bk][:], r=['pb%d' % bk], w=['yo%d' % s_])
            P.add('pool', dma(y_d[g][j * 128:(j + 1) * 128, :], yo[s_]), r=['yo%d' % s_], chan='out_y%d' % s_)

    for gi_, g in enumerate(groups):
        if gi_ == 0:
            load_x(g)
        if g == groups[0]:
            ada_slots0 = [carve([128, 8, 256], BF16) for _ in range(2)]
            for f_ in adaln_closures(0):
                f_(ada_slots0)
            pending_ada.extend(adaln_closures(1))
            if 'm0' not in parts:
                while pending_ada:
                    pending_ada.pop(0)(ada_slots0)
        if 'm0' in parts:
            mixer0(g)
            if dbg:
                dump('xm0_%d' % g, xT[:], [128, 8, NT], ['xT'])
        if 'f0' in parts:
            ffn(0, g)
        if 'm1' in parts:
            mixer1(g)
            if dbg:
                dump('xm1_%d' % g, xT[:], [128, 8, NT], ['xT'])
        if 'f1' in parts:
            ffn(1, g)
        store_y(g)
        if gi_ + 1 < len(groups):
            load_x(groups[gi_ + 1], reset=False)

    stats = P.emit()
    return nc, stats, P


def _prep_inputs(inp, ncores=8):
    f = np.float32
    cst = _consts()
    c32, s32, c128, s128 = _rope_tables()
    rope = np.zeros((4, 128, 1024), f)
    rope[0, 0:32] = c32
    rope[1, 0:32] = s32
    rope[2] = c128
    rope[3] = s128
    w_in_ab = np.ascontiguousarray(inp['w_in_ab'][0])
    perm_kr = 384 + _swap_perm(32, 8)
    perm_dq = 416 + _swap_perm(512, 16)
    perm_dk = 928 + _swap_perm(512, 16)
    w_in_sw = np.ascontiguousarray(w_in_ab[:, np.concatenate([perm_kr, perm_dq, perm_dk])])
    wuq = inp['w_uq'][0].reshape(256, 8, 96)
    nope = wuq[:, :, 0:64].reshape(256, 512)
    rp = wuq[:, :, 64:96].reshape(256, 256)
    rp_sw = rp[:, _swap_perm(256, 8)]
    w_uq = np.ascontiguousarray(np.concatenate([nope, rp, rp_sw], 1))
    wukv = inp['w_ukv'][0].reshape(128, 8, 128)
    w_ukv = np.ascontiguousarray(np.concatenate([wukv[:, :, 0:64].reshape(128, 512), wukv[:, :, 64:128].reshape(128, 512)], 1))
    shared = dict(
        cst=cst, rope=rope,
        w_ada=np.ascontiguousarray(inp['w_ada']), w_ffn_in=np.ascontiguousarray(inp['w_ffn_in']),
        w_ffn_out=np.ascontiguousarray(inp['w_ffn_out']), w_in_ab=w_in_ab, w_in_sw=w_in_sw, w_uq=w_uq, w_ukv=w_ukv,
        w_out_ab=np.ascontiguousarray(inp['w_out_ab'][0]), w_in_c=np.ascontiguousarray(inp['w_in_c'][0]),
        w_out_c=np.ascontiguousarray(inp['w_out_c'][0]))

    def fm(v):
        return np.ascontiguousarray(v.reshape(-1, 128).T)

    maps = []
    for i in range(ncores):
        vec = np.zeros((128, NVEC), f)
        vec[:, V_COND:V_COND + 8] = fm(inp['c_ctx'])
        vec[:, V_COND + 8:V_COND + 16] = fm(inp['c'][i])
        for l in range(2):
            vec[:, V_BADA + l * 48:V_BADA + (l + 1) * 48] = fm(inp['b_ada'][l])
        gl = [inp['g_mix'][0], inp['g_ffn'][0], inp['g_mix'][1], inp['g_ffn'][1], inp['g_final']]
        for k, gv in enumerate(gl):
            vec[:, V_G + k * 8:V_G + (k + 1) * 8] = fm(gv)
        vec[:, V_GQL:V_GQL + 2] = fm(inp['g_q_lora'][0])
        vec[:, V_GKV:V_GKV + 1] = fm(inp['g_kv_lora'][0])
        vec[:, V_GSUB:V_GSUB + 1] = fm(inp['g_diff_subln'][0])
        vec[:, V_GKVB:V_GKVB + 128] = inp['g_kv_lora'][0][None, :]
        vec[:, V_GMLB:V_GMLB + 128] = inp['g_mlstm'][0][None, :]
        vec[:, V_BGB:V_BGB + 32] = inp['b_gate_c'][0][None, :]
        vec[:, V_LAMB:V_LAMB + 256] = inp['diff_lambda'][0].reshape(1, 256)
        n0 = inp['state_mlstm_n'][i, 0]
        vec[:, V_N0:V_N0 + 8] = n0.reshape(2, 4, 2, 64).transpose(2, 3, 0, 1).reshape(128, 8)
        vec[:, V_M0:V_M0 + 16] = inp['state_mlstm_m'][i, 0].reshape(1, 16)
        c0 = inp['state_mlstm_C'][i, 0]
        c0 = np.ascontiguousarray(c0.reshape(2, 4, 2, 64, 128).transpose(2, 3, 0, 1, 4).reshape(128, 2, 4, 128))
        m = dict(shared)
        m.update(
            xg=np.ascontiguousarray(np.stack([inp['x_prompt'][4 * i:4 * i + 4].reshape(NT, D), inp['x_sample'][i]])),
            vec=vec, c0=c0,
            ckv_c=np.ascontiguousarray(inp['cache_mla_ckv'][i, 0]), kr_c=np.ascontiguousarray(inp['cache_mla_krope'][i, 0]),
            dk_c=np.ascontiguousarray(inp['cache_diff_k'][i, 0]), dv_c=np.ascontiguousarray(inp['cache_diff_v'][i, 0]))
        maps.append(m)
    return maps


_CACHE = {}


def kernel(**inputs):
    inp = {k: np.asarray(v, dtype=np.float32) for k, v in inputs.items()}
    if 'nc' not in _CACHE:
        _CACHE['nc'] = build()[0]
    nc = _CACHE['nc']
    maps = _prep_inputs(inp, 8)
    res = run_bass_kernel_spmd(nc, maps, core_ids=list(range(8)))
    R = res.results
    y_prompt = np.concatenate([r['y'][0].reshape(4, 256, D) for r in R], 0)
    y_sample = np.stack([r['y'][1] for r in R], 0)
    new_ckv = np.concatenate([r['o_ckv'].reshape(4, 1, 256, 128) for r in R], 0)
    new_kr = np.concatenate([r['o_kr'].reshape(4, 1, 256, 32) for r in R], 0)
    new_dk = np.concatenate([r['o_dk'].reshape(4, 1, 4, 256, 128) for r in R], 0)
    new_dv = np.concatenate([r['o_dv'].reshape(4, 1, 4, 256, 128) for r in R], 0)
    new_C = np.concatenate([r['o_C'].reshape(4, 1, 2, 8, 64, 128) for r in R], 0)
    new_n = np.concatenate([r['o_n'].reshape(4, 1, 2, 8, 64) for r in R], 0)
    new_m = np.concatenate([r['o_m'].reshape(4, 1, 2, 8) for r in R], 0)
    return (y_prompt, y_sample, new_ckv, new_kr, new_dk, new_dv, new_C, new_n, new_m)
```

```python
import math
import numpy as np
import concourse.bass as bass
import concourse.mybir as mybir
from concourse.bass_utils import run_bass_kernel_spmd

F32 = mybir.dt.float32
BF16 = mybir.dt.bfloat16
ALU = mybir.AluOpType
AF = mybir.ActivationFunctionType
AX = mybir.AxisListType

ENGS = ['pe', 'act', 'dve', 'pool', 'sp']
D = 1024
NT = 1024
DFF = 2816
EPS = 1e-6
LAM_INIT0 = 0.8 - 0.6 * math.exp(-0.3 * 0)


class Op:
    __slots__ = ('eng', 'fn', 'deps', 'needed', 'sig', 'chan', 'key', 'seq')
    _n = [0]

    def __init__(self, eng, fn, chan):
        self.eng = eng
        self.fn = fn
        self.chan = chan
        self.deps = set()
        self.needed = chan is not None
        self.sig = 0
        self.key = None
        Op._n[0] += 1
        self.seq = Op._n[0]


class Prog:
    def __init__(self, nc):
        self.nc = nc
        self.ops = {e: [] for e in ENGS}
        self.res_w = {}
        self.res_r = {}

    def add(self, eng, fn, r=(), w=(), chan=None):
        op = Op(eng, fn, chan)
        deps = set()
        if eng != 'pe':
            w = list(w) + [n for n in r if n.startswith('pb') and n not in w]
        for n in r:
            o = self.res_w.get(n)
            if o is not None:
                deps.add(o)
        for n in w:
            o = self.res_w.get(n)
            if o is not None:
                deps.add(o)
            for o in self.res_r.get(n, {}).values():
                deps.add(o)
        for n in r:
            self.res_r.setdefault(n, {})[(eng, chan)] = op
        for n in w:
            self.res_w[n] = op
            self.res_r[n] = {}
        deps.discard(op)
        if eng == 'pe':
            deps = {d for d in deps if not (d.eng == 'pe' and d.chan is None)}
        op.deps = deps
        self.ops[eng].append(op)
        return op

    def barrier(self):
        lasts = []
        for e in ENGS:
            for o in reversed(self.ops[e]):
                if o.fn is not None and o.chan is None:
                    lasts.append(o)
                    break
        chans = {}
        for e in ENGS:
            for o in self.ops[e]:
                if o.chan is not None:
                    chans[o.chan] = o
        lasts += list(chans.values())
        for e in ENGS:
            op = Op(e, None, None)
            op.deps = {d for d in lasts if not (e == 'pe' and d.eng == 'pe' and d.chan is None)}
            self.ops[e].append(op)

    def emit(self, final_wait_prefix='out'):
        nc = self.nc
        chans = {}
        for e in ENGS:
            for o in self.ops[e]:
                if o.chan is not None and o.chan.startswith(final_wait_prefix):
                    chans[o.chan] = o
        fin = Op('sp', None, None)
        fin.deps = set(chans.values())
        self.ops['sp'].append(fin)
        for e in ENGS:
            for op in self.ops[e]:
                for d in op.deps:
                    d.needed = True
        counters = {}
        for e in ENGS:
            for op in self.ops[e]:
                if op.needed:
                    key = ('ch', op.chan) if op.chan else ('eng', e)
                    counters[key] = counters.get(key, 0) + (16 if op.chan else 1)
                    op.sig = counters[key]
                    op.key = key
        sems = {}
        for i, key in enumerate(counters):
            sems[key] = nc.alloc_semaphore("s%d" % i)
        self.counters = counters
        engobj = {'pe': 'tensor', 'act': 'scalar', 'dve': 'vector', 'pool': 'gpsimd', 'sp': 'sync'}
        stats = {}

        chan_ops = {}
        for e in ENGS:
            for op in self.ops[e]:
                if op.chan is not None:
                    chan_ops.setdefault(op.key, []).append((op.seq, op.sig))
        for k in chan_ops:
            chan_ops[k].sort()

        def emit_engine(eo, ops):
            waited = {}
            nw = 0
            for op in ops:
                need = {}
                for d in op.deps:
                    v = d.sig
                    if d.chan is not None:
                        for (sq, sg) in chan_ops[d.key]:
                            if sq < op.seq:
                                v = max(v, sg)
                            else:
                                break
                    if need.get(d.key, 0) < v:
                        need[d.key] = v
                for key, val in need.items():
                    if waited.get(key, 0) < val:
                        eo.wait_ge(sems[key], val)
                        waited[key] = val
                        nw += 1
                if op.fn is not None:
                    ins = op.fn(eo)
                    if op.needed:
                        ins.then_inc(sems[op.key], 16 if op.chan else 1)
            return nw

        with nc.Block() as block:
            for e in ENGS:
                ops = self.ops[e]

                def f(eo, ops=ops, e=e):
                    stats[e] = (len(ops), emit_engine(eo, ops))
                getattr(block, engobj[e])(f)
        self.stats = stats
        return stats


def mm(out, lhsT, rhs, start=True, stop=True, sgc=False):
    if sgc:
        return lambda e: e.matmul(out, lhsT, rhs, start=start, stop=stop, skip_group_check=True)
    return lambda e: e.matmul(out, lhsT, rhs, start=start, stop=stop)


def tr(out, in_, ident):
    return lambda e: e.transpose(out, in_, ident)


def act(out, in_, func, **kw):
    return lambda e: e.activation(out=out, in_=in_, func=func, **kw)


def tt(out, in0, in1, op):
    return lambda e: e.tensor_tensor(out=out, in0=in0, in1=in1, op=op)


def ts(out, in0, s1, op0, s2=None, op1=None):
    if op1 is None:
        return lambda e: e.tensor_scalar(out=out, in0=in0, scalar1=s1, scalar2=None, op0=op0)
    return lambda e: e.tensor_scalar(out=out, in0=in0, scalar1=s1, scalar2=s2, op0=op0, op1=op1)


def stt(out, in0, scalar, in1, op0, op1):
    return lambda e: e.scalar_tensor_tensor(out=out, in0=in0, scalar=scalar, in1=in1, op0=op0, op1=op1)


def cp(out, in_):
    return lambda e: e.tensor_copy(out=out, in_=in_)


def acp(out, in_):
    return lambda e: e.copy(out=out, in_=in_)


def rcp(out, in_):
    return lambda e: e.reciprocal(out=out, in_=in_)


def dma(out, in_, slow=False):
    if slow:
        return lambda e: e.dma_start(out=out, in_=in_, allow_slow_non_contiguous=True)
    return lambda e: e.dma_start(out=out, in_=in_)


def mset(ap, v):
    return lambda e: e.memset(ap, v)


def _rope_tables():
    t = np.arange(1024)
    rows = (t // 64).astype(np.float64)
    cols = (t % 64).astype(np.float64)

    def blk(pos, half):
        fr = np.power(10000.0, -np.arange(half, dtype=np.float64) / half)
        ang = pos[None, :] * fr[:, None]
        c = np.concatenate([np.cos(ang), np.cos(ang)], 0)
        s = np.concatenate([-np.sin(ang), np.sin(ang)], 0)
        return c, s

    c1, s1 = blk(rows, 8)
    c2, s2 = blk(cols, 8)
    c32 = np.concatenate([c1, c2], 0)
    s32 = np.concatenate([s1, s2], 0)
    d1, e1 = blk(rows, 16)
    d2, e2 = blk(cols, 16)
    c64 = np.concatenate([d1, d2], 0)
    s64 = np.concatenate([e1, e2], 0)
    c128 = np.concatenate([c64, c64], 0)
    s128 = np.concatenate([s64, s64], 0)
    return (c32.astype(np.float32), s32.astype(np.float32), c128.astype(np.float32), s128.astype(np.float32))


def _swap_perm(n, half):
    p = np.arange(n)
    for b in range(0, n, 2 * half):
        p[b:b + half] = np.arange(b + half, b + 2 * half)
        p[b + half:b + 2 * half] = np.arange(b, b + half)
    return p


C_ID = 0
C_TUI = 128
C_TLS = 256
C_TLI = 384
C_TUS = 512
C_ONE = 640
C_SELF = 768
C_SELB = 769
NCST = 772

V_COND = 0
V_BADA = 16
V_G = 112
V_GQL = 152
V_GKV = 154
V_GSUB = 155
V_GKVB = 156
V_GMLB = 284
V_BGB = 412
V_LAMB = 444
V_N0 = 700
V_M0 = 708
NVEC = 724


def _consts():
    c = np.zeros((128, NCST), np.float32)
    s = np.arange(128)[:, None]
    l = np.arange(128)[None, :]
    c[:, C_ID:C_ID + 128] = (s == l)
    c[:, C_TUI:C_TUI + 128] = (s <= l)
    c[:, C_TLS:C_TLS + 128] = (s > l)
    c[:, C_TLI:C_TLI + 128] = (s >= l)
    c[:, C_TUS:C_TUS + 128] = (s < l)
    c[:, C_ONE:C_ONE + 128] = 1.0
    c[0:8, C_SELF] = 1.0
    c[8:16, C_SELB] = 1.0
    return c


def build(parts=('m0', 'f0', 'm1', 'f1'), dbg=False, groups=(0, 1), m0_stop=99, skip=(), m1_stop=99):
    nc = bass.Bass("TRN2", target_bir_lowering=False)
    P = Prog(nc)

    def din(name, shape):
        return nc.dram_tensor(name, list(shape), F32, kind="ExternalInput").ap()

    def dout(name, shape):
        return nc.dram_tensor(name, list(shape), F32, kind="ExternalOutput").ap()

    xg = din("xg", [2, NT, D])
    cst_d = din("cst", [128, NCST])
    vec_d = din("vec", [128, NVEC])
    rope_d = din("rope", [4, 128, 1024])
    w_ada = din("w_ada", [2, D, 6 * D])
    w_ffn_in = din("w_ffn_in", [2, D, 2 * DFF])
    w_ffn_out = din("w_ffn_out", [2, DFF, D])
    w_in_ab = din("w_in_ab", [D, 1952])
    w_in_sw = din("w_in_sw", [D, 1056])
    w_uq = din("w_uq", [256, 1024])
    w_ukv = din("w_ukv", [128, 1024])
    w_out_ab = din("w_out_ab", [D, D])
    w_in_c = din("w_in_c", [D, 3104])
    w_out_c = din("w_out_c", [D, D])
    ckv_c = din("ckv_c", [256, 128])
    kr_c = din("kr_c", [256, 32])
    dk_c = din("dk_c", [4, 256, 128])
    dv_c = din("dv_c", [4, 256, 128])
    c0_d = din("c0", [128, 2, 4, 128])
    y_d = dout("y", [2, NT, D])
    o_ckv = dout("o_ckv", [NT, 128])
    o_kr = dout("o_kr", [NT, 32])
    o_dk = dout("o_dk", [4, 4, 256, 128])
    o_dv = dout("o_dv", [4, 4, 256, 128])
    o_C = dout("o_C", [4, 2, 8, 64, 128])
    o_n = dout("o_n", [4, 2, 8, 64])
    o_m = dout("o_m", [4, 2, 8])
    dbg_out = {}

    def sb(name, shape, dt=F32):
        return nc.alloc_sbuf_tensor(name, list(shape), dt)

    cst = sb("cstS", [128, NCST])
    vec = sb("vecS", [128, NVEC])
    cstb = sb("cstB", [128, NCST], BF16)
    xT = sb("xT", [128, 8, NT])
    modraw = [sb("modraw%d" % l, [128, 48, 2]) for l in range(2)]
    modA = sb("modA", [128, 2, 2, 2, 8])
    condb = sb("condb", [128, 2, 8], BF16)
    stage = [sb("stage%d" % i, [128, 2048]) for i in range(2)]
    wsl = [sb("wsl%d" % i, [128, 2048], BF16) for i in range(3)]
    sqb = [sb("sqb%d" % i, [128, 512], BF16) for i in range(2)]
    tmpf = [sb("tmpf%d" % i, [128, 512]) for i in range(4)]
    rstdb = [sb("rstd%d" % i, [128, 512]) for i in range(2)]
    joinb = sb("joinb", [128, 8])
    pb = [nc.alloc_psum_tensor("pb%d" % i, [128, 512], F32) for i in range(8)]
    pbb = [p[:].bitcast(BF16) for p in pb]
    ARENA = 118 * 1024
    arena = sb("arena", [128, ARENA // 4])

    idf = cst[:, C_ID:C_ID + 128]
    idb = cstb[:, C_ID:C_ID + 128]
    onesb = cstb[:, C_ONE:C_ONE + 128]
    onesf = cst[:, C_ONE:C_ONE + 128]

    st = {'bank': 0, 'stage': 0, 'wsl': 0, 'ev': 0, 'tmpf': 0, 'sqb': 0, 'aoff': 0, 'castq': 0, 'rstd': 0, 'reserved': set()}

    def nb():
        while True:
            b = st['bank']
            st['bank'] = (b + 1) % 8
            if b not in st['reserved']:
                return b

    def ev_eng():
        st['ev'] ^= 1
        return 'act' if st['ev'] else 'dve'

    def evac(out, in_, r, w):
        e = ev_eng()
        if e == 'act':
            P.add('act', acp(out, in_), r=r, w=w)
        else:
            P.add('dve', cp(out, in_), r=r, w=w)

    def ntmp():
        i = st['tmpf']
        st['tmpf'] = (i + 1) % 4
        return i

    def carve(shape, dt=F32, parts=128):
        n = 1
        for s_ in shape[1:]:
            n *= s_
        esz = 2 if dt == BF16 else 4
        nbytes = (n * esz + 31) // 32 * 32
        off = st['aoff']
        assert off + nbytes <= ARENA, ("arena overflow", off, nbytes)
        st['aoff'] = off + nbytes
        a = arena[0:shape[0], off // 4: off // 4 + nbytes // 4]
        if dt == BF16:
            a = a.bitcast(BF16)
        a = a[:, 0:n]
        if len(shape) == 3:
            a = a.rearrange("p (a b) -> p a b", b=shape[2])
        elif len(shape) == 4:
            a = a.rearrange("p (a b c) -> p a b c", b=shape[2], c=shape[3])
        return a

    def arena_reset():
        P.barrier()
        st['aoff'] = 0

    def dump(name, ap, shape, r):
        if not dbg or name in dbg_out:
            return
        d = nc.dram_tensor("dbg_" + name, list(shape), F32, kind="ExternalOutput").ap()
        dbg_out[name] = d
        P.add('pool', dma(d, ap), r=r, chan='out_dbg_' + name)

    st['stage_pool'] = [(stage[i][:], 'stage%d' % i) for i in range(2)]
    st['wsl_pool'] = [(wsl[i][:], 'wsl%d' % i) for i in range(3)]
    base_pools = (st['stage_pool'], st['wsl_pool'])

    def sload(src, shape):
        n = 1
        for s_ in shape[1:]:
            n *= s_
        assert n <= 2048
        pool_ = st['stage_pool']
        si = st['stage'] % len(pool_)
        st['stage'] = si + 1
        stg_ap, stg_name = pool_[si]
        sv = stg_ap[0:shape[0], 0:n]
        if len(shape) == 3:
            sv = sv.rearrange("p (a b) -> p a b", b=shape[2])
        elif len(shape) == 4:
            sv = sv.rearrange("p (a b c) -> p a b c", b=shape[2], c=shape[3])
        P.add('sp', dma(sv, src), w=[stg_name], chan=stg_name)
        return sv, stg_name

    def wload(src, shape, dst=None, dstname=None):
        sv, sn = sload(src, shape)
        if dst is None:
            n = 1
            for s_ in shape[1:]:
                n *= s_
            wpool = st['wsl_pool']
            wi = st['wsl'] % len(wpool)
            st['wsl'] = wi + 1
            w_ap, dstname = wpool[wi]
            wv = w_ap[0:shape[0], 0:n]
            if len(shape) == 3:
                wv = wv.rearrange("p (a b) -> p a b", b=shape[2])
        else:
            wv = dst
        P.add('act', acp(wv, sv), r=[sn], w=[dstname])
        return wv, dstname

    P.add('sp', dma(cst[:], cst_d), w=['cst'], chan='ld_cst')
    P.add('sp', dma(vec[:], vec_d), w=['vec'], chan='ld_vec')
    P.add('dve', cp(cstb[:], cst[:]), r=['cst'], w=['cstb'])

    P.add('act', act(condb[:], vec[:, V_COND:V_COND + 16].rearrange("p (g c) -> p g c", c=8), AF.Silu), r=['vec'], w=['condb'])
    def adaln_closures(l):
        def make(gi):
            def run(slots):
                bk = nb()
                src = w_ada[l][:, gi * 256:(gi + 1) * 256].rearrange("(kc p) n -> p kc n", p=128)
                wv, wn = wload(src, [128, 8, 256], dst=slots[gi % 2], dstname='adas%d' % (gi % 2))
                for c2 in range(2):
                    for kc in range(8):
                        P.add('pe', mm(pb[bk][:, c2 * 2:c2 * 2 + 2], wv[:, kc, c2 * 128:(c2 + 1) * 128], condb[:, :, kc],
                                       start=(kc == 0), stop=(kc == 7)), r=[wn, 'condb'], w=['pb%d' % bk])
                badaT = vec[:, V_BADA + l * 48 + gi * 2:V_BADA + l * 48 + gi * 2 + 2]
                P.add('dve', tt(modraw[l][:, gi * 2:gi * 2 + 2, :], pb[bk][:, 0:4].rearrange("p (o g) -> p o g", g=2),
                                badaT.unsqueeze(2).to_broadcast([128, 2, 2]), ALU.add), r=['pb%d' % bk, 'vec'], w=['modraw%d.%d' % (l, gi)])
            return run

        def fin(slots):
            names = ['modraw%d.%d' % (l, gi) for gi in range(24)]
            first = True
            for g in range(2):
                for wh in range(2):
                    sc = modraw[l][:, (1 + 3 * wh) * 8:(2 + 3 * wh) * 8, g]
                    gv = vec[:, V_G + (2 * l + wh) * 8:V_G + (2 * l + wh + 1) * 8]
                    P.add('dve', stt(modA[:, l, g, wh, :], sc, 1.0, gv, ALU.add, ALU.mult), r=names + ['vec'],
                          w=['modA'] + (['modraw%d' % l] if first else []))
                    first = False
        return [make(gi) for gi in range(24)] + [fin]

    pending_ada = []

    dump('modraw0', modraw[0][:], [128, 48, 2], ['modraw0'])
    dump('modA', modA[:], [128, 2, 2, 2, 8], ['modA'])

    def mcol(l, g, j, c):
        return modraw[l][:, j * 8 + c, g:g + 1]

    def norm_mod(src, dst, dstname, l, g, wh, n_tt=2, final=False):
        for t_ in range(n_tt):
            tok = slice(t_ * 512, (t_ + 1) * 512)
            bk = nb()
            for c in range(8):
                si = st['sqb']
                st['sqb'] ^= 1
                if c % 3 == 0:
                    P.add('act', act(sqb[si][:], src[:, c, tok], AF.Square), r=['xT'], w=['sqb%d' % si])
                elif c % 3 == 1:
                    P.add('dve', tt(sqb[si][:], src[:, c, tok], src[:, c, tok], ALU.mult), r=['xT'], w=['sqb%d' % si])
                else:
                    P.add('pool', tt(sqb[si][:], src[:, c, tok], src[:, c, tok], ALU.mult), r=['xT'], w=['sqb%d' % si])
                P.add('pe', mm(pb[bk][:], onesb, sqb[si][:], start=(c == 0), stop=(c == 7)), r=['sqb%d' % si, 'cstb'], w=['pb%d' % bk])
            ri = st['rstd']
            st['rstd'] ^= 1
            rstd = rstdb[ri]
            rn = 'rstd%d' % ri
            P.add('act', act(rstd[:], pb[bk][:], AF.Ln, scale=1.0 / D, bias=EPS), r=['pb%d' % bk], w=[rn])
            P.add('act', act(rstd[:], rstd[:], AF.Exp, scale=-0.5), r=[rn], w=[rn])
            for c in range(8):
                ti = ntmp()
                if final:
                    gv = vec[:, V_G + 4 * 8 + c:V_G + 4 * 8 + c + 1]
                    P.add('dve', stt(dst[:, c, tok], src[:, c, tok], gv, rstd[:], ALU.mult, ALU.mult),
                          r=['xT', rn, 'vec'], w=[dstname + '.%d' % t_])
                else:
                    P.add('dve', tt(tmpf[ti][:], src[:, c, tok], rstd[:], ALU.mult), r=['xT', rn], w=['tmpf%d' % ti])
                    if c % 8 in (2, 5, 7):
                        P.add('dve', ts(dst[:, c, tok], tmpf[ti][:], modA[:, l, g, wh, c:c + 1], ALU.mult, mcol(l, g, 3 * wh, c), ALU.add),
                              r=['tmpf%d' % ti, 'modA', 'modraw%d' % l], w=[dstname + '.%d.%d' % (t_, c)])
                    else:
                        P.add('act', act(dst[:, c, tok], tmpf[ti][:], AF.Identity, scale=modA[:, l, g, wh, c:c + 1],
                                         bias=mcol(l, g, 3 * wh, c)), r=['tmpf%d' % ti, 'modA', 'modraw%d' % l], w=[dstname + '.%d.%d' % (t_, c)])
            if not final:
                P.add('dve', mset(joinb[:, t_:t_ + 1], 0.0), r=[dstname + '.%d.%d' % (t_, c) for c in range(8)], w=[dstname + '.%d' % t_])

    def ffn(l, g):
        arena_reset()
        hm = carve([128, 8, NT], BF16)
        u = carve([128, 22, NT], BF16)
        ada_sl = [carve([128, 8, 256], BF16) for _ in range(2)]
        st['stage_pool'] = base_pools[0] + [(carve([128, 2048]), 'stageF%d' % i) for i in range(2)]
        st['wsl_pool'] = base_pools[1] + [(carve([128, 2048], BF16), 'wslF%d' % i) for i in range(3)]

        def bg():
            if pending_ada:
                pending_ada.pop(0)(ada_sl)
        norm_mod(xT, hm, 'hm', l, g, 1)
        if l == 0 and g == 0:
            dump('xT0', xT[:], [128, 8, NT], ['xT'])
            dump('hm0', hm, [128, 8, NT], ['hm.0', 'hm.1'])
        for j2 in range(11):
            wa, wan = wload(w_ffn_in[l][:, j2 * 256:(j2 + 1) * 256].rearrange("(kc p) n -> p kc n", p=128), [128, 8, 256])
            wb_, wbn = wload(w_ffn_in[l][:, DFF + j2 * 256:DFF + (j2 + 1) * 256].rearrange("(kc p) n -> p kc n", p=128), [128, 8, 256])
            bg()
            for jj in range(2):
                j = j2 * 2 + jj
                for t_ in range(2):
                    tok = slice(t_ * 512, (t_ + 1) * 512)
                    ba = nb()
                    for kc in range(8):
                        P.add('pe', mm(pb[ba][:], wa[:, kc, jj * 128:(jj + 1) * 128], hm[:, kc, tok], start=(kc == 0), stop=(kc == 7)),
                              r=[wan, 'hm.%d' % t_], w=['pb%d' % ba])
                    bb = nb()
                    for kc in range(8):
                        P.add('pe', mm(pb[bb][:], wb_[:, kc, jj * 128:(jj + 1) * 128], hm[:, kc, tok], start=(kc == 0), stop=(kc == 7)),
                              r=[wbn, 'hm.%d' % t_], w=['pb%d' % bb])
                    ti = ntmp()
                    P.add('act', act(tmpf[ti][:], pb[ba][:], AF.Silu), r=['pb%d' % ba], w=['tmpf%d' % ti])
                    P.add('dve', tt(u[:, j, tok], tmpf[ti][:], pb[bb][:], ALU.mult), r=['tmpf%d' % ti, 'pb%d' % bb], w=['u.%d.%d' % (j, t_)])
        if l == 0 and g == 0:
            dump('u0', u, [128, 22, NT], ['u.%d.%d' % (j, t_) for j in range(22) for t_ in range(2)])
        for oc in range(8):
            w1, w1n = wload(w_ffn_out[l][0:11 * 128, oc * 128:(oc + 1) * 128].rearrange("(j p) n -> p j n", p=128), [128, 11, 128])
            w2, w2n = wload(w_ffn_out[l][11 * 128:22 * 128, oc * 128:(oc + 1) * 128].rearrange("(j p) n -> p j n", p=128), [128, 11, 128])
            bg()
            bg()
            for t_ in range(2):
                tok = slice(t_ * 512, (t_ + 1) * 512)
                bk = nb()
                for j in range(22):
                    wv = w1 if j < 11 else w2
                    P.add('pe', mm(pb[bk][:], wv[:, j % 11, :], u[:, j, tok], start=(j == 0), stop=(j == 21)),
                          r=[w1n if j < 11 else w2n, 'u.%d.%d' % (j, t_)], w=['pb%d' % bk])
                P.add('dve', stt(xT[:, oc, tok], pb[bk][:], mcol(l, g, 5, oc), xT[:, oc, tok], ALU.mult, ALU.add),
                      r=['pb%d' % bk, 'modraw%d' % l, 'xT'], w=['xT'])
        while pending_ada:
            bg()
        st['stage_pool'], st['wsl_pool'] = base_pools


    lamc = sb("lamc", [128, 4])
    lamt = sb("lamt", [128, 2, 64])
    lv = vec[:, V_LAMB:V_LAMB + 256].rearrange("p (a b) -> p a b", b=64)
    P.add('dve', tt(lamt[:], lv[:, 0::2, :], lv[:, 1::2, :], ALU.mult), r=['vec'], w=['lamt'])
    P.add('dve', lambda e: e.tensor_reduce(out=lamc[:, 2:4], in_=lamt[:], axis=AX.X, op=ALU.add), r=['lamt'], w=['lamc'])
    P.add('act', act(lamc[:, 2:4], lamc[:, 2:4], AF.Exp), r=['lamc'], w=['lamc'])
    P.add('dve', tt(lamc[:, 0:1], lamc[:, 2:3], lamc[:, 3:4], ALU.subtract), r=['lamc'], w=['lamc'])
    P.add('dve', ts(lamc[:, 0:1], lamc[:, 0:1], LAM_INIT0, ALU.add), r=['lamc'], w=['lamc'])
    P.add('dve', ts(lamc[:, 1:2], lamc[:, 0:1], -1.0, ALU.mult), r=['lamc'], w=['lamc'])
    gsubc = sb("gsubc", [128, 1])
    P.add('dve', ts(gsubc[:], vec[:, V_GSUB:V_GSUB + 1], 1.0 - LAM_INIT0, ALU.mult), r=['vec'], w=['gsubc'])

    SCALE_A = 96.0 ** -0.5
    SCALE_B = 64.0 ** -0.5

    def mixer0(g):
        arena_reset()
        l = 0
        NK = 1024 if g == 0 else 1280
        NKT = NK // 128
        hm = carve([128, 8, NT], BF16)
        kN = carve([128, 4, NK], BF16)
        krT = carve([128, NK], BF16)
        ckvn = carve([128, NK], BF16)
        Vm = carve([128, NKT, 512], BF16)
        dkT = carve([128, 4, NK], BF16)
        dvt = carve([128, NKT, 512], BF16)
        wukv_b = carve([128, 1024], BF16)
        wuq_b = carve([128, 2, 1024], BF16)
        cqn = carve([128, 2, 512], BF16)
        qNz = carve([128, 8, 512], BF16)
        qR = carve([128, 8, 512], BF16)
        dqT = carve([128, 4, 512], BF16)
        outA = carve([128, 4, 512], BF16)
        outB = carve([128, 4, 512], BF16)
        PT = [carve([128, 512], BF16) for _ in range(6 if g == 1 else 4)]
        ES = [carve([128, 512]) for _ in range(2)]
        ESb = [carve([128, 512], BF16) for _ in range(2)]
        if g == 0:
            stg = [carve([128, 512]) for _ in range(4)]
        if g == 1:
            rc32 = carve([32, 512])
            rs32 = carve([32, 512])
            rc128 = carve([128, 512])
            rs128 = carve([128, 512])

            def load_rope(t_):
                tk = slice(t_ * 512, (t_ + 1) * 512)
                P.add('sp', dma(rc32, rope_d[0][0:32, tk]), w=['rc32'], chan='ld_rope0')
                P.add('sp', dma(rs32, rope_d[1][0:32, tk]), w=['rs32'], chan='ld_rope1')
                P.add('sp', dma(rc128, rope_d[2][:, tk]), w=['rc128'], chan='ld_rope2')
                P.add('sp', dma(rs128, rope_d[3][:, tk]), w=['rs128'], chan='ld_rope3')
        cnt = {'pt': 0, 'stg': 0, 'sb': 0, 'ab': 0, 'es': 0, 'eseng': 0}

        def nsb():
            cnt['sb'] = (cnt['sb'] + 1) % 4
            return cnt['sb']

        def nab():
            cnt['ab'] = (cnt['ab'] + 1) % 4
            return 4 + cnt['ab']

        def nstg():
            cnt['stg'] = (cnt['stg'] + 1) % 4
            return cnt['stg']

        ada_sl = [carve([128, 8, 256], BF16) for _ in range(2)] if pending_ada else None

        def bg():
            if pending_ada:
                pending_ada.pop(0)(ada_sl)

        P.add('pool', mset(qNz, 0.0), w=['qNz'])
        P.add('pool', mset(qR, 0.0), w=['qRz'])
        P.add('pool', mset(krT, 0.0), w=['krTz'])
        norm_mod(xT, hm, 'hm', 0, g, 0)

        def fm_mm(bk, wv, wn, c0, M, t_):
            tok = slice(t_ * 512, (t_ + 1) * 512)
            for kc in range(8):
                P.add('pe', mm(pb[bk][0:M, :], wv[:, kc, c0:c0 + M], hm[:, kc, tok], start=(kc == 0), stop=(kc == 7)),
                      r=[wn, 'hm.%d' % t_], w=['pb%d' % bk])

        def tm_mm(bk, col0, wv, wn, n, j):
            for kc in range(8):
                P.add('pe', mm(pb[bk][:, col0:col0 + n], hm[:, kc, j * 128:(j + 1) * 128], wv[:, kc, 0:n], start=(kc == 0), stop=(kc == 7)),
                      r=[wn, 'hm.%d' % (j // 4)], w=['pb%d' % bk])

        def rope_evac(dst, dstname, M, bk, bks, ctab, stab, tok, cn, sn_, dst2=None, extra_r=()):
            t1 = ntmp()
            P.add('dve', tt(tmpf[t1][0:M, :], pb[bk][0:M, :], ctab[0:M, :], ALU.mult), r=['pb%d' % bk, cn], w=['tmpf%d' % t1])
            t2 = ntmp()
            P.add('dve', tt(tmpf[t2][0:M, :], pb[bks][0:M, :], stab[0:M, :], ALU.mult), r=['pb%d' % bks, sn_], w=['tmpf%d' % t2])
            P.add('pool', tt(dst, tmpf[t1][0:M, :], tmpf[t2][0:M, :], ALU.add), r=['tmpf%d' % t1, 'tmpf%d' % t2] + list(extra_r), w=[dstname])
            if dst2 is not None:
                P.add('pool', tt(dst2, tmpf[t1][0:M, :], tmpf[t2][0:M, :], ALU.add), r=['tmpf%d' % t1, 'tmpf%d' % t2], w=[dstname])

        wukv_b, _ = wload(w_ukv, [128, 1024], dst=wukv_b, dstname='wukv')
        wuq_b, _ = wload(w_uq.rearrange("(kc p) n -> p kc n", p=128), [128, 2, 1024], dst=wuq_b, dstname='wuq')
        wA, wAn = wload(w_in_ab[:, 256:416].rearrange("(kc p) n -> p kc n", p=128), [128, 8, 160])
        bg()
        bg()
        if g == 1:
            wAs, wAsn = wload(w_in_sw[:, 0:32].rearrange("(kc p) n -> p kc n", p=128), [128, 8, 32])
        for t_ in range(2):
            tok = slice(t_ * 512, (t_ + 1) * 512)
            gtok = slice(t_ * 512, (t_ + 1) * 512)
            if g == 1:
                load_rope(t_)
            bk = nb()
            fm_mm(bk, wA, wAn, 0, 128, t_)
            si = st['sqb']
            st['sqb'] ^= 1
            P.add('act', act(sqb[si][:], pb[bk][:], AF.Square), r=['pb%d' % bk], w=['sqb%d' % si])
            b2 = nb()
            P.add('pe', mm(pb[b2][:], onesb, sqb[si][:]), r=['sqb%d' % si, 'cstb'], w=['pb%d' % b2])
            ri = st['rstd']
            st['rstd'] ^= 1
            rn = 'rstd%d' % ri
            P.add('act', act(rstdb[ri][:], pb[b2][:], AF.Ln, scale=1.0 / 128, bias=EPS), r=['pb%d' % b2], w=[rn])
            P.add('act', act(rstdb[ri][:], rstdb[ri][:], AF.Exp, scale=-0.5), r=[rn], w=[rn])
            P.add('dve', stt(ckvn[:, gtok], pb[bk][:], vec[:, V_GKV:V_GKV + 1], rstdb[ri][:], ALU.mult, ALU.mult),
                  r=['pb%d' % bk, rn, 'vec'], w=['ckvn.%d' % t_])
            bk = nb()
            fm_mm(bk, wA, wAn, 128, 32, t_)
            if g == 0:
                evac(krT[0:32, gtok], pb[bk][0:32, :], r=['pb%d' % bk, 'krTz'], w=['krT.%d' % t_])
            else:
                bks = nb()
                fm_mm(bks, wAs, wAsn, 0, 32, t_)
                rope_evac(krT[0:32, gtok], 'krT.%d' % t_, 32, bk, bks, rc32, rs32, tok, 'rc32', 'rs32', extra_r=['krTz'])
        if g == 0:
            for j in range(8):
                bk = nb()
                tm_mm(bk, 0, wA, wAn, 160, j)
                si_ = nstg()
                sg = stg[si_]
                ti = ntmp()
                P.add('act', act(tmpf[ti][:, 0:128], pb[bk][:, 0:128], AF.Square, accum_out=tmpf[ti][:, 128:129]), r=['pb%d' % bk], w=['tmpf%d' % ti])
                P.add('act', act(tmpf[ti][:, 129:130], tmpf[ti][:, 128:129], AF.Ln, scale=1.0 / 128, bias=EPS), r=['tmpf%d' % ti], w=['tmpf%d' % ti])
                P.add('act', act(tmpf[ti][:, 130:131], tmpf[ti][:, 129:130], AF.Exp, scale=-0.5), r=['tmpf%d' % ti], w=['tmpf%d' % ti])
                P.add('dve', stt(sg[:, 0:128], pb[bk][:, 0:128], tmpf[ti][:, 130:131], vec[:, V_GKVB:V_GKVB + 128], ALU.mult, ALU.mult),
                      r=['pb%d' % bk, 'tmpf%d' % ti, 'vec'], w=['stg%d' % si_])
                P.add('act', acp(sg[:, 128:160], pb[bk][:, 128:160]), r=['pb%d' % bk], w=['stg%d' % si_])
                P.add('pool', dma(o_ckv[j * 128:(j + 1) * 128, :], sg[:, 0:128]), r=['stg%d' % si_], chan='out_stg%d' % si_)
                P.add('pool', dma(o_kr[j * 128:(j + 1) * 128, :], sg[:, 128:160]), r=['stg%d' % si_], chan='out_stg%d' % si_)
        if m0_stop <= 1:
            return
        for hh in range(0 if 'dk' in skip else 2):
            wK, wKn = wload(w_in_ab[:, 928 + hh * 256:928 + (hh + 1) * 256].rearrange("(kc p) n -> p kc n", p=128), [128, 8, 256])
            bg()
            if g == 1:
                wKs, wKsn = wload(w_in_sw[:, 544 + hh * 256:544 + (hh + 1) * 256].rearrange("(kc p) n -> p kc n", p=128), [128, 8, 256])
            for t_ in range(2):
                if g == 1:
                    load_rope(t_)
                for h2 in range(2):
                    h = hh * 2 + h2
                    tok = slice(t_ * 512, (t_ + 1) * 512)
                    bk = nb()
                    fm_mm(bk, wK, wKn, h2 * 128, 128, t_)
                    if g == 0:
                        evac(dkT[:, h, tok], pb[bk][:], r=['pb%d' % bk], w=['dkT.%d.%d' % (h, t_)])
                    else:
                        bks = nb()
                        fm_mm(bks, wKs, wKsn, h2 * 128, 128, t_)
                        rope_evac(dkT[:, h, tok], 'dkT.%d.%d' % (h, t_), 128, bk, bks, rc128, rs128, tok, 'rc128', 'rs128')
            if g == 0:
                for j in range(8):
                    bk = nb()
                    tm_mm(bk, 0, wK, wKn, 256, j)
                    si_ = nstg()
                    evac(stg[si_][:, 0:256], pb[bk][:, 0:256], r=['pb%d' % bk], w=['stg%d' % si_])
                    if 'odk' not in skip:
                      P.add('pool', dma(o_dk[j // 2, 2 * hh:2 * hh + 2, (j % 2) * 128:(j % 2 + 1) * 128, :].rearrange("h t d -> t h d"),
                                    stg[si_][:, 0:256].rearrange("t (h d) -> t h d", d=128)), r=['stg%d' % si_], chan='out_stg%d' % si_)
        wV0, wV0n = wload(w_in_ab[:, 1440:1696].rearrange("(kc p) n -> p kc n", p=128), [128, 8, 256])
        wV1, wV1n = wload(w_in_ab[:, 1696:1952].rearrange("(kc p) n -> p kc n", p=128), [128, 8, 256])
        bg()
        bg()
        for j in range(0 if 'dv' in skip else 8):
            bk = nb()
            tm_mm(bk, 0, wV0, wV0n, 256, j)
            tm_mm(bk, 256, wV1, wV1n, 256, j)
            P.add('act', acp(dvt[:, j, :], pb[bk][:]), r=['pb%d' % bk], w=['dvt.%d' % j])
            if g == 0:
                si_ = nstg()
                P.add('dve', cp(stg[si_][:], pb[bk][:]), r=['pb%d' % bk], w=['stg%d' % si_])
                if 'odv' not in skip:
                  P.add('pool', dma(o_dv[j // 2, :, (j % 2) * 128:(j % 2 + 1) * 128, :].rearrange("h t d -> t h d"),
                                stg[si_][:].rearrange("t (h d) -> t h d", d=128)), r=['stg%d' % si_], chan='out_stg%d' % si_)
        if g == 1:
            sv, sn = sload(ckv_c.rearrange("(a p) f -> p a f", p=128), [128, 2, 128])
            bk = nb()
            for a in range(2):
                P.add('pe', tr(pb[bk][:, a * 128:(a + 1) * 128], sv[:, a, :], idf), r=[sn, 'cst'], w=['pb%d' % bk])
            evac(ckvn[:, 1024:1280], pb[bk][:, 0:256], r=['pb%d' % bk], w=['ckvn.c'])
            sv, sn = sload(kr_c.rearrange("(a p) f -> p a f", p=128), [128, 2, 32])
            bk = nb()
            for a in range(2):
                P.add('pe', tr(pb[bk][0:32, a * 128:(a + 1) * 128], sv[:, a, :], idf), r=[sn, 'cst'], w=['pb%d' % bk])
            evac(krT[0:32, 1024:1280], pb[bk][0:32, 0:256], r=['pb%d' % bk, 'krTz'], w=['krT.c'])
            sv, sn = sload(dk_c.rearrange("h (a p) f -> p h a f", p=128), [128, 4, 2, 128])
            for h in range(4):
                bk = nb()
                for a in range(2):
                    P.add('pe', tr(pb[bk][:, a * 128:(a + 1) * 128], sv[:, h, a, :], idf), r=[sn, 'cst'], w=['pb%d' % bk])
                evac(dkT[:, h, 1024:1280], pb[bk][:, 0:256], r=['pb%d' % bk], w=['dkT.%d.c' % h])
            sv, sn = sload(dv_c.rearrange("h (a p) f -> p h a f", p=128), [128, 4, 2, 128])
            for a in range(2):
                P.add('pool', cp(dvt[:, 8 + a, :].rearrange("p (h f) -> p h f", f=128), sv[:, :, a, :]), r=[sn], w=['dvt.%d' % (8 + a)])
        if m0_stop <= 2:
            return
        ckv_names = ['ckvn.0', 'ckvn.1'] + (['ckvn.c'] if g == 1 else [])
        kblocks = [(0, 512), (512, 512)] + ([(1024, 256)] if g == 1 else [])
        for c in range(4):
            for (k0, kn_) in kblocks:
                bk = nb()
                P.add('pe', mm(pb[bk][:, 0:kn_], wukv_b[:, c * 128:(c + 1) * 128], ckvn[:, k0:k0 + kn_]), r=['wukv'] + ckv_names, w=['pb%d' % bk])
                evac(kN[:, c, k0:k0 + kn_], pb[bk][:, 0:kn_], r=['pb%d' % bk], w=['kN.%d.%d' % (c, k0)])
        for kt in range(NKT):
            bk = nb()
            P.add('pe', mm(pb[bk][:], ckvn[:, kt * 128:(kt + 1) * 128], wukv_b[:, 512:1024]), r=['wukv'] + ckv_names, w=['pb%d' % bk])
            evac(Vm[:, kt, :], pb[bk][:], r=['pb%d' % bk], w=['Vm.%d' % kt])
        kN_names = ['kN.%d.%d' % (c, k0) for c in range(4) for (k0, _) in kblocks]
        krT_names = ['krT.0', 'krT.1'] + (['krT.c'] if g == 1 else [])
        Vm_names = ['Vm.%d' % kt for kt in range(NKT)]
        dvt_names = ['dvt.%d' % kt for kt in range(NKT)]

        if m0_stop <= 3:
            return
        for t_ in range(2):
            tok = slice(t_ * 512, (t_ + 1) * 512)
            if g == 1:
                load_rope(t_)
            wQ, wQn = wload(w_in_ab[:, 0:256].rearrange("(kc p) n -> p kc n", p=128), [128, 8, 256])
            bg()
            b2 = nb()
            cqi = {}
            for c in range(2):
                bk = nb()
                fm_mm(bk, wQ, wQn, c * 128, 128, t_)
                cqi[c] = ntmp()
                P.add('dve', cp(tmpf[cqi[c]][:], pb[bk][:]), r=['pb%d' % bk], w=['tmpf%d' % cqi[c]])
                si = st['sqb']
                st['sqb'] ^= 1
                P.add('act', act(sqb[si][:], pb[bk][:], AF.Square), r=['pb%d' % bk], w=['sqb%d' % si])
                P.add('pe', mm(pb[b2][:], onesb, sqb[si][:], start=(c == 0), stop=(c == 1)), r=['sqb%d' % si, 'cstb'], w=['pb%d' % b2])
            ri = st['rstd']
            st['rstd'] ^= 1
            rn = 'rstd%d' % ri
            P.add('act', act(rstdb[ri][:], pb[b2][:], AF.Ln, scale=1.0 / 256, bias=EPS), r=['pb%d' % b2], w=[rn])
            P.add('act', act(rstdb[ri][:], rstdb[ri][:], AF.Exp, scale=-0.5), r=[rn], w=[rn])
            for c in range(2):
                P.add('dve', stt(cqn[:, c, :], tmpf[cqi[c]][:], vec[:, V_GQL + c:V_GQL + c + 1], rstdb[ri][:], ALU.mult, ALU.mult),
                      r=['tmpf%d' % cqi[c], rn, 'vec'], w=['cqn'])
            for c in range(4):
                bk = nb()
                for kc in range(2):
                    P.add('pe', mm(pb[bk][:], wuq_b[:, kc, c * 128:(c + 1) * 128], cqn[:, kc, :], start=(kc == 0), stop=(kc == 1)),
                          r=['wuq', 'cqn'], w=['pb%d' % bk])
                P.add('act', acp(qNz[0:64, 2 * c, :], pb[bk][0:64, :]), r=['pb%d' % bk, 'qNz'], w=['qN.%d' % c])
                P.add('dve', cp(qNz[64:128, 2 * c + 1, :], pb[bk][64:128, :]), r=['pb%d' % bk, 'qNz'], w=['qN.%d' % c])
            for h in range(8):
                bk = nb()
                for kc in range(2):
                    P.add('pe', mm(pb[bk][0:32, :], wuq_b[:, kc, 512 + h * 32:512 + (h + 1) * 32], cqn[:, kc, :], start=(kc == 0), stop=(kc == 1)),
                          r=['wuq', 'cqn'], w=['pb%d' % bk])
                if g == 0:
                    evac(qR[0:32, h, :], pb[bk][0:32, :], r=['pb%d' % bk, 'qRz'], w=['qR.%d' % h])
                else:
                    bks = nb()
                    for kc in range(2):
                        P.add('pe', mm(pb[bks][0:32, :], wuq_b[:, kc, 768 + h * 32:768 + (h + 1) * 32], cqn[:, kc, :], start=(kc == 0), stop=(kc == 1)),
                              r=['wuq', 'cqn'], w=['pb%d' % bks])
                    rope_evac(qR[0:32, h, :], 'qR.%d' % h, 32, bk, bks, rc32, rs32, tok, 'rc32', 'rs32', extra_r=['qRz'])
            for hh in range(2):
                wD, wDn = wload(w_in_ab[:, 416 + hh * 256:416 + (hh + 1) * 256].rearrange("(kc p) n -> p kc n", p=128), [128, 8, 256])
                bg()
                if g == 1:
                    wDs, wDsn = wload(w_in_sw[:, 32 + hh * 256:32 + (hh + 1) * 256].rearrange("(kc p) n -> p kc n", p=128), [128, 8, 256])
                for h2 in range(2):
                    h = hh * 2 + h2
                    bk = nb()
                    fm_mm(bk, wD, wDn, h2 * 128, 128, t_)
                    if g == 0:
                        evac(dqT[:, h, :], pb[bk][:], r=['pb%d' % bk], w=['dqT.%d' % h])
                    else:
                        bks = nb()
                        fm_mm(bks, wDs, wDsn, h2 * 128, 128, t_)
                        rope_evac(dqT[:, h, :], 'dqT.%d' % h, 128, bk, bks, rc128, rs128, tok, 'rc128', 'rs128')
            LOOK = 3
            NQ = 256 if g == 0 else 512
            for qb in range(512 // NQ):
                qc = slice(qb * NQ, (qb + 1) * NQ)
                if g == 0:
                    seq = t_ * 2 + qb
                    kts = [2 * seq, 2 * seq + 1]
                else:
                    kts = list(range(NKT))
                nk = len(kts)
                steps = []

                def esum_step(hd_, i_, nk_, pi):
                    if i_ == 0:
                        hd_['es'] = cnt['es']
                        cnt['es'] ^= 1
                    k_ = hd_['es']
                    eng_ = 'pool' if cnt['eseng'] % 3 == 0 else 'dve'
                    cnt['eseng'] += 1
                    if nk_ == 1:
                        P.add(eng_, cp(ESb[k_][:, 0:NQ], PT[pi][:, 0:NQ]), r=['PT%d' % pi], w=['ESb%d' % k_])
                    elif i_ == 0:
                        P.add(eng_, cp(ES[k_][:, 0:NQ], PT[pi][:, 0:NQ]), r=['PT%d' % pi], w=['ES%d' % k_])
                    elif i_ < nk_ - 1:
                        P.add(eng_, tt(ES[k_][:, 0:NQ], ES[k_][:, 0:NQ], PT[pi][:, 0:NQ], ALU.add), r=['PT%d' % pi, 'ES%d' % k_], w=['ES%d' % k_])
                    else:
                        P.add(eng_, tt(ESb[k_][:, 0:NQ], ES[k_][:, 0:NQ], PT[pi][:, 0:NQ], ALU.add), r=['PT%d' % pi, 'ES%d' % k_], w=['ESb%d' % k_])
                for h in range(8):
                    hd = {}
                    for i_, kt in enumerate(kts):
                        def s1(h=h, kt=kt, i_=i_, hd=hd):
                            c, e = h // 2, h % 2
                            pr = slice(e * 64, (e + 1) * 64)
                            ks = slice(kt * 128, (kt + 1) * 128)
                            if i_ == 0:
                                hd['ba'] = nab()
                                hd['bd'] = nab()
                            bs = nsb()
                            P.add('pe', mm(pb[bs][:, 0:NQ], kN[:, c, ks], qNz[:, h, qc], start=True, stop=False),
                                  r=kN_names + ['qN.%d' % c], w=['pb%d' % bs])
                            P.add('pe', mm(pb[bs][:, 0:NQ], krT[:, ks], qR[:, h, qc], start=False, stop=True),
                                  r=krT_names + ['qR.%d' % h], w=['pb%d' % bs])
                            return bs

                        def s2(bs, h=h, kt=kt, i_=i_, hd=hd):
                            c, e = h // 2, h % 2
                            pr = slice(e * 64, (e + 1) * 64)
                            ba, bd = hd['ba'], hd['bd']
                            pi = cnt['pt']
                            cnt['pt'] = (pi + 1) % len(PT)
                            P.add('act', act(PT[pi][:, 0:NQ], pb[bs][:, 0:NQ], AF.Exp, scale=SCALE_A), r=['pb%d' % bs], w=['PT%d' % pi])
                            P.add('pe', mm(pb[ba][0:64, 0:NQ], Vm[:, kt, h * 64:(h + 1) * 64], PT[pi][:, 0:NQ], start=(i_ == 0), stop=(i_ == nk - 1)),
                                  r=Vm_names + ['PT%d' % pi], w=['pb%d' % ba])
                            esum_step(hd, i_, nk, pi)
                            if i_ == nk - 1:
                                P.add('pe', mm(pb[bd][0:64, 0:NQ], onesb[:, 0:64], ESb[hd['es']][:, 0:NQ]), r=['cstb', 'ESb%d' % hd['es']], w=['pb%d' % bd])
                            if i_ == nk - 1:
                                ti = ntmp()
                                P.add('act', act(tmpf[ti][0:64, 0:NQ], pb[bd][0:64, 0:NQ], AF.Ln), r=['pb%d' % bd], w=['tmpf%d' % ti])
                                P.add('act', act(tmpf[ti][0:64, 0:NQ], tmpf[ti][0:64, 0:NQ], AF.Exp, scale=-1.0), r=['tmpf%d' % ti], w=['tmpf%d' % ti])
                                P.add('dve', tt(outA[pr, c, qc], pb[ba][0:64, 0:NQ], tmpf[ti][0:64, 0:NQ], ALU.mult),
                                      r=['pb%d' % ba, 'tmpf%d' % ti], w=['outA.%d' % c])
                        steps.append((s1, s2))
                pend = []
                for idx in range(len(steps) + LOOK):
                    if idx < len(steps):
                        pend.append(steps[idx][0]())
                    if idx >= LOOK:
                        steps[idx - LOOK][1](pend[idx - LOOK])
                steps = []
                LOOKD = 1
                for h in range(4):
                    hd = {}
                    for i_, kt in enumerate(kts):
                        def s1(h=h, kt=kt, i_=i_, hd=hd):
                            ks = slice(kt * 128, (kt + 1) * 128)
                            if i_ == 0:
                                hd[0] = (nab(), nab())
                                hd[1] = (nab(), nab())
                            bss = []
                            for sub in range(2):
                                pr = slice(sub * 64, (sub + 1) * 64)
                                bs = nsb()
                                bss.append(bs)
                                P.add('pe', mm(pb[bs][:, 0:NQ], dkT[pr, h, ks], dqT[pr, h, qc]),
                                      r=['dkT.%d.0' % h, 'dkT.%d.1' % h] + (['dkT.%d.c' % h] if g == 1 else []) + ['dqT.%d' % h], w=['pb%d' % bs])
                            return bss

                        def s2(bss, h=h, kt=kt, i_=i_, hd=hd):
                            pis = []
                            for sub in range(2):
                                pi = cnt['pt']
                                cnt['pt'] = (pi + 1) % len(PT)
                                pis.append(pi)
                                P.add('act', act(PT[pi][:, 0:NQ], pb[bss[sub]][:, 0:NQ], AF.Exp, scale=SCALE_B), r=['pb%d' % bss[sub]], w=['PT%d' % pi])
                            for sub in range(2):
                                bnum, bden = hd[sub]
                                pi = pis[sub]
                                P.add('pe', mm(pb[bnum][:, 0:NQ], dvt[:, kt, h * 128:(h + 1) * 128], PT[pi][:, 0:NQ], start=(i_ == 0), stop=(i_ == nk - 1)),
                                      r=dvt_names + ['PT%d' % pi], w=['pb%d' % bnum])
                                hs_ = hd.setdefault(('es', sub), {})
                                esum_step(hs_, i_, nk, pi)
                                if i_ == nk - 1:
                                    P.add('pe', mm(pb[bden][:, 0:NQ], onesb, ESb[hs_['es']][:, 0:NQ]), r=['cstb', 'ESb%d' % hs_['es']], w=['pb%d' % bden])
                            if i_ == nk - 1:
                                (n1, d1), (n2, d2) = hd[0], hd[1]
                                t1 = ntmp()
                                A1 = tmpf[t1][:, 0:NQ]
                                P.add('act', act(A1, pb[d1][:, 0:NQ], AF.Ln), r=['pb%d' % d1], w=['tmpf%d' % t1])
                                P.add('act', act(A1, A1, AF.Exp, scale=-1.0), r=['tmpf%d' % t1], w=['tmpf%d' % t1])
                                P.add('dve', tt(A1, pb[n1][:, 0:NQ], A1, ALU.mult), r=['pb%d' % n1, 'tmpf%d' % t1], w=['tmpf%d' % t1])
                                t2 = ntmp()
                                A2 = tmpf[t2][:, 0:NQ]
                                P.add('act', act(A2, pb[d2][:, 0:NQ], AF.Ln), r=['pb%d' % d2], w=['tmpf%d' % t2])
                                P.add('act', act(A2, A2, AF.Exp, scale=-1.0), r=['tmpf%d' % t2], w=['tmpf%d' % t2])
                                P.add('dve', tt(A2, pb[n2][:, 0:NQ], A2, ALU.mult), r=['pb%d' % n2, 'tmpf%d' % t2], w=['tmpf%d' % t2])
                                P.add('dve', stt(A1, A2, lamc[:, 1:2], A1, ALU.mult, ALU.add), r=['tmpf%d' % t1, 'tmpf%d' % t2, 'lamc'], w=['tmpf%d' % t1])
                                si = st['sqb']
                                st['sqb'] ^= 1
                                P.add('act', act(sqb[si][:, 0:NQ], A1, AF.Square), r=['tmpf%d' % t1], w=['sqb%d' % si])
                                bq = d1
                                P.add('pe', mm(pb[bq][:, 0:NQ], onesb, sqb[si][:, 0:NQ]), r=['sqb%d' % si, 'cstb'], w=['pb%d' % bq])
                                P.add('act', act(A2, pb[bq][:, 0:NQ], AF.Ln, scale=1.0 / 128, bias=EPS), r=['pb%d' % bq], w=['tmpf%d' % t2])
                                P.add('act', act(A2, A2, AF.Exp, scale=-0.5), r=['tmpf%d' % t2], w=['tmpf%d' % t2])
                                P.add('dve', stt(outB[:, h, qc], A1, gsubc[:, 0:1], A2, ALU.mult, ALU.mult),
                                      r=['tmpf%d' % t1, 'tmpf%d' % t2, 'gsubc'], w=['outB.%d' % h])
                        steps.append((s1, s2))
                LOOK_SAVE = LOOK
                LOOK = LOOKD
                pend = []
                for idx in range(len(steps) + LOOK):
                    if idx < len(steps):
                        pend.append(steps[idx][0]())
                    if idx >= LOOK:
                        steps[idx - LOOK][1](pend[idx - LOOK])
                LOOK = LOOK_SAVE
            if dbg and t_ == 1:
                dump('outA_%d' % g, outA, [128, 4, 512], ['outA.%d' % c for c in range(4)])
                dump('outB_%d' % g, outB, [128, 4, 512], ['outB.%d' % c for c in range(4)])
                dump('qN_%d' % g, qNz, [128, 8, 512], ['qN.%d' % c for c in range(4)])
                dump('qR_%d' % g, qR[0:32], [32, 8, 512], ['qR.%d' % c for c in range(8)])
                dump('dqT_%d' % g, dqT, [128, 4, 512], ['dqT.%d' % c for c in range(4)])
                dump('kN_%d' % g, kN, [128, 4, NK], kN_names)
                dump('krT_%d' % g, krT[0:32], [32, NK], krT_names)
                dump('ckvn_%d' % g, ckvn, [128, NK], ckv_names)
                dump('Vm_%d' % g, Vm, [128, NKT, 512], Vm_names)
                dump('dkT_%d' % g, dkT, [128, 4, NK], ['dkT.%d.%d' % (h, t) for h in range(4) for t in range(2)])
                dump('dvt_%d' % g, dvt, [128, NKT, 512], dvt_names)
                dump('cqn_%d' % g, cqn, [128, 2, 512], ['cqn'])
                dump('lamc', lamc[:], [128, 4], ['lamc'])
            for o2 in range(4):
                wO, wOn = wload(w_out_ab[:, o2 * 256:(o2 + 1) * 256].rearrange("(kc p) n -> p kc n", p=128), [128, 8, 256])
                bg()
                bg()
                for oo in range(2):
                    oc = o2 * 2 + oo
                    bk = nb()
                    for kc in range(8):
                        rhs = outA[:, kc, :] if kc < 4 else outB[:, kc - 4, :]
                        rn_ = 'outA.%d' % kc if kc < 4 else 'outB.%d' % (kc - 4)
                        P.add('pe', mm(pb[bk][:], wO[:, kc, oo * 128:(oo + 1) * 128], rhs, start=(kc == 0), stop=(kc == 7)),
                              r=[wOn, rn_], w=['pb%d' % bk])
                    P.add('dve', stt(xT[:, oc, tok], pb[bk][:], mcol(0, g, 2, oc), xT[:, oc, tok], ALU.mult, ALU.add),
                          r=['pb%d' % bk, 'modraw0', 'xT'], w=['xT'])
        while pending_ada:
            bg()


    LN8 = math.log(0.125)

    def mixer1(g):
        arena_reset()
        hm = carve([128, 8, NT], BF16)
        qk = carve([128, 8, 1024], BF16)
        vv = carve([128, 8, 1024], BF16)
        so = carve([128, 8, 1024], BF16)
        hf = carve([128, 8, 1024], BF16)
        gt = carve([128, 8, 32])
        nlf = carve([128, 8, 16])
        argA = carve([128, 8, 16])
        argK = carve([128, 16])
        fac = carve([128, 8, 48])
        eG = carve([128, 8, 16])
        eGs = carve([128, 8, 8])
        State = carve([128, 2, 4, 128])
        Nst = carve([128, 2, 4])
        Stb = carve([128, 2, 4, 128], BF16)
        Nstb = carve([128, 2, 4], BF16)
        qp = [carve([128, 512], BF16) for _ in range(2)]
        kp = [carve([128, 512], BF16) for _ in range(2)]
        kpp = [carve([128, 512], BF16) for _ in range(2)]
        qz = [carve([128, 8, 128], BF16) for _ in range(2)]
        kT = [carve([128, 4, 128], BF16) for _ in range(2)]
        ST = [carve([128, 8, 128], BF16) for _ in range(2)]
        dn = carve([128, 2, 8])
        hs32 = carve([128, 1024])
        ssq = carve([128, 16])
        hsn = carve([128, 1024], BF16)
        hsq = hsn
        if g == 0:
            amaxT = carve([16, 8])
            GtT = carve([16, 8])
            mw = carve([16, 12, 4])
            emfbc = carve([128, 16, 4])
            rhsD = carve([16, 16, 4])
            Cst = carve([128, 4, 128])
            nst_o = carve([128, 4])
        else:
            em0 = carve([128, 16])
        hsT = hm
        cnt = {'i': 0}

        norm_mod(xT, hm, 'hm', 1, g, 0)

        def tm_mm(bk, col0, wv, wn, n, j):
            for kc in range(8):
                P.add('pe', mm(pb[bk][:, col0:col0 + n], hm[:, kc, j * 128:(j + 1) * 128], wv[:, kc, 0:n], start=(kc == 0), stop=(kc == 7)),
                      r=[wn, 'hm.%d' % (j // 4)], w=['pb%d' % bk])

        for ld in range(12):
            c0 = ld * 256
            wv, wn = wload(w_in_c[:, c0:c0 + 256].rearrange("(kc p) n -> p kc n", p=128), [128, 8, 256])
            for j in range(8):
                bk = nb()
                tm_mm(bk, 0, wv, wn, 256, j)
                if c0 < 1024:
                    evac(qk[:, j, c0:c0 + 256], pb[bk][:, 0:256], r=['pb%d' % bk], w=['qk.%d.%d' % (j, ld)])
                elif c0 < 2048:
                    evac(vv[:, j, c0 - 1024:c0 - 1024 + 256], pb[bk][:, 0:256], r=['pb%d' % bk], w=['vv.%d.%d' % (j, ld)])
                else:
                    sov = so[:, j, c0 - 2048:c0 - 2048 + 256]
                    P.add('act', act(sov, pb[bk][:, 0:256], AF.Sigmoid), r=['pb%d' % bk], w=['so.%d.%d' % (j, ld)])
                    P.add('pool', tt(sov.rearrange("p (h v) -> p h v", v=128), sov.rearrange("p (h v) -> p h v", v=128),
                                     vec[:, V_GMLB:V_GMLB + 128].unsqueeze(1).to_broadcast([128, 2, 128]), ALU.mult),
                          r=['so.%d.%d' % (j, ld), 'vec'], w=['so.%d.%d' % (j, ld)])
        qk_names = lambda j: ['qk.%d.%d' % (j, ld) for ld in range(4)]
        vv_names = lambda j: ['vv.%d.%d' % (j, ld) for ld in range(4, 8)]
        so_names = lambda j: ['so.%d.%d' % (j, ld) for ld in range(8, 12)]
        if m1_stop <= 1:
            return
        wv, wn = wload(w_in_c[:, 3072:3104].rearrange("(kc p) n -> p kc n", p=128), [128, 8, 32])
        for j in range(8):
            bk = nb()
            tm_mm(bk, 0, wv, wn, 32, j)
            P.add('dve', tt(gt[:, j, :], pb[bk][:, 0:32], vec[:, V_BGB:V_BGB + 32], ALU.add), r=['pb%d' % bk, 'vec'], w=['gt.%d' % j])
            g4 = gt[:, j, :].rearrange("p (a b) -> p a b", b=8)
            nl = nlf[:, j, :].rearrange("p (a b) -> p a b", b=8)
            P.add('act', act(nl, g4[:, 1::2, :], AF.Exp, scale=-1.0), r=['gt.%d' % j], w=['nlf.%d' % j])
            P.add('act', act(nl, nl, AF.Ln, bias=1.0), r=['nlf.%d' % j], w=['nlf.%d' % j])
            bc = nb()
            for k_, (co, d_) in enumerate(((C_TUI, 0), (C_TLS, 0), (C_TLI, 1), (C_TUS, 1))):
                P.add('pe', mm(pb[bc][:, k_ * 8:(k_ + 1) * 8], cst[:, co:co + 128], nlf[:, j, d_ * 8:(d_ + 1) * 8]), r=['cst', 'nlf.%d' % j], w=['pb%d' % bc])
            P.add('pe', mm(pb[bc][:, 32:48], onesf, nlf[:, j, :]), r=['cst', 'nlf.%d' % j], w=['pb%d' % bc])
            p4 = pb[bc][:, 0:32].rearrange("p (a b) -> p a b", b=8)
            f6 = fac[:, j, :].rearrange("p (a b) -> p a b", b=8)
            aK = argK[:, :].rearrange("p (a b) -> p a b", b=8)
            aA = argA[:, j, :].rearrange("p (a b) -> p a b", b=8)
            P.add('dve', tt(aK, g4[:, 0::2, :], p4[:, 0::2, :], ALU.add), r=['gt.%d' % j, 'pb%d' % bc], w=['argK'])
            P.add('dve', tt(aA, g4[:, 0::2, :], p4[:, 1::2, :], ALU.subtract), r=['gt.%d' % j, 'pb%d' % bc], w=['argA.%d' % j])
            P.add('act', act(f6[:, 0::3, :], p4[:, 0::2, :], AF.Exp, scale=-1.0), r=['pb%d' % bc], w=['fac.%d' % j])
            P.add('act', act(eG[:, j, :], pb[bc][:, 32:48], AF.Exp, scale=-1.0), r=['pb%d' % bc], w=['eG.%d' % j])
            P.add('act', act(f6[:, 1::3, :], aK, AF.Exp, bias=LN8), r=['argK'], w=['fac.%d' % j])
            P.add('act', act(f6[:, 2::3, :], aA, AF.Exp, bias=LN8), r=['argA.%d' % j], w=['fac.%d' % j])
            eg4 = eG[:, j, :].rearrange("p (d c e) -> p d c e", c=4, e=2)
            es = eGs[:, j, :].rearrange("p (d c) -> p d c", c=4)
            for e_ in range(2):
                pe_ = slice(e_ * 64, (e_ + 1) * 64)
                P.add('dve', cp(es[pe_], eg4[pe_, :, :, e_]), r=['eG.%d' % j], w=['eGs.%d' % j])

        if m1_stop <= 2:
            return
        if g == 0:
            for half in range(2):
                bk = nb()
                for jj in range(4):
                    j = half * 4 + jj
                    P.add('pe', tr(pb[bk][0:16, jj * 128:(jj + 1) * 128], argA[:, j, :], idf), r=['argA.%d' % j, 'cst'], w=['pb%d' % bk])
                P.add('dve', lambda e, bk=bk, half=half: e.tensor_reduce(out=amaxT[0:16, half * 4:(half + 1) * 4],
                                                                        in_=pb[bk][0:16, :].rearrange("p (a b) -> p a b", b=128), axis=AX.X, op=ALU.max),
                      r=['pb%d' % bk], w=['amaxT'])
            bk = nb()
            for j in range(8):
                P.add('pe', mm(pb[bk][0:16, j:j + 1], nlf[:, j, :], onesf[:, 0:1]), r=['nlf.%d' % j, 'cst'], w=['pb%d' % bk])
            P.add('dve', cp(GtT[0:16, :], pb[bk][0:16, 0:8]), r=['pb%d' % bk], w=['GtT'])
            am = amaxT[0:16, :].rearrange("p (s t) -> p s t", t=2)
            gm = GtT[0:16, :].rearrange("p (s t) -> p s t", t=2)
            a0, a1, G0, G1 = am[:, :, 0], am[:, :, 1], gm[:, :, 0], gm[:, :, 1]
            W = lambda i: mw[0:16, i, :]
            mops = [
                ts(W(0), G0, -1.0, ALU.mult), tt(W(1), a0, W(0), ALU.max), tt(W(2), W(1), G1, ALU.subtract), tt(W(3), a1, W(2), ALU.max),
                ts(W(4), G1, -1.0, ALU.mult), tt(W(5), a1, W(4), ALU.max), tt(W(6), W(5), G0, ALU.subtract), tt(W(7), a0, W(6), ALU.max),
                ts(W(8), W(3), cst[0:16, C_SELF:C_SELF + 1], ALU.mult),
                stt(W(9), W(7), cst[0:16, C_SELB:C_SELB + 1], W(8), ALU.mult, ALU.add),
            ]
            for f_ in mops:
                P.add('dve', f_, r=['amaxT', 'GtT', 'mw', 'cst'], w=['mw'])
            P.add('act', act(W(10), W(9), AF.Exp, scale=-1.0), r=['mw'], w=['mw'])
            P.add('sp', dma(o_m.rearrange("s d h -> (d h) s"), W(9), slow=True), r=['mw'], chan='out_m')
            P.add('dve', tt(rhsD, cst[0:16, C_ID:C_ID + 16].unsqueeze(2).to_broadcast([16, 16, 4]),
                            W(10).unsqueeze(1).to_broadcast([16, 16, 4]), ALU.mult), r=['mw', 'cst'], w=['rhsD'])
            bk = nb()
            P.add('pe', mm(pb[bk][:, 0:64], onesf[0:16, :], rhsD.rearrange("p a b -> p (a b)")), r=['rhsD', 'cst'], w=['pb%d' % bk])
            P.add('dve', cp(emfbc.rearrange("p a b -> p (a b)"), pb[bk][:, 0:64]), r=['pb%d' % bk], w=['emfbc'])
        else:
            P.add('act', act(em0, vec[:, V_M0:V_M0 + 16], AF.Exp), r=['vec'], w=['em0'])

        if m1_stop <= 3:
            return
        seqs = [[2 * s_, 2 * s_ + 1] for s_ in range(4)] if g == 0 else [list(range(8))]
        mask4 = carve([128, 2, 4, 128], BF16)
        for i__ in range(2):
            P.add('pool', mset(qz[i__], 0.0), w=['qzz'])
        for d_ in range(2):
            co = C_TUI if d_ == 0 else C_TLI
            for hh in range(4):
                P.add('pool', cp(mask4[:, d_, hh, :], cst[:, co:co + 128]), r=['cst'], w=['mask4'])
        for si_, tiles in enumerate(seqs):
            ntl = len(tiles)
            for d in range(2):
                sn_ = 'State.%d' % d
                if g == 0:
                    P.add('pool', mset(State[:, d], 0.0), w=[sn_])
                    P.add('pool', mset(Nst[:, d, :], 0.0), w=['Nst.%d' % d])
                else:
                    P.add('sp', dma(State[:, d], c0_d[:, d]), w=[sn_], chan='ld_c0_%d' % d)
                    e4 = em0[:, :].rearrange("p (d c e) -> p d c e", c=4, e=2)
                    for e_ in range(2):
                        pe_ = slice(e_ * 64, (e_ + 1) * 64)
                        P.add('dve', tt(State[pe_, d], State[pe_, d], e4[pe_, d, :, e_].unsqueeze(2).to_broadcast([64, 4, 128]), ALU.mult),
                              r=[sn_, 'em0'], w=[sn_])
                        P.add('dve', tt(Nst[pe_, d, :], vec[pe_, V_N0 + d * 4:V_N0 + d * 4 + 4], e4[pe_, d, :, e_], ALU.mult),
                              r=['vec', 'em0'], w=['Nst.%d' % d])
                P.add('act', acp(Stb[:, d], State[:, d]), r=[sn_], w=['Stb.%d' % d])
                P.add('act', acp(Nstb[:, d, :], Nst[:, d, :]), r=['Nst.%d' % d], w=['Nstb.%d' % d])
            def tile_step(d, j, k_, ntl=ntl, si_=si_):
                        i_ = d
                        store_h = (k_ < ntl // 2)
                        sn_ = 'State.%d' % d
                        f6 = fac[:, j, :].rearrange("p (a b) -> p a b", b=8)
                        q3 = qk[:, j, 0:512].rearrange("p (h k) -> p h k", k=64)
                        k3 = qk[:, j, 512:1024].rearrange("p (h k) -> p h k", k=64)
                        bc3 = lambda a: f6[:, a, :].unsqueeze(2).to_broadcast([128, 8, 64])
                        P.add('dve', tt(qp[i_].rearrange("p (h k) -> p h k", k=64), q3, bc3(3 * d), ALU.mult), r=qk_names(j) + ['fac.%d' % j], w=['qp%d' % i_])
                        P.add('pool', tt(kp[i_].rearrange("p (h k) -> p h k", k=64), k3, bc3(3 * d + 1), ALU.mult), r=qk_names(j) + ['fac.%d' % j], w=['kp%d' % i_])
                        P.add('dve', tt(kpp[i_].rearrange("p (h k) -> p h k", k=64), k3, bc3(3 * d + 2), ALU.mult), r=qk_names(j) + ['fac.%d' % j], w=['kpp%d' % i_])
                        bt = nb()
                        for c in range(4):
                            P.add('pe', tr(pbb[bt][:, c * 128:(c + 1) * 128], qp[i_][:, c * 128:(c + 1) * 128], idb), r=['qp%d' % i_, 'cstb'], w=['pb%d' % bt])
                        for c in range(4):
                            P.add('pe', tr(pbb[bt][:, (4 + c) * 128:(5 + c) * 128], kp[i_][:, c * 128:(c + 1) * 128], idb), r=['kp%d' % i_, 'cstb'], w=['pb%d' % bt])
                        pt3 = pbb[bt][:, :].rearrange("p (a b) -> p a b", b=128)
                        P.add('act', acp(qz[i_][0:64, 0::2, :], pt3[0:64, 0:4, :]), r=['pb%d' % bt, 'qzz'], w=['qkT%d' % i_])
                        P.add('act', acp(qz[i_][64:128, 1::2, :], pt3[64:128, 0:4, :]), r=['pb%d' % bt, 'qzz'], w=['qkT%d' % i_])
                        P.add('act', acp(kT[i_][:, :, :], pt3[:, 4:8, :]), r=['pb%d' % bt], w=['qkT%d' % i_])
                        yield
                        if m1_stop <= 4:
                            return
                        bsp = [nb(), nb()]
                        for h in range(8):
                            c, e_ = h // 2, h % 2
                            pe_ = slice(e_ * 64, (e_ + 1) * 64)
                            P.add('pe', mm(pb[bsp[e_]][:, c * 128:(c + 1) * 128], kT[i_][:, c, :], qz[i_][:, h, :]), r=['qkT%d' % i_], w=['pb%d' % bsp[e_]])
                        for e_ in range(2):
                            P.add('dve', tt(ST[i_][:, e_::2, :], pb[bsp[e_]][:].rearrange("p (h l) -> p h l", l=128),
                                            mask4[:, d], ALU.mult), r=['pb%d' % bsp[e_], 'mask4'], w=['ST%d.%d' % (i_, e_)])
                        yield
                        if m1_stop <= 5:
                            return
                        bd = nb()
                        bnum = []
                        for h4 in range(2):
                            bn_ = nb()
                            bnum.append(bn_)
                            for hh in range(4):
                                h = h4 * 4 + hh
                                c, e_ = h // 2, h % 2
                                pe_ = slice(e_ * 64, (e_ + 1) * 64)
                                P.add('pe', mm(pb[bn_][:, hh * 128:(hh + 1) * 128], ST[i_][:, h, :], vv[:, j, h * 128:(h + 1) * 128], start=True, stop=False),
                                      r=['ST%d.%d' % (i_, h % 2)] + vv_names(j), w=['pb%d' % bn_])
                                P.add('pe', mm(pb[bn_][:, hh * 128:(hh + 1) * 128], qz[i_][:, h, :], Stb[:, d, c, :], start=False, stop=True),
                                      r=['qkT%d' % i_, 'Stb.%d' % d], w=['pb%d' % bn_])
                        for h in range(8):
                            c, e_ = h // 2, h % 2
                            pe_ = slice(e_ * 64, (e_ + 1) * 64)
                            P.add('pe', mm(pb[bd][:, h:h + 1], ST[i_][:, h, :], onesb[:, 0:1], start=True, stop=False),
                                  r=['ST%d.%d' % (i_, h % 2), 'cstb'], w=['pb%d' % bd])
                            P.add('pe', mm(pb[bd][:, h:h + 1], qz[i_][:, h, :], Nstb[:, d, c:c + 1], start=False, stop=True),
                                  r=['qkT%d' % i_, 'Nstb.%d' % d], w=['pb%d' % bd])
                        yield
                        P.add('act', act(dn[:, i_, :], pb[bd][:, 0:8], AF.Abs), r=['pb%d' % bd], w=['dn%d' % i_])
                        P.add('dve', ts(dn[:, i_, :], dn[:, i_, :], 1.0, ALU.max), r=['dn%d' % i_], w=['dn%d' % i_])
                        P.add('dve', rcp(dn[:, i_, :], dn[:, i_, :]), r=['dn%d' % i_], w=['dn%d' % i_])
                        for h4 in range(2):
                            num3 = pb[bnum[h4]][:].rearrange("p (h v) -> p h v", v=128)
                            rd = dn[:, i_, h4 * 4:(h4 + 1) * 4].unsqueeze(2).to_broadcast([128, 4, 128])
                            if store_h:
                                P.add('dve', tt(hf[:, j, h4 * 512:(h4 + 1) * 512].rearrange("p (h v) -> p h v", v=128), num3, rd, ALU.mult),
                                      r=['pb%d' % bnum[h4], 'dn%d' % i_], w=['hf.%d.%d' % (j, h4)])
                            else:
                                P.add('dve', tt(hs32[:, h4 * 512:(h4 + 1) * 512].rearrange("p (h v) -> p h v", v=128), num3, rd, ALU.mult),
                                      r=['pb%d' % bnum[h4], 'dn%d' % i_], w=['hs32.%d' % h4])
                                P.add('pool' if h4 == 0 else 'dve', tt(hs32[:, h4 * 512:(h4 + 1) * 512], hs32[:, h4 * 512:(h4 + 1) * 512], hf[:, j, h4 * 512:(h4 + 1) * 512], ALU.add),
                                      r=['hs32.%d' % h4, 'hf.%d.%d' % (j, h4)], w=['hs32.%d' % h4])
                        if m1_stop <= 6:
                            return
                        bkv = []
                        for b_ in range(2):
                            bk = nb()
                            bkv.append(bk)
                            for cc in range(2):
                                c = b_ * 2 + cc
                                P.add('pe', mm(pb[bk][:, cc * 256:(cc + 1) * 256], kpp[i_][:, c * 128:(c + 1) * 128], vv[:, j, c * 256:(c + 1) * 256]),
                                      r=['kpp%d' % i_] + vv_names(j), w=['pb%d' % bk])
                        bkn = nb()
                        for c in range(4):
                            P.add('pe', mm(pb[bkn][:, c:c + 1], kpp[i_][:, c * 128:(c + 1) * 128], onesb[:, 0:1]), r=['kpp%d' % i_, 'cstb'], w=['pb%d' % bkn])
                        es = eGs[:, j, :].rearrange("p (d c) -> p d c", c=4)
                        P.add('dve', tt(State[:, d], State[:, d], es[:, d, :].unsqueeze(2).to_broadcast([128, 4, 128]), ALU.mult),
                              r=[sn_, 'eGs.%d' % j, 'Stb.%d' % d], w=[sn_])
                        for b_ in range(2):
                            for e_ in range(2):
                                pe_ = slice(e_ * 64, (e_ + 1) * 64)
                                kvv = pb[bkv[b_]][pe_, :].rearrange("p (c e v) -> p c e v", e=2, v=128)[:, :, e_, :]
                                P.add('dve', tt(State[pe_, d, b_ * 2:b_ * 2 + 2, :], State[pe_, d, b_ * 2:b_ * 2 + 2, :], kvv, ALU.add),
                                      r=[sn_, 'pb%d' % bkv[b_]], w=[sn_])
                        P.add('pool', tt(Nst[:, d, :], Nst[:, d, :], es[:, d, :], ALU.mult), r=['Nst.%d' % d, 'eGs.%d' % j], w=['Nst.%d' % d])
                        P.add('dve', tt(Nst[:, d, :], Nst[:, d, :], pb[bkn][:, 0:4], ALU.add), r=['Nst.%d' % d, 'pb%d' % bkn], w=['Nst.%d' % d])
                        P.add('act', acp(Stb[:, d], State[:, d]), r=[sn_], w=['Stb.%d' % d])
                        P.add('act', acp(Nstb[:, d, :], Nst[:, d, :]), r=['Nst.%d' % d], w=['Nstb.%d' % d])
                        if m1_stop <= 7:
                            return
                        if not store_h:
                            P.add('act', act(hsq[:], hs32[:], AF.Square), r=['hs32.0', 'hs32.1'], w=['hsn'] + ['hsn.%d' % h_ for h_ in range(8)])
                            P.add('dve', lambda e: e.tensor_reduce(out=ssq[:, 0:8], in_=hsq[:].rearrange("p (h v) -> p h v", v=128), axis=AX.X, op=ALU.add),
                                  r=['hsn'], w=['ssq'])
                            P.add('act', act(ssq[:, 8:16], ssq[:, 0:8], AF.Ln, scale=1.0 / 128, bias=EPS), r=['ssq'], w=['ssq'])
                            P.add('act', act(ssq[:, 8:16], ssq[:, 8:16], AF.Exp, scale=-0.5), r=['ssq'], w=['ssq'])
                            h3 = hs32[:].rearrange("p (h v) -> p h v", v=128)
                            for h_ in range(8):
                                P.add('dve',
                                      stt(hsn[:, h_ * 128:(h_ + 1) * 128], hs32[:, h_ * 128:(h_ + 1) * 128], ssq[:, 8 + h_:9 + h_],
                                          so[:, j, h_ * 128:(h_ + 1) * 128], ALU.mult, ALU.mult),
                                      r=['hs32.0', 'hs32.1', 'ssq'] + so_names(j), w=['hsn.%d' % h_, 'hsnr.%d' % h_])
                            bt = nb()
                            for c in range(8):
                                P.add('pe', tr(pbb[bt][:, c * 128:(c + 1) * 128], hsn[:, c * 128:(c + 1) * 128], idb), r=['hsn.%d' % c, 'cstb'], w=['pb%d' % bt])
                            P.add('act', acp(hsT[:, :, j * 128:(j + 1) * 128], pbb[bt][:, :].rearrange("p (c t) -> p c t", t=128)),
                                  r=['pb%d' % bt], w=['hm.%d' % (j // 4)])
            for k_ in range(ntl):
                gens = [tile_step(0, tiles[k_], k_), tile_step(1, tiles[ntl - 1 - k_], k_)]
                while gens:
                    for gn in list(gens):
                        try:
                            next(gn)
                        except StopIteration:
                            gens.remove(gn)
            if g == 0 and m1_stop > 8:
                for d in range(2):
                    for e_ in range(2):
                        pe_ = slice(e_ * 64, (e_ + 1) * 64)
                        emsel = emfbc[pe_, d * 8 + e_:d * 8 + 8:2, si_]
                        P.add('dve', tt(Cst[pe_], State[pe_, d], emsel.unsqueeze(2).to_broadcast([64, 4, 128]), ALU.mult),
                              r=['State.%d' % d, 'emfbc'], w=['Cst'])
                        P.add('dve', tt(nst_o[pe_, :], Nst[pe_, d, :], emsel, ALU.mult), r=['Nst.%d' % d, 'emfbc'], w=['nst_o'])
                        P.add('sp', dma(o_C[si_, d].rearrange("(c e) k v -> e k c v", e=2)[e_], Cst[pe_]), r=['Cst'], chan='out_C')
                        P.add('sp', dma(o_n[si_, d].rearrange("(c e) k -> e k c", e=2)[e_], nst_o[pe_, :], slow=True), r=['nst_o'], chan='out_n')

        for o2 in range(4):
            wO, wOn = wload(w_out_c[:, o2 * 256:(o2 + 1) * 256].rearrange("(kc p) n -> p kc n", p=128), [128, 8, 256])
            for oo in range(2):
                oc = o2 * 2 + oo
                for t_ in range(2):
                    tok = slice(t_ * 512, (t_ + 1) * 512)
                    bk = nb()
                    for kc in range(8):
                        P.add('pe', mm(pb[bk][:], wO[:, kc, oo * 128:(oo + 1) * 128], hsT[:, kc, tok], start=(kc == 0), stop=(kc == 7)),
                              r=[wOn, 'hm.%d' % t_], w=['pb%d' % bk])
                    P.add('dve', stt(xT[:, oc, tok], pb[bk][:], mcol(1, g, 2, oc), xT[:, oc, tok], ALU.mult, ALU.add),
                          r=['pb%d' % bk, 'modraw1', 'xT'], w=['xT'])

    def load_x(g, reset=True):
        if reset:
            arena_reset()
        xin = [carve([128, D]) for _ in range(2)]
        for j in range(8):
            s_ = j % 2
            P.add('sp', dma(xin[s_], xg[g][j * 128:(j + 1) * 128, :]), w=['xin%d' % s_], chan='xin%d' % s_)
            for half in range(2):
                bk = nb()
                for c4 in range(4):
                    c = half * 4 + c4
                    P.add('pe', tr(pb[bk][:, c4 * 128:(c4 + 1) * 128], xin[s_][:, c * 128:(c + 1) * 128], idf),
                          r=['xin%d' % s_, 'cst'], w=['pb%d' % bk])
                evac(xT[:, half * 4:(half + 1) * 4, j * 128:(j + 1) * 128], pb[bk][:].rearrange("p (c t) -> p c t", c=4),
                     r=['pb%d' % bk], w=['xT'])

    def store_y(g):
        arena_reset()
        yT = carve([128, 8, NT])
        yo = [carve([128, D]) for _ in range(2)]
        norm_mod(xT, yT, 'yT', 0, g, 0, final=True)
        for j in range(8):
            s_ = j % 2
            for half in range(2):
                bk = nb()
                for c4 in range(4):
                    c = half * 4 + c4
                    P.add('pe', tr(pb[bk][:, c4 * 128:(c4 + 1) * 128], yT[:, c, j * 128:(j + 1) * 128], idf),
                          r=['yT.%d' % (j // 4), 'cst'], w=['pb%d' % bk])
                evac(yo[s_][:, half * 512:(half + 1) * 512], pb[bk][:], r=['pb%d' % bk], w=['yo%d' % s_])
            P.add('pool', dma(y_d[g][j * 128:(j + 1) * 128, :], yo[s_]), r=['yo%d' % s_], chan='out_y%d' % s_)

    for gi_, g in enumerate(groups):
        if gi_ == 0:
            load_x(g)
        if g == groups[0]:
            ada_slots0 = [carve([128, 8, 256], BF16) for _ in range(2)]
            for f_ in adaln_closures(0):
                f_(ada_slots0)
            pending_ada.extend(adaln_closures(1))
            if 'm0' not in parts:
                while pending_ada:
                    pending_ada.pop(0)(ada_slots0)
        if 'm0' in parts:
            mixer0(g)
            if dbg:
                dump('xm0_%d' % g, xT[:], [128, 8, NT], ['xT'])
        if 'f0' in parts:
            ffn(0, g)
        if 'm1' in parts:
            mixer1(g)
            if dbg:
                dump('xm1_%d' % g, xT[:], [128, 8, NT], ['xT'])
        if 'f1' in parts:
            ffn(1, g)
        store_y(g)
        if gi_ + 1 < len(groups):
            load_x(groups[gi_ + 1], reset=False)

    stats = P.emit()
    return nc, stats, P


def _prep_inputs(inp, ncores=8):
    f = np.float32
    cst = _consts()
    c32, s32, c128, s128 = _rope_tables()
    rope = np.zeros((4, 128, 1024), f)
    rope[0, 0:32] = c32
    rope[1, 0:32] = s32
    rope[2] = c128
    rope[3] = s128
    w_in_ab = np.ascontiguousarray(inp['w_in_ab'][0])
    perm_kr = 384 + _swap_perm(32, 8)
    perm_dq = 416 + _swap_perm(512, 16)
    perm_dk = 928 + _swap_perm(512, 16)
    w_in_sw = np.ascontiguousarray(w_in_ab[:, np.concatenate([perm_kr, perm_dq, perm_dk])])
    wuq = inp['w_uq'][0].reshape(256, 8, 96)
    nope = wuq[:, :, 0:64].reshape(256, 512)
    rp = wuq[:, :, 64:96].reshape(256, 256)
    rp_sw = rp[:, _swap_perm(256, 8)]
    w_uq = np.ascontiguousarray(np.concatenate([nope, rp, rp_sw], 1))
    wukv = inp['w_ukv'][0].reshape(128, 8, 128)
    w_ukv = np.ascontiguousarray(np.concatenate([wukv[:, :, 0:64].reshape(128, 512), wukv[:, :, 64:128].reshape(128, 512)], 1))
    shared = dict(
        cst=cst, rope=rope,
        w_ada=np.ascontiguousarray(inp['w_ada']), w_ffn_in=np.ascontiguousarray(inp['w_ffn_in']),
        w_ffn_out=np.ascontiguousarray(inp['w_ffn_out']), w_in_ab=w_in_ab, w_in_sw=w_in_sw, w_uq=w_uq, w_ukv=w_ukv,
        w_out_ab=np.ascontiguousarray(inp['w_out_ab'][0]), w_in_c=np.ascontiguousarray(inp['w_in_c'][0]),
        w_out_c=np.ascontiguousarray(inp['w_out_c'][0]))

    def fm(v):
        return np.ascontiguousarray(v.reshape(-1, 128).T)

    maps = []
    for i in range(ncores):
        vec = np.zeros((128, NVEC), f)
        vec[:, V_COND:V_COND + 8] = fm(inp['c_ctx'])
        vec[:, V_COND + 8:V_COND + 16] = fm(inp['c'][i])
        for l in range(2):
            vec[:, V_BADA + l * 48:V_BADA + (l + 1) * 48] = fm(inp['b_ada'][l])
        gl = [inp['g_mix'][0], inp['g_ffn'][0], inp['g_mix'][1], inp['g_ffn'][1], inp['g_final']]
        for k, gv in enumerate(gl):
            vec[:, V_G + k * 8:V_G + (k + 1) * 8] = fm(gv)
        vec[:, V_GQL:V_GQL + 2] = fm(inp['g_q_lora'][0])
        vec[:, V_GKV:V_GKV + 1] = fm(inp['g_kv_lora'][0])
        vec[:, V_GSUB:V_GSUB + 1] = fm(inp['g_diff_subln'][0])
        vec[:, V_GKVB:V_GKVB + 128] = inp['g_kv_lora'][0][None, :]
        vec[:, V_GMLB:V_GMLB + 128] = inp['g_mlstm'][0][None, :]
        vec[:, V_BGB:V_BGB + 32] = inp['b_gate_c'][0][None, :]
        vec[:, V_LAMB:V_LAMB + 256] = inp['diff_lambda'][0].reshape(1, 256)
        n0 = inp['state_mlstm_n'][i, 0]
        vec[:, V_N0:V_N0 + 8] = n0.reshape(2, 4, 2, 64).transpose(2, 3, 0, 1).reshape(128, 8)
        vec[:, V_M0:V_M0 + 16] = inp['state_mlstm_m'][i, 0].reshape(1, 16)
        c0 = inp['state_mlstm_C'][i, 0]
        c0 = np.ascontiguousarray(c0.reshape(2, 4, 2, 64, 128).transpose(2, 3, 0, 1, 4).reshape(128, 2, 4, 128))
        m = dict(shared)
        m.update(
            xg=np.ascontiguousarray(np.stack([inp['x_prompt'][4 * i:4 * i + 4].reshape(NT, D), inp['x_sample'][i]])),
            vec=vec, c0=c0,
            ckv_c=np.ascontiguousarray(inp['cache_mla_ckv'][i, 0]), kr_c=np.ascontiguousarray(inp['cache_mla_krope'][i, 0]),
            dk_c=np.ascontiguousarray(inp['cache_diff_k'][i, 0]), dv_c=np.ascontiguousarray(inp['cache_diff_v'][i, 0]))
        maps.append(m)
    return maps


_CACHE = {}


def kernel(**inputs):
    inp = {k: np.asarray(v, dtype=np.float32) for k, v in inputs.items()}
    if 'nc' not in _CACHE:
        _CACHE['nc'] = build()[0]
    nc = _CACHE['nc']
    maps = _prep_inputs(inp, 8)
    res = run_bass_kernel_spmd(nc, maps, core_ids=list(range(8)))
    R = res.results
    y_prompt = np.concatenate([r['y'][0].reshape(4, 256, D) for r in R], 0)
    y_sample = np.stack([r['y'][1] for r in R], 0)
    new_ckv = np.concatenate([r['o_ckv'].reshape(4, 1, 256, 128) for r in R], 0)
    new_kr = np.concatenate([r['o_kr'].reshape(4, 1, 256, 32) for r in R], 0)
    new_dk = np.concatenate([r['o_dk'].reshape(4, 1, 4, 256, 128) for r in R], 0)
    new_dv = np.concatenate([r['o_dv'].reshape(4, 1, 4, 256, 128) for r in R], 0)
    new_C = np.concatenate([r['o_C'].reshape(4, 1, 2, 8, 64, 128) for r in R], 0)
    new_n = np.concatenate([r['o_n'].reshape(4, 1, 2, 8, 64) for r in R], 0)
    new_m = np.concatenate([r['o_m'].reshape(4, 1, 2, 8) for r in R], 0)
    return (y_prompt, y_sample, new_ckv, new_kr, new_dk, new_dv, new_C, new_n, new_m)
```

```python
import math
import numpy as np
import concourse.bass as bass
import concourse.mybir as mybir
from concourse.bass_utils import run_bass_kernel_spmd

F32 = mybir.dt.float32
BF16 = mybir.dt.bfloat16
ALU = mybir.AluOpType
AF = mybir.ActivationFunctionType
AX = mybir.AxisListType

ENGS = ['pe', 'act', 'dve', 'pool', 'sp']
D = 1024
NT = 1024
DFF = 2816
EPS = 1e-6
LAM_INIT0 = 0.8 - 0.6 * math.exp(-0.3 * 0)


class Op:
    __slots__ = ('eng', 'fn', 'deps', 'needed', 'sig', 'chan', 'key', 'seq')
    _n = [0]

    def __init__(self, eng, fn, chan):
        self.eng = eng
        self.fn = fn
        self.chan = chan
        self.deps = set()
        self.needed = chan is not None
        self.sig = 0
        self.key = None
        Op._n[0] += 1
        self.seq = Op._n[0]


class Prog:
    def __init__(self, nc):
        self.nc = nc
        self.ops = {e: [] for e in ENGS}
        self.res_w = {}
        self.res_r = {}

    def add(self, eng, fn, r=(), w=(), chan=None):
        op = Op(eng, fn, chan)
        deps = set()
        if eng != 'pe':
            w = list(w) + [n for n in r if n.startswith('pb') and n not in w]
        for n in r:
            o = self.res_w.get(n)
            if o is not None:
                deps.add(o)
        for n in w:
            o = self.res_w.get(n)
            if o is not None:
                deps.add(o)
            for o in self.res_r.get(n, {}).values():
                deps.add(o)
        for n in r:
            self.res_r.setdefault(n, {})[(eng, chan)] = op
        for n in w:
            self.res_w[n] = op
            self.res_r[n] = {}
        deps.discard(op)
        if eng == 'pe':
            deps = {d for d in deps if not (d.eng == 'pe' and d.chan is None)}
        op.deps = deps
        self.ops[eng].append(op)
        return op

    def barrier(self):
        lasts = []
        for e in ENGS:
            for o in reversed(self.ops[e]):
                if o.fn is not None and o.chan is None:
                    lasts.append(o)
                    break
        chans = {}
        for e in ENGS:
            for o in self.ops[e]:
                if o.chan is not None:
                    chans[o.chan] = o
        lasts += list(chans.values())
        for e in ENGS:
            op = Op(e, None, None)
            op.deps = {d for d in lasts if not (e == 'pe' and d.eng == 'pe' and d.chan is None)}
            self.ops[e].append(op)

    def emit(self, final_wait_prefix='out'):
        nc = self.nc
        chans = {}
        for e in ENGS:
            for o in self.ops[e]:
                if o.chan is not None and o.chan.startswith(final_wait_prefix):
                    chans[o.chan] = o
        fin = Op('sp', None, None)
        fin.deps = set(chans.values())
        self.ops['sp'].append(fin)
        for e in ENGS:
            for op in self.ops[e]:
                for d in op.deps:
                    d.needed = True
        counters = {}
        for e in ENGS:
            for op in self.ops[e]:
                if op.needed:
                    key = ('ch', op.chan) if op.chan else ('eng', e)
                    counters[key] = counters.get(key, 0) + (16 if op.chan else 1)
                    op.sig = counters[key]
                    op.key = key
        sems = {}
        for i, key in enumerate(counters):
            sems[key] = nc.alloc_semaphore("s%d" % i)
        self.counters = counters
        engobj = {'pe': 'tensor', 'act': 'scalar', 'dve': 'vector', 'pool': 'gpsimd', 'sp': 'sync'}
        stats = {}

        chan_ops = {}
        for e in ENGS:
            for op in self.ops[e]:
                if op.chan is not None:
                    chan_ops.setdefault(op.key, []).append((op.seq, op.sig))
        for k in chan_ops:
            chan_ops[k].sort()

        allops = sorted((op for e in ENGS for op in self.ops[e]), key=lambda o: o.seq)
        eng_known = {e: {} for e in ENGS}
        known_after = {}
        wait_lists = {}
        for op in allops:
            k = eng_known[op.eng]
            needs = []
            for d in op.deps:
                v = d.sig
                if d.chan is not None:
                    for (sq, sg) in chan_ops[d.key]:
                        if sq < op.seq:
                            v = max(v, sg)
                        else:
                            break
                needs.append((d.seq, d.key, v, d))
            needs.sort(key=lambda t: -t[0])
            wl = []
            for (_, key, v, d) in needs:
                if k.get(key, 0) >= v:
                    continue
                wl.append((key, v))
                k[key] = v
                for kk, vv in known_after.get(id(d), {}).items():
                    if k.get(kk, 0) < vv:
                        k[kk] = vv
            wait_lists[id(op)] = wl
            ka = dict(k)
            if op.needed and op.key is not None:
                ka[op.key] = max(ka.get(op.key, 0), op.sig)
            known_after[id(op)] = ka

        def emit_engine(eo, ops):
            nw = 0
            for op in ops:
                for key, val in wait_lists[id(op)]:
                    eo.wait_ge(sems[key], val)
                    nw += 1
                if op.fn is not None:
                    ins = op.fn(eo)
                    if op.needed:
                        ins.then_inc(sems[op.key], 16 if op.chan else 1)
            return nw

        with nc.Block() as block:
            for e in ENGS:
                ops = self.ops[e]

                def f(eo, ops=ops, e=e):
                    stats[e] = (len(ops), emit_engine(eo, ops))
                getattr(block, engobj[e])(f)
        self.stats = stats
        return stats


def mm(out, lhsT, rhs, start=True, stop=True, sgc=False):
    if sgc:
        return lambda e: e.matmul(out, lhsT, rhs, start=start, stop=stop, skip_group_check=True)
    return lambda e: e.matmul(out, lhsT, rhs, start=start, stop=stop)


def tr(out, in_, ident):
    return lambda e: e.transpose(out, in_, ident)


def act(out, in_, func, **kw):
    return lambda e: e.activation(out=out, in_=in_, func=func, **kw)


def tt(out, in0, in1, op):
    return lambda e: e.tensor_tensor(out=out, in0=in0, in1=in1, op=op)


def ts(out, in0, s1, op0, s2=None, op1=None):
    if op1 is None:
        return lambda e: e.tensor_scalar(out=out, in0=in0, scalar1=s1, scalar2=None, op0=op0)
    return lambda e: e.tensor_scalar(out=out, in0=in0, scalar1=s1, scalar2=s2, op0=op0, op1=op1)


def stt(out, in0, scalar, in1, op0, op1):
    return lambda e: e.scalar_tensor_tensor(out=out, in0=in0, scalar=scalar, in1=in1, op0=op0, op1=op1)


def cp(out, in_):
    return lambda e: e.tensor_copy(out=out, in_=in_)


def acp(out, in_):
    return lambda e: e.copy(out=out, in_=in_)


def rcp(out, in_):
    return lambda e: e.reciprocal(out=out, in_=in_)


def dma(out, in_, slow=False):
    if slow:
        return lambda e: e.dma_start(out=out, in_=in_, allow_slow_non_contiguous=True)
    return lambda e: e.dma_start(out=out, in_=in_)


def mset(ap, v):
    return lambda e: e.memset(ap, v)


def _rope_tables():
    t = np.arange(1024)
    rows = (t // 64).astype(np.float64)
    cols = (t % 64).astype(np.float64)

    def blk(pos, half):
        fr = np.power(10000.0, -np.arange(half, dtype=np.float64) / half)
        ang = pos[None, :] * fr[:, None]
        c = np.concatenate([np.cos(ang), np.cos(ang)], 0)
        s = np.concatenate([-np.sin(ang), np.sin(ang)], 0)
        return c, s

    c1, s1 = blk(rows, 8)
    c2, s2 = blk(cols, 8)
    c32 = np.concatenate([c1, c2], 0)
    s32 = np.concatenate([s1, s2], 0)
    d1, e1 = blk(rows, 16)
    d2, e2 = blk(cols, 16)
    c64 = np.concatenate([d1, d2], 0)
    s64 = np.concatenate([e1, e2], 0)
    c128 = np.concatenate([c64, c64], 0)
    s128 = np.concatenate([s64, s64], 0)
    return (c32.astype(np.float32), s32.astype(np.float32), c128.astype(np.float32), s128.astype(np.float32))


def _swap_perm(n, half):
    p = np.arange(n)
    for b in range(0, n, 2 * half):
        p[b:b + half] = np.arange(b + half, b + 2 * half)
        p[b + half:b + 2 * half] = np.arange(b, b + half)
    return p


C_ID = 0
C_TUI = 128
C_TLS = 256
C_TLI = 384
C_TUS = 512
C_ONE = 640
C_SELF = 768
C_SELB = 769
NCST = 772

V_COND = 0
V_BADA = 16
V_G = 112
V_GQL = 152
V_GKV = 154
V_GSUB = 155
V_GKVB = 156
V_GMLB = 284
V_BGB = 412
V_LAMB = 444
V_N0 = 700
V_M0 = 708
NVEC = 724


def _consts():
    c = np.zeros((128, NCST), np.float32)
    s = np.arange(128)[:, None]
    l = np.arange(128)[None, :]
    c[:, C_ID:C_ID + 128] = (s == l)
    c[:, C_TUI:C_TUI + 128] = (s <= l)
    c[:, C_TLS:C_TLS + 128] = (s > l)
    c[:, C_TLI:C_TLI + 128] = (s >= l)
    c[:, C_TUS:C_TUS + 128] = (s < l)
    c[:, C_ONE:C_ONE + 128] = 1.0
    c[0:8, C_SELF] = 1.0
    c[8:16, C_SELB] = 1.0
    return c


def build(parts=('m0', 'f0', 'm1', 'f1'), dbg=False, groups=(0, 1), m0_stop=99, skip=(), m1_stop=99):
    nc = bass.Bass("TRN2", target_bir_lowering=False)
    P = Prog(nc)

    def din(name, shape):
        return nc.dram_tensor(name, list(shape), F32, kind="ExternalInput").ap()

    def dout(name, shape):
        return nc.dram_tensor(name, list(shape), F32, kind="ExternalOutput").ap()

    xg = din("xg", [2, NT, D])
    cst_d = din("cst", [128, NCST])
    vec_d = din("vec", [128, NVEC])
    rope_d = din("rope", [4, 128, 1024])
    w_ada = din("w_ada", [2, D, 6 * D])
    w_ffn_in = din("w_ffn_in", [2, D, 2 * DFF])
    w_ffn_out = din("w_ffn_out", [2, DFF, D])
    w_in_ab = din("w_in_ab", [D, 1952])
    w_in_sw = din("w_in_sw", [D, 1056])
    w_uq = din("w_uq", [256, 1024])
    w_ukv = din("w_ukv", [128, 1024])
    w_out_ab = din("w_out_ab", [D, D])
    w_in_c = din("w_in_c", [D, 3104])
    w_out_c = din("w_out_c", [D, D])
    ckv_c = din("ckv_c", [256, 128])
    kr_c = din("kr_c", [256, 32])
    dk_c = din("dk_c", [4, 256, 128])
    dv_c = din("dv_c", [4, 256, 128])
    c0_d = din("c0", [128, 2, 4, 128])
    y_d = dout("y", [2, NT, D])
    o_ckv = dout("o_ckv", [NT, 128])
    o_kr = dout("o_kr", [NT, 32])
    o_dk = dout("o_dk", [4, 4, 256, 128])
    o_dv = dout("o_dv", [4, 4, 256, 128])
    o_C = dout("o_C", [4, 2, 8, 64, 128])
    o_n = dout("o_n", [4, 2, 8, 64])
    o_m = dout("o_m", [4, 2, 8])
    dbg_out = {}

    def sb(name, shape, dt=F32):
        return nc.alloc_sbuf_tensor(name, list(shape), dt)

    cst = sb("cstS", [128, NCST])
    vec = sb("vecS", [128, NVEC])
    cstb = sb("cstB", [128, NCST], BF16)
    xT = sb("xT", [128, 8, NT])
    modraw = [sb("modraw%d" % l, [128, 48, 2]) for l in range(2)]
    modA = sb("modA", [128, 2, 2, 2, 8])
    condb = sb("condb", [128, 2, 8], BF16)
    stage = [sb("stage%d" % i, [128, 2048]) for i in range(2)]
    wsl = [sb("wsl%d" % i, [128, 2048], BF16) for i in range(3)]
    sqb = [sb("sqb%d" % i, [128, 512], BF16) for i in range(2)]
    tmpf = [sb("tmpf%d" % i, [128, 512]) for i in range(4)]
    rstdb = [sb("rstd%d" % i, [128, 512]) for i in range(2)]
    joinb = sb("joinb", [128, 8])
    pb = [nc.alloc_psum_tensor("pb%d" % i, [128, 512], F32) for i in range(8)]
    pbb = [p[:].bitcast(BF16) for p in pb]
    ARENA = 118 * 1024
    arena = sb("arena", [128, ARENA // 4])

    idf = cst[:, C_ID:C_ID + 128]
    idb = cstb[:, C_ID:C_ID + 128]
    onesb = cstb[:, C_ONE:C_ONE + 128]
    onesf = cst[:, C_ONE:C_ONE + 128]

    st = {'bank': 0, 'stage': 0, 'wsl': 0, 'ev': 0, 'tmpf': 0, 'sqb': 0, 'aoff': 0, 'castq': 0, 'rstd': 0, 'reserved': set()}

    def nb():
        while True:
            b = st['bank']
            st['bank'] = (b + 1) % 8
            if b not in st['reserved']:
                return b

    def ev_eng():
        st['ev'] ^= 1
        return 'act' if st['ev'] else 'dve'

    def evac(out, in_, r, w):
        e = ev_eng()
        if e == 'act':
            P.add('act', acp(out, in_), r=r, w=w)
        else:
            P.add('dve', cp(out, in_), r=r, w=w)

    def ntmp():
        i = st['tmpf']
        st['tmpf'] = (i + 1) % 4
        return i

    def carve(shape, dt=F32, parts=128):
        n = 1
        for s_ in shape[1:]:
            n *= s_
        esz = 2 if dt == BF16 else 4
        nbytes = (n * esz + 31) // 32 * 32
        off = st['aoff']
        assert off + nbytes <= ARENA, ("arena overflow", off, nbytes)
        st['aoff'] = off + nbytes
        a = arena[0:shape[0], off // 4: off // 4 + nbytes // 4]
        if dt == BF16:
            a = a.bitcast(BF16)
        a = a[:, 0:n]
        if len(shape) == 3:
            a = a.rearrange("p (a b) -> p a b", b=shape[2])
        elif len(shape) == 4:
            a = a.rearrange("p (a b c) -> p a b c", b=shape[2], c=shape[3])
        return a

    def arena_reset():
        P.barrier()
        st['aoff'] = 0

    def dump(name, ap, shape, r):
        if not dbg or name in dbg_out:
            return
        d = nc.dram_tensor("dbg_" + name, list(shape), F32, kind="ExternalOutput").ap()
        dbg_out[name] = d
        P.add('pool', dma(d, ap), r=r, chan='out_dbg_' + name)

    st['stage_pool'] = [(stage[i][:], 'stage%d' % i) for i in range(2)]
    st['wsl_pool'] = [(wsl[i][:], 'wsl%d' % i) for i in range(3)]
    base_pools = (st['stage_pool'], st['wsl_pool'])

    def sload(src, shape):
        n = 1
        for s_ in shape[1:]:
            n *= s_
        assert n <= 2048
        pool_ = st['stage_pool']
        si = st['stage'] % len(pool_)
        st['stage'] = si + 1
        stg_ap, stg_name = pool_[si]
        sv = stg_ap[0:shape[0], 0:n]
        if len(shape) == 3:
            sv = sv.rearrange("p (a b) -> p a b", b=shape[2])
        elif len(shape) == 4:
            sv = sv.rearrange("p (a b c) -> p a b c", b=shape[2], c=shape[3])
        P.add('sp', dma(sv, src), w=[stg_name], chan=stg_name)
        return sv, stg_name

    def wload(src, shape, dst=None, dstname=None):
        sv, sn = sload(src, shape)
        if dst is None:
            n = 1
            for s_ in shape[1:]:
                n *= s_
            wpool = st['wsl_pool']
            wi = st['wsl'] % len(wpool)
            st['wsl'] = wi + 1
            w_ap, dstname = wpool[wi]
            wv = w_ap[0:shape[0], 0:n]
            if len(shape) == 3:
                wv = wv.rearrange("p (a b) -> p a b", b=shape[2])
        else:
            wv = dst
        P.add('act', acp(wv, sv), r=[sn], w=[dstname])
        return wv, dstname

    P.add('sp', dma(cst[:], cst_d), w=['cst'], chan='ld_cst')
    P.add('sp', dma(vec[:], vec_d), w=['vec'], chan='ld_vec')
    P.add('dve', cp(cstb[:], cst[:]), r=['cst'], w=['cstb'])

    P.add('act', act(condb[:], vec[:, V_COND:V_COND + 16].rearrange("p (g c) -> p g c", c=8), AF.Silu), r=['vec'], w=['condb'])
    def adaln_closures(l):
        def make(gi):
            def run(slots):
                bk = nb()
                src = w_ada[l][:, gi * 256:(gi + 1) * 256].rearrange("(kc p) n -> p kc n", p=128)
                wv, wn = wload(src, [128, 8, 256], dst=slots[gi % 2], dstname='adas%d' % (gi % 2))
                for c2 in range(2):
                    for kc in range(8):
                        P.add('pe', mm(pb[bk][:, c2 * 2:c2 * 2 + 2], wv[:, kc, c2 * 128:(c2 + 1) * 128], condb[:, :, kc],
                                       start=(kc == 0), stop=(kc == 7)), r=[wn, 'condb'], w=['pb%d' % bk])
                badaT = vec[:, V_BADA + l * 48 + gi * 2:V_BADA + l * 48 + gi * 2 + 2]
                P.add('dve', tt(modraw[l][:, gi * 2:gi * 2 + 2, :], pb[bk][:, 0:4].rearrange("p (o g) -> p o g", g=2),
                                badaT.unsqueeze(2).to_broadcast([128, 2, 2]), ALU.add), r=['pb%d' % bk, 'vec'], w=['modraw%d.%d' % (l, gi)])
            return run

        def fin(slots):
            names = ['modraw%d.%d' % (l, gi) for gi in range(24)]
            first = True
            for g in range(2):
                for wh in range(2):
                    sc = modraw[l][:, (1 + 3 * wh) * 8:(2 + 3 * wh) * 8, g]
                    gv = vec[:, V_G + (2 * l + wh) * 8:V_G + (2 * l + wh + 1) * 8]
                    P.add('dve', stt(modA[:, l, g, wh, :], sc, 1.0, gv, ALU.add, ALU.mult), r=names + ['vec'],
                          w=['modA'] + (['modraw%d' % l] if first else []))
                    first = False
        return [make(gi) for gi in range(24)] + [fin]

    pending_ada = []

    dump('modraw0', modraw[0][:], [128, 48, 2], ['modraw0'])
    dump('modA', modA[:], [128, 2, 2, 2, 8], ['modA'])

    def mcol(l, g, j, c):
        return modraw[l][:, j * 8 + c, g:g + 1]

    def norm_mod(src, dst, dstname, l, g, wh, n_tt=2, final=False):
        for t_ in range(n_tt):
            tok = slice(t_ * 512, (t_ + 1) * 512)
            bk = nb()
            for c in range(8):
                si = st['sqb']
                st['sqb'] ^= 1
                if c % 3 == 0:
                    P.add('act', act(sqb[si][:], src[:, c, tok], AF.Square), r=['xT'], w=['sqb%d' % si])
                elif c % 3 == 1:
                    P.add('dve', tt(sqb[si][:], src[:, c, tok], src[:, c, tok], ALU.mult), r=['xT'], w=['sqb%d' % si])
                else:
                    P.add('pool', tt(sqb[si][:], src[:, c, tok], src[:, c, tok], ALU.mult), r=['xT'], w=['sqb%d' % si])
                P.add('pe', mm(pb[bk][:], onesb, sqb[si][:], start=(c == 0), stop=(c == 7)), r=['sqb%d' % si, 'cstb'], w=['pb%d' % bk])
            ri = st['rstd']
            st['rstd'] ^= 1
            rstd = rstdb[ri]
            rn = 'rstd%d' % ri
            P.add('act', act(rstd[:], pb[bk][:], AF.Ln, scale=1.0 / D, bias=EPS), r=['pb%d' % bk], w=[rn])
            P.add('act', act(rstd[:], rstd[:], AF.Exp, scale=-0.5), r=[rn], w=[rn])
            for c in range(8):
                ti = ntmp()
                if final:
                    gv = vec[:, V_G + 4 * 8 + c:V_G + 4 * 8 + c + 1]
                    P.add('dve', stt(dst[:, c, tok], src[:, c, tok], gv, rstd[:], ALU.mult, ALU.mult),
                          r=['xT', rn, 'vec'], w=[dstname + '.%d' % t_])
                else:
                    P.add('dve', tt(tmpf[ti][:], src[:, c, tok], rstd[:], ALU.mult), r=['xT', rn], w=['tmpf%d' % ti])
                    if c % 8 in (2, 5, 7):
                        P.add('dve', ts(dst[:, c, tok], tmpf[ti][:], modA[:, l, g, wh, c:c + 1], ALU.mult, mcol(l, g, 3 * wh, c), ALU.add),
                              r=['tmpf%d' % ti, 'modA', 'modraw%d' % l], w=[dstname + '.%d.%d' % (t_, c)])
                    else:
                        P.add('act', act(dst[:, c, tok], tmpf[ti][:], AF.Identity, scale=modA[:, l, g, wh, c:c + 1],
                                         bias=mcol(l, g, 3 * wh, c)), r=['tmpf%d' % ti, 'modA', 'modraw%d' % l], w=[dstname + '.%d.%d' % (t_, c)])
            if not final:
                P.add('dve', mset(joinb[:, t_:t_ + 1], 0.0), r=[dstname + '.%d.%d' % (t_, c) for c in range(8)], w=[dstname + '.%d' % t_])

    def ffn(l, g):
        arena_reset()
        hm = carve([128, 8, NT], BF16)
        u = carve([128, 22, NT], BF16)
        ada_sl = [carve([128, 8, 256], BF16) for _ in range(2)]
        st['stage_pool'] = base_pools[0] + [(carve([128, 2048]), 'stageF%d' % i) for i in range(2)]
        st['wsl_pool'] = base_pools[1] + [(carve([128, 2048], BF16), 'wslF%d' % i) for i in range(3)]

        def bg():
            if pending_ada:
                pending_ada.pop(0)(ada_sl)
        norm_mod(xT, hm, 'hm', l, g, 1)
        if l == 0 and g == 0:
            dump('xT0', xT[:], [128, 8, NT], ['xT'])
            dump('hm0', hm, [128, 8, NT], ['hm.0', 'hm.1'])
        for j2 in range(11):
            wa, wan = wload(w_ffn_in[l][:, j2 * 256:(j2 + 1) * 256].rearrange("(kc p) n -> p kc n", p=128), [128, 8, 256])
            wb_, wbn = wload(w_ffn_in[l][:, DFF + j2 * 256:DFF + (j2 + 1) * 256].rearrange("(kc p) n -> p kc n", p=128), [128, 8, 256])
            bg()
            for jj in range(2):
                j = j2 * 2 + jj
                for t_ in range(2):
                    tok = slice(t_ * 512, (t_ + 1) * 512)
                    ba = nb()
                    for kc in range(8):
                        P.add('pe', mm(pb[ba][:], wa[:, kc, jj * 128:(jj + 1) * 128], hm[:, kc, tok], start=(kc == 0), stop=(kc == 7)),
                              r=[wan, 'hm.%d' % t_], w=['pb%d' % ba])
                    bb = nb()
                    for kc in range(8):
                        P.add('pe', mm(pb[bb][:], wb_[:, kc, jj * 128:(jj + 1) * 128], hm[:, kc, tok], start=(kc == 0), stop=(kc == 7)),
                              r=[wbn, 'hm.%d' % t_], w=['pb%d' % bb])
                    ti = ntmp()
                    P.add('act', act(tmpf[ti][:], pb[ba][:], AF.Silu), r=['pb%d' % ba], w=['tmpf%d' % ti])
                    P.add('dve', tt(u[:, j, tok], tmpf[ti][:], pb[bb][:], ALU.mult), r=['tmpf%d' % ti, 'pb%d' % bb], w=['u.%d.%d' % (j, t_)])
        if l == 0 and g == 0:
            dump('u0', u, [128, 22, NT], ['u.%d.%d' % (j, t_) for j in range(22) for t_ in range(2)])
        for oc in range(8):
            w1, w1n = wload(w_ffn_out[l][0:11 * 128, oc * 128:(oc + 1) * 128].rearrange("(j p) n -> p j n", p=128), [128, 11, 128])
            w2, w2n = wload(w_ffn_out[l][11 * 128:22 * 128, oc * 128:(oc + 1) * 128].rearrange("(j p) n -> p j n", p=128), [128, 11, 128])
            bg()
            bg()
            for t_ in range(2):
                tok = slice(t_ * 512, (t_ + 1) * 512)
                bk = nb()
                for j in range(22):
                    wv = w1 if j < 11 else w2
                    P.add('pe', mm(pb[bk][:], wv[:, j % 11, :], u[:, j, tok], start=(j == 0), stop=(j == 21)),
                          r=[w1n if j < 11 else w2n, 'u.%d.%d' % (j, t_)], w=['pb%d' % bk])
                P.add('dve', stt(xT[:, oc, tok], pb[bk][:], mcol(l, g, 5, oc), xT[:, oc, tok], ALU.mult, ALU.add),
                      r=['pb%d' % bk, 'modraw%d' % l, 'xT'], w=['xT'])
        while pending_ada:
            bg()
        st['stage_pool'], st['wsl_pool'] = base_pools


    lamc = sb("lamc", [128, 4])
    lamt = sb("lamt", [128, 2, 64])
    lv = vec[:, V_LAMB:V_LAMB + 256].rearrange("p (a b) -> p a b", b=64)
    P.add('dve', tt(lamt[:], lv[:, 0::2, :], lv[:, 1::2, :], ALU.mult), r=['vec'], w=['lamt'])
    P.add('dve', lambda e: e.tensor_reduce(out=lamc[:, 2:4], in_=lamt[:], axis=AX.X, op=ALU.add), r=['lamt'], w=['lamc'])
    P.add('act', act(lamc[:, 2:4], lamc[:, 2:4], AF.Exp), r=['lamc'], w=['lamc'])
    P.add('dve', tt(lamc[:, 0:1], lamc[:, 2:3], lamc[:, 3:4], ALU.subtract), r=['lamc'], w=['lamc'])
    P.add('dve', ts(lamc[:, 0:1], lamc[:, 0:1], LAM_INIT0, ALU.add), r=['lamc'], w=['lamc'])
    P.add('dve', ts(lamc[:, 1:2], lamc[:, 0:1], -1.0, ALU.mult), r=['lamc'], w=['lamc'])
    gsubc = sb("gsubc", [128, 1])
    P.add('dve', ts(gsubc[:], vec[:, V_GSUB:V_GSUB + 1], 1.0 - LAM_INIT0, ALU.mult), r=['vec'], w=['gsubc'])

    SCALE_A = 96.0 ** -0.5
    SCALE_B = 64.0 ** -0.5

    def mixer0(g):
        arena_reset()
        l = 0
        NK = 1024 if g == 0 else 1280
        NKT = NK // 128
        hm = carve([128, 8, NT], BF16)
        kN = carve([128, 4, NK], BF16)
        krT = carve([128, NK], BF16)
        ckvn = carve([128, NK], BF16)
        Vm = carve([128, NKT, 512], BF16)
        dkT = carve([128, 4, NK], BF16)
        dvt = carve([128, NKT, 512], BF16)
        wukv_b = carve([128, 1024], BF16)
        wuq_b = carve([128, 2, 1024], BF16)
        cqn = carve([128, 2, 512], BF16)
        qNz = carve([128, 8, 512], BF16)
        qR = carve([128, 8, 512], BF16)
        dqT = carve([128, 4, 512], BF16)
        outA = carve([128, 4, 512], BF16)
        outB = carve([128, 4, 512], BF16)
        PT = [carve([128, 512], BF16) for _ in range(4)]
        ES = [carve([128, 512]) for _ in range(2)]
        ESb = [carve([128, 512], BF16) for _ in range(2)]
        if g == 0:
            stg = [carve([128, 512]) for _ in range(4)]
        if g == 1:
            rc32 = carve([32, 512])
            rs32 = carve([32, 512])
            rc128 = carve([128, 512])
            rs128 = carve([128, 512])

            def load_rope(t_):
                tk = slice(t_ * 512, (t_ + 1) * 512)
                P.add('sp', dma(rc32, rope_d[0][0:32, tk]), w=['rc32'], chan='ld_rope0')
                P.add('sp', dma(rs32, rope_d[1][0:32, tk]), w=['rs32'], chan='ld_rope1')
                P.add('sp', dma(rc128, rope_d[2][:, tk]), w=['rc128'], chan='ld_rope2')
                P.add('sp', dma(rs128, rope_d[3][:, tk]), w=['rs128'], chan='ld_rope3')
        cnt = {'pt': 0, 'stg': 0, 'sb': 0, 'ab': 0, 'es': 0, 'eseng': 0}

        def nsb():
            cnt['sb'] = (cnt['sb'] + 1) % 4
            return cnt['sb']

        def nab():
            cnt['ab'] = (cnt['ab'] + 1) % 4
            return 4 + cnt['ab']

        def nstg():
            cnt['stg'] = (cnt['stg'] + 1) % 4
            return cnt['stg']

        ada_sl = [carve([128, 8, 256], BF16) for _ in range(2)] if pending_ada else None

        def bg():
            if pending_ada:
                pending_ada.pop(0)(ada_sl)

        P.add('pool', mset(qNz, 0.0), w=['qNz'])
        P.add('pool', mset(qR, 0.0), w=['qRz'])
        P.add('pool', mset(krT, 0.0), w=['krTz'])
        norm_mod(xT, hm, 'hm', 0, g, 0)

        def fm_mm(bk, wv, wn, c0, M, t_):
            tok = slice(t_ * 512, (t_ + 1) * 512)
            for kc in range(8):
                P.add('pe', mm(pb[bk][0:M, :], wv[:, kc, c0:c0 + M], hm[:, kc, tok], start=(kc == 0), stop=(kc == 7)),
                      r=[wn, 'hm.%d' % t_], w=['pb%d' % bk])

        def tm_mm(bk, col0, wv, wn, n, j):
            for kc in range(8):
                P.add('pe', mm(pb[bk][:, col0:col0 + n], hm[:, kc, j * 128:(j + 1) * 128], wv[:, kc, 0:n], start=(kc == 0), stop=(kc == 7)),
                      r=[wn, 'hm.%d' % (j // 4)], w=['pb%d' % bk])

        def rope_evac(dst, dstname, M, bk, bks, ctab, stab, tok, cn, sn_, dst2=None, extra_r=()):
            t1 = ntmp()
            P.add('dve', tt(tmpf[t1][0:M, :], pb[bk][0:M, :], ctab[0:M, :], ALU.mult), r=['pb%d' % bk, cn], w=['tmpf%d' % t1])
            t2 = ntmp()
            P.add('dve', tt(tmpf[t2][0:M, :], pb[bks][0:M, :], stab[0:M, :], ALU.mult), r=['pb%d' % bks, sn_], w=['tmpf%d' % t2])
            P.add('pool', tt(dst, tmpf[t1][0:M, :], tmpf[t2][0:M, :], ALU.add), r=['tmpf%d' % t1, 'tmpf%d' % t2] + list(extra_r), w=[dstname])
            if dst2 is not None:
                P.add('pool', tt(dst2, tmpf[t1][0:M, :], tmpf[t2][0:M, :], ALU.add), r=['tmpf%d' % t1, 'tmpf%d' % t2], w=[dstname])

        wukv_b, _ = wload(w_ukv, [128, 1024], dst=wukv_b, dstname='wukv')
        wuq_b, _ = wload(w_uq.rearrange("(kc p) n -> p kc n", p=128), [128, 2, 1024], dst=wuq_b, dstname='wuq')
        wA, wAn = wload(w_in_ab[:, 256:416].rearrange("(kc p) n -> p kc n", p=128), [128, 8, 160])
        bg()
        bg()
        if g == 1:
            wAs, wAsn = wload(w_in_sw[:, 0:32].rearrange("(kc p) n -> p kc n", p=128), [128, 8, 32])
        for t_ in range(2):
            tok = slice(t_ * 512, (t_ + 1) * 512)
            gtok = slice(t_ * 512, (t_ + 1) * 512)
            if g == 1:
                load_rope(t_)
            bk = nb()
            fm_mm(bk, wA, wAn, 0, 128, t_)
            si = st['sqb']
            st['sqb'] ^= 1
            P.add('act', act(sqb[si][:], pb[bk][:], AF.Square), r=['pb%d' % bk], w=['sqb%d' % si])
            b2 = nb()
            P.add('pe', mm(pb[b2][:], onesb, sqb[si][:]), r=['sqb%d' % si, 'cstb'], w=['pb%d' % b2])
            ri = st['rstd']
            st['rstd'] ^= 1
            rn = 'rstd%d' % ri
            P.add('act', act(rstdb[ri][:], pb[b2][:], AF.Ln, scale=1.0 / 128, bias=EPS), r=['pb%d' % b2], w=[rn])
            P.add('act', act(rstdb[ri][:], rstdb[ri][:], AF.Exp, scale=-0.5), r=[rn], w=[rn])
            P.add('dve', stt(ckvn[:, gtok], pb[bk][:], vec[:, V_GKV:V_GKV + 1], rstdb[ri][:], ALU.mult, ALU.mult),
                  r=['pb%d' % bk, rn, 'vec'], w=['ckvn.%d' % t_])
            bk = nb()
            fm_mm(bk, wA, wAn, 128, 32, t_)
            if g == 0:
                evac(krT[0:32, gtok], pb[bk][0:32, :], r=['pb%d' % bk, 'krTz'], w=['krT.%d' % t_])
            else:
                bks = nb()
                fm_mm(bks, wAs, wAsn, 0, 32, t_)
                rope_evac(krT[0:32, gtok], 'krT.%d' % t_, 32, bk, bks, rc32, rs32, tok, 'rc32', 'rs32', extra_r=['krTz'])
        if g == 0:
            for j in range(8):
                bk = nb()
                tm_mm(bk, 0, wA, wAn, 160, j)
                si_ = nstg()
                sg = stg[si_]
                ti = ntmp()
                P.add('act', act(tmpf[ti][:, 0:128], pb[bk][:, 0:128], AF.Square, accum_out=tmpf[ti][:, 128:129]), r=['pb%d' % bk], w=['tmpf%d' % ti])
                P.add('act', act(tmpf[ti][:, 129:130], tmpf[ti][:, 128:129], AF.Ln, scale=1.0 / 128, bias=EPS), r=['tmpf%d' % ti], w=['tmpf%d' % ti])
                P.add('act', act(tmpf[ti][:, 130:131], tmpf[ti][:, 129:130], AF.Exp, scale=-0.5), r=['tmpf%d' % ti], w=['tmpf%d' % ti])
                P.add('dve', stt(sg[:, 0:128], pb[bk][:, 0:128], tmpf[ti][:, 130:131], vec[:, V_GKVB:V_GKVB + 128], ALU.mult, ALU.mult),
                      r=['pb%d' % bk, 'tmpf%d' % ti, 'vec'], w=['stg%d' % si_])
                P.add('act', acp(sg[:, 128:160], pb[bk][:, 128:160]), r=['pb%d' % bk], w=['stg%d' % si_])
                P.add('pool', dma(o_ckv[j * 128:(j + 1) * 128, :], sg[:, 0:128]), r=['stg%d' % si_], chan='out_stg%d' % si_)
                P.add('pool', dma(o_kr[j * 128:(j + 1) * 128, :], sg[:, 128:160]), r=['stg%d' % si_], chan='out_stg%d' % si_)
        if m0_stop <= 1:
            return
        for hh in range(0 if 'dk' in skip else 2):
            wK, wKn = wload(w_in_ab[:, 928 + hh * 256:928 + (hh + 1) * 256].rearrange("(kc p) n -> p kc n", p=128), [128, 8, 256])
            bg()
            if g == 1:
                wKs, wKsn = wload(w_in_sw[:, 544 + hh * 256:544 + (hh + 1) * 256].rearrange("(kc p) n -> p kc n", p=128), [128, 8, 256])
            for t_ in range(2):
                if g == 1:
                    load_rope(t_)
                for h2 in range(2):
                    h = hh * 2 + h2
                    tok = slice(t_ * 512, (t_ + 1) * 512)
                    bk = nb()
                    fm_mm(bk, wK, wKn, h2 * 128, 128, t_)
                    if g == 0:
                        evac(dkT[:, h, tok], pb[bk][:], r=['pb%d' % bk], w=['dkT.%d.%d' % (h, t_)])
                    else:
                        bks = nb()
                        fm_mm(bks, wKs, wKsn, h2 * 128, 128, t_)
                        rope_evac(dkT[:, h, tok], 'dkT.%d.%d' % (h, t_), 128, bk, bks, rc128, rs128, tok, 'rc128', 'rs128')
            if g == 0:
                for j in range(8):
                    bk = nb()
                    tm_mm(bk, 0, wK, wKn, 256, j)
                    si_ = nstg()
                    evac(stg[si_][:, 0:256], pb[bk][:, 0:256], r=['pb%d' % bk], w=['stg%d' % si_])
                    if 'odk' not in skip:
                      P.add('pool', dma(o_dk[j // 2, 2 * hh:2 * hh + 2, (j % 2) * 128:(j % 2 + 1) * 128, :].rearrange("h t d -> t h d"),
                                    stg[si_][:, 0:256].rearrange("t (h d) -> t h d", d=128)), r=['stg%d' % si_], chan='out_stg%d' % si_)
        wV0, wV0n = wload(w_in_ab[:, 1440:1696].rearrange("(kc p) n -> p kc n", p=128), [128, 8, 256])
        wV1, wV1n = wload(w_in_ab[:, 1696:1952].rearrange("(kc p) n -> p kc n", p=128), [128, 8, 256])
        bg()
        bg()
        for j in range(0 if 'dv' in skip else 8):
            bk = nb()
            tm_mm(bk, 0, wV0, wV0n, 256, j)
            tm_mm(bk, 256, wV1, wV1n, 256, j)
            P.add('act', acp(dvt[:, j, :], pb[bk][:]), r=['pb%d' % bk], w=['dvt.%d' % j])
            if g == 0:
                si_ = nstg()
                P.add('dve', cp(stg[si_][:], pb[bk][:]), r=['pb%d' % bk], w=['stg%d' % si_])
                if 'odv' not in skip:
                  P.add('pool', dma(o_dv[j // 2, :, (j % 2) * 128:(j % 2 + 1) * 128, :].rearrange("h t d -> t h d"),
                                stg[si_][:].rearrange("t (h d) -> t h d", d=128)), r=['stg%d' % si_], chan='out_stg%d' % si_)
        if g == 1:
            sv, sn = sload(ckv_c.rearrange("(a p) f -> p a f", p=128), [128, 2, 128])
            bk = nb()
            for a in range(2):
                P.add('pe', tr(pb[bk][:, a * 128:(a + 1) * 128], sv[:, a, :], idf), r=[sn, 'cst'], w=['pb%d' % bk])
            evac(ckvn[:, 1024:1280], pb[bk][:, 0:256], r=['pb%d' % bk], w=['ckvn.c'])
            sv, sn = sload(kr_c.rearrange("(a p) f -> p a f", p=128), [128, 2, 32])
            bk = nb()
            for a in range(2):
                P.add('pe', tr(pb[bk][0:32, a * 128:(a + 1) * 128], sv[:, a, :], idf), r=[sn, 'cst'], w=['pb%d' % bk])
            evac(krT[0:32, 1024:1280], pb[bk][0:32, 0:256], r=['pb%d' % bk, 'krTz'], w=['krT.c'])
            sv, sn = sload(dk_c.rearrange("h (a p) f -> p h a f", p=128), [128, 4, 2, 128])
            for h in range(4):
                bk = nb()
                for a in range(2):
                    P.add('pe', tr(pb[bk][:, a * 128:(a + 1) * 128], sv[:, h, a, :], idf), r=[sn, 'cst'], w=['pb%d' % bk])
                evac(dkT[:, h, 1024:1280], pb[bk][:, 0:256], r=['pb%d' % bk], w=['dkT.%d.c' % h])
            sv, sn = sload(dv_c.rearrange("h (a p) f -> p h a f", p=128), [128, 4, 2, 128])
            for a in range(2):
                P.add('pool', cp(dvt[:, 8 + a, :].rearrange("p (h f) -> p h f", f=128), sv[:, :, a, :]), r=[sn], w=['dvt.%d' % (8 + a)])
        if m0_stop <= 2:
            return
        ckv_names = ['ckvn.0', 'ckvn.1'] + (['ckvn.c'] if g == 1 else [])
        kblocks = [(0, 512), (512, 512)] + ([(1024, 256)] if g == 1 else [])
        for c in range(4):
            for (k0, kn_) in kblocks:
                bk = nb()
                P.add('pe', mm(pb[bk][:, 0:kn_], wukv_b[:, c * 128:(c + 1) * 128], ckvn[:, k0:k0 + kn_]), r=['wukv'] + ckv_names, w=['pb%d' % bk])
                evac(kN[:, c, k0:k0 + kn_], pb[bk][:, 0:kn_], r=['pb%d' % bk], w=['kN.%d.%d' % (c, k0)])
        for kt in range(NKT):
            bk = nb()
            P.add('pe', mm(pb[bk][:], ckvn[:, kt * 128:(kt + 1) * 128], wukv_b[:, 512:1024]), r=['wukv'] + ckv_names, w=['pb%d' % bk])
            evac(Vm[:, kt, :], pb[bk][:], r=['pb%d' % bk], w=['Vm.%d' % kt])
        kN_names = ['kN.%d.%d' % (c, k0) for c in range(4) for (k0, _) in kblocks]
        krT_names = ['krT.0', 'krT.1'] + (['krT.c'] if g == 1 else [])
        Vm_names = ['Vm.%d' % kt for kt in range(NKT)]
        dvt_names = ['dvt.%d' % kt for kt in range(NKT)]

        if m0_stop <= 3:
            return
        for t_ in range(2):
            tok = slice(t_ * 512, (t_ + 1) * 512)
            if g == 1:
                load_rope(t_)
            wQ, wQn = wload(w_in_ab[:, 0:256].rearrange("(kc p) n -> p kc n", p=128), [128, 8, 256])
            bg()
            b2 = nb()
            cqi = {}
            for c in range(2):
                bk = nb()
                fm_mm(bk, wQ, wQn, c * 128, 128, t_)
                cqi[c] = ntmp()
                P.add('dve', cp(tmpf[cqi[c]][:], pb[bk][:]), r=['pb%d' % bk], w=['tmpf%d' % cqi[c]])
                si = st['sqb']
                st['sqb'] ^= 1
                P.add('act', act(sqb[si][:], pb[bk][:], AF.Square), r=['pb%d' % bk], w=['sqb%d' % si])
                P.add('pe', mm(pb[b2][:], onesb, sqb[si][:], start=(c == 0), stop=(c == 1)), r=['sqb%d' % si, 'cstb'], w=['pb%d' % b2])
            ri = st['rstd']
            st['rstd'] ^= 1
            rn = 'rstd%d' % ri
            P.add('act', act(rstdb[ri][:], pb[b2][:], AF.Ln, scale=1.0 / 256, bias=EPS), r=['pb%d' % b2], w=[rn])
            P.add('act', act(rstdb[ri][:], rstdb[ri][:], AF.Exp, scale=-0.5), r=[rn], w=[rn])
            for c in range(2):
                P.add('dve', stt(cqn[:, c, :], tmpf[cqi[c]][:], vec[:, V_GQL + c:V_GQL + c + 1], rstdb[ri][:], ALU.mult, ALU.mult),
                      r=['tmpf%d' % cqi[c], rn, 'vec'], w=['cqn'])
            for c in range(4):
                bk = nb()
                for kc in range(2):
                    P.add('pe', mm(pb[bk][:], wuq_b[:, kc, c * 128:(c + 1) * 128], cqn[:, kc, :], start=(kc == 0), stop=(kc == 1)),
                          r=['wuq', 'cqn'], w=['pb%d' % bk])
                P.add('act', acp(qNz[0:64, 2 * c, :], pb[bk][0:64, :]), r=['pb%d' % bk, 'qNz'], w=['qN.%d' % c])
                P.add('dve', cp(qNz[64:128, 2 * c + 1, :], pb[bk][64:128, :]), r=['pb%d' % bk, 'qNz'], w=['qN.%d' % c])
            for h in range(8):
                bk = nb()
                for kc in range(2):
                    P.add('pe', mm(pb[bk][0:32, :], wuq_b[:, kc, 512 + h * 32:512 + (h + 1) * 32], cqn[:, kc, :], start=(kc == 0), stop=(kc == 1)),
                          r=['wuq', 'cqn'], w=['pb%d' % bk])
                if g == 0:
                    evac(qR[0:32, h, :], pb[bk][0:32, :], r=['pb%d' % bk, 'qRz'], w=['qR.%d' % h])
                else:
                    bks = nb()
                    for kc in range(2):
                        P.add('pe', mm(pb[bks][0:32, :], wuq_b[:, kc, 768 + h * 32:768 + (h + 1) * 32], cqn[:, kc, :], start=(kc == 0), stop=(kc == 1)),
                              r=['wuq', 'cqn'], w=['pb%d' % bks])
                    rope_evac(qR[0:32, h, :], 'qR.%d' % h, 32, bk, bks, rc32, rs32, tok, 'rc32', 'rs32', extra_r=['qRz'])
            for hh in range(2):
                wD, wDn = wload(w_in_ab[:, 416 + hh * 256:416 + (hh + 1) * 256].rearrange("(kc p) n -> p kc n", p=128), [128, 8, 256])
                bg()
                if g == 1:
                    wDs, wDsn = wload(w_in_sw[:, 32 + hh * 256:32 + (hh + 1) * 256].rearrange("(kc p) n -> p kc n", p=128), [128, 8, 256])
                for h2 in range(2):
                    h = hh * 2 + h2
                    bk = nb()
                    fm_mm(bk, wD, wDn, h2 * 128, 128, t_)
                    if g == 0:
                        evac(dqT[:, h, :], pb[bk][:], r=['pb%d' % bk], w=['dqT.%d' % h])
                    else:
                        bks = nb()
                        fm_mm(bks, wDs, wDsn, h2 * 128, 128, t_)
                        rope_evac(dqT[:, h, :], 'dqT.%d' % h, 128, bk, bks, rc128, rs128, tok, 'rc128', 'rs128')
            LOOK = 3
            NQ = 256 if g == 0 else 512
            for qb in range(512 // NQ):
                qc = slice(qb * NQ, (qb + 1) * NQ)
                if g == 0:
                    seq = t_ * 2 + qb
                    kts = [2 * seq, 2 * seq + 1]
                else:
                    kts = list(range(NKT))
                nk = len(kts)
                steps = []

                def esum_step(hd_, i_, nk_, pi):
                    if i_ == 0:
                        hd_['es'] = cnt['es']
                        cnt['es'] ^= 1
                    k_ = hd_['es']
                    eng_ = 'pool' if cnt['eseng'] % 3 == 0 else 'dve'
                    cnt['eseng'] += 1
                    if nk_ == 1:
                        P.add(eng_, cp(ESb[k_][:, 0:NQ], PT[pi][:, 0:NQ]), r=['PT%d' % pi], w=['ESb%d' % k_])
                    elif i_ == 0:
                        P.add(eng_, cp(ES[k_][:, 0:NQ], PT[pi][:, 0:NQ]), r=['PT%d' % pi], w=['ES%d' % k_])
                    elif i_ < nk_ - 1:
                        P.add(eng_, tt(ES[k_][:, 0:NQ], ES[k_][:, 0:NQ], PT[pi][:, 0:NQ], ALU.add), r=['PT%d' % pi, 'ES%d' % k_], w=['ES%d' % k_])
                    else:
                        P.add(eng_, tt(ESb[k_][:, 0:NQ], ES[k_][:, 0:NQ], PT[pi][:, 0:NQ], ALU.add), r=['PT%d' % pi, 'ES%d' % k_], w=['ESb%d' % k_])
                for h in range(8):
                    hd = {}
                    for i_, kt in enumerate(kts):
                        def s1(h=h, kt=kt, i_=i_, hd=hd):
                            c, e = h // 2, h % 2
                            pr = slice(e * 64, (e + 1) * 64)
                            ks = slice(kt * 128, (kt + 1) * 128)
                            if i_ == 0:
                                hd['ba'] = nab()
                                hd['bd'] = nab()
                            bs = nsb()
                            P.add('pe', mm(pb[bs][:, 0:NQ], kN[:, c, ks], qNz[:, h, qc], start=True, stop=False),
                                  r=kN_names + ['qN.%d' % c], w=['pb%d' % bs])
                            P.add('pe', mm(pb[bs][:, 0:NQ], krT[:, ks], qR[:, h, qc], start=False, stop=True),
                                  r=krT_names + ['qR.%d' % h], w=['pb%d' % bs])
                            return bs

                        def s2(bs, h=h, kt=kt, i_=i_, hd=hd):
                            c, e = h // 2, h % 2
                            pr = slice(e * 64, (e + 1) * 64)
                            ba, bd = hd['ba'], hd['bd']
                            pi = cnt['pt']
                            cnt['pt'] = (pi + 1) % 4
                            P.add('act', act(PT[pi][:, 0:NQ], pb[bs][:, 0:NQ], AF.Exp, scale=SCALE_A), r=['pb%d' % bs], w=['PT%d' % pi])
                            P.add('pe', mm(pb[ba][0:64, 0:NQ], Vm[:, kt, h * 64:(h + 1) * 64], PT[pi][:, 0:NQ], start=(i_ == 0), stop=(i_ == nk - 1)),
                                  r=Vm_names + ['PT%d' % pi], w=['pb%d' % ba])
                            esum_step(hd, i_, nk, pi)
                            if i_ == nk - 1:
                                P.add('pe', mm(pb[bd][0:64, 0:NQ], onesb[:, 0:64], ESb[hd['es']][:, 0:NQ]), r=['cstb', 'ESb%d' % hd['es']], w=['pb%d' % bd])
                            if i_ == nk - 1:
                                ti = ntmp()
                                P.add('act', act(tmpf[ti][0:64, 0:NQ], pb[bd][0:64, 0:NQ], AF.Ln), r=['pb%d' % bd], w=['tmpf%d' % ti])
                                P.add('act', act(tmpf[ti][0:64, 0:NQ], tmpf[ti][0:64, 0:NQ], AF.Exp, scale=-1.0), r=['tmpf%d' % ti], w=['tmpf%d' % ti])
                                P.add('dve', tt(outA[pr, c, qc], pb[ba][0:64, 0:NQ], tmpf[ti][0:64, 0:NQ], ALU.mult),
                                      r=['pb%d' % ba, 'tmpf%d' % ti], w=['outA.%d' % c])
                        steps.append((s1, s2))
                pend = []
                for idx in range(len(steps) + LOOK):
                    if idx < len(steps):
                        pend.append(steps[idx][0]())
                    if idx >= LOOK:
                        steps[idx - LOOK][1](pend[idx - LOOK])
                steps = []
                LOOKD = 1
                for h in range(4):
                    hd = {}
                    for i_, kt in enumerate(kts):
                        def s1(h=h, kt=kt, i_=i_, hd=hd):
                            ks = slice(kt * 128, (kt + 1) * 128)
                            if i_ == 0:
                                hd[0] = (nab(), nab())
                                hd[1] = (nab(), nab())
                            bss = []
                            for sub in range(2):
                                pr = slice(sub * 64, (sub + 1) * 64)
                                bs = nsb()
                                bss.append(bs)
                                P.add('pe', mm(pb[bs][:, 0:NQ], dkT[pr, h, ks], dqT[pr, h, qc]),
                                      r=['dkT.%d.0' % h, 'dkT.%d.1' % h] + (['dkT.%d.c' % h] if g == 1 else []) + ['dqT.%d' % h], w=['pb%d' % bs])
                            return bss

                        def s2(bss, h=h, kt=kt, i_=i_, hd=hd):
                            pis = []
                            for sub in range(2):
                                pi = cnt['pt']
                                cnt['pt'] = (pi + 1) % 4
                                pis.append(pi)
                                P.add('act', act(PT[pi][:, 0:NQ], pb[bss[sub]][:, 0:NQ], AF.Exp, scale=SCALE_B), r=['pb%d' % bss[sub]], w=['PT%d' % pi])
                            for sub in range(2):
                                bnum, bden = hd[sub]
                                pi = pis[sub]
                                P.add('pe', mm(pb[bnum][:, 0:NQ], dvt[:, kt, h * 128:(h + 1) * 128], PT[pi][:, 0:NQ], start=(i_ == 0), stop=(i_ == nk - 1)),
                                      r=dvt_names + ['PT%d' % pi], w=['pb%d' % bnum])
                                hs_ = hd.setdefault(('es', sub), {})
                                esum_step(hs_, i_, nk, pi)
                                if i_ == nk - 1:
                                    P.add('pe', mm(pb[bden][:, 0:NQ], onesb, ESb[hs_['es']][:, 0:NQ]), r=['cstb', 'ESb%d' % hs_['es']], w=['pb%d' % bden])
                            if i_ == nk - 1:
                                (n1, d1), (n2, d2) = hd[0], hd[1]
                                t1 = ntmp()
                                A1 = tmpf[t1][:, 0:NQ]
                                P.add('act', act(A1, pb[d1][:, 0:NQ], AF.Ln), r=['pb%d' % d1], w=['tmpf%d' % t1])
                                P.add('act', act(A1, A1, AF.Exp, scale=-1.0), r=['tmpf%d' % t1], w=['tmpf%d' % t1])
                                P.add('dve', tt(A1, pb[n1][:, 0:NQ], A1, ALU.mult), r=['pb%d' % n1, 'tmpf%d' % t1], w=['tmpf%d' % t1])
                                t2 = ntmp()
                                A2 = tmpf[t2][:, 0:NQ]
                                P.add('act', act(A2, pb[d2][:, 0:NQ], AF.Ln), r=['pb%d' % d2], w=['tmpf%d' % t2])
                                P.add('act', act(A2, A2, AF.Exp, scale=-1.0), r=['tmpf%d' % t2], w=['tmpf%d' % t2])
                                P.add('dve', tt(A2, pb[n2][:, 0:NQ], A2, ALU.mult), r=['pb%d' % n2, 'tmpf%d' % t2], w=['tmpf%d' % t2])
                                P.add('dve', stt(A1, A2, lamc[:, 1:2], A1, ALU.mult, ALU.add), r=['tmpf%d' % t1, 'tmpf%d' % t2, 'lamc'], w=['tmpf%d' % t1])
                                si = st['sqb']
                                st['sqb'] ^= 1
                                P.add('act', act(sqb[si][:, 0:NQ], A1, AF.Square), r=['tmpf%d' % t1], w=['sqb%d' % si])
                                bq = d1
                                P.add('pe', mm(pb[bq][:, 0:NQ], onesb, sqb[si][:, 0:NQ]), r=['sqb%d' % si, 'cstb'], w=['pb%d' % bq])
                                P.add('act', act(A2, pb[bq][:, 0:NQ], AF.Ln, scale=1.0 / 128, bias=EPS), r=['pb%d' % bq], w=['tmpf%d' % t2])
                                P.add('act', act(A2, A2, AF.Exp, scale=-0.5), r=['tmpf%d' % t2], w=['tmpf%d' % t2])
                                P.add('dve', stt(outB[:, h, qc], A1, gsubc[:, 0:1], A2, ALU.mult, ALU.mult),
                                      r=['tmpf%d' % t1, 'tmpf%d' % t2, 'gsubc'], w=['outB.%d' % h])
                        steps.append((s1, s2))
                LOOK_SAVE = LOOK
                LOOK = LOOKD
                pend = []
                for idx in range(len(steps) + LOOK):
                    if idx < len(steps):
                        pend.append(steps[idx][0]())
                    if idx >= LOOK:
                        steps[idx - LOOK][1](pend[idx - LOOK])
                LOOK = LOOK_SAVE
            if dbg and t_ == 1:
                dump('outA_%d' % g, outA, [128, 4, 512], ['outA.%d' % c for c in range(4)])
                dump('outB_%d' % g, outB, [128, 4, 512], ['outB.%d' % c for c in range(4)])
                dump('qN_%d' % g, qNz, [128, 8, 512], ['qN.%d' % c for c in range(4)])
                dump('qR_%d' % g, qR[0:32], [32, 8, 512], ['qR.%d' % c for c in range(8)])
                dump('dqT_%d' % g, dqT, [128, 4, 512], ['dqT.%d' % c for c in range(4)])
                dump('kN_%d' % g, kN, [128, 4, NK], kN_names)
                dump('krT_%d' % g, krT[0:32], [32, NK], krT_names)
                dump('ckvn_%d' % g, ckvn, [128, NK], ckv_names)
                dump('Vm_%d' % g, Vm, [128, NKT, 512], Vm_names)
                dump('dkT_%d' % g, dkT, [128, 4, NK], ['dkT.%d.%d' % (h, t) for h in range(4) for t in range(2)])
                dump('dvt_%d' % g, dvt, [128, NKT, 512], dvt_names)
                dump('cqn_%d' % g, cqn, [128, 2, 512], ['cqn'])
                dump('lamc', lamc[:], [128, 4], ['lamc'])
            for o2 in range(4):
                wO, wOn = wload(w_out_ab[:, o2 * 256:(o2 + 1) * 256].rearrange("(kc p) n -> p kc n", p=128), [128, 8, 256])
                bg()
                bg()
                for oo in range(2):
                    oc = o2 * 2 + oo
                    bk = nb()
                    for kc in range(8):
                        rhs = outA[:, kc, :] if kc < 4 else outB[:, kc - 4, :]
                        rn_ = 'outA.%d' % kc if kc < 4 else 'outB.%d' % (kc - 4)
                        P.add('pe', mm(pb[bk][:], wO[:, kc, oo * 128:(oo + 1) * 128], rhs, start=(kc == 0), stop=(kc == 7)),
                              r=[wOn, rn_], w=['pb%d' % bk])
                    P.add('dve', stt(xT[:, oc, tok], pb[bk][:], mcol(0, g, 2, oc), xT[:, oc, tok], ALU.mult, ALU.add),
                          r=['pb%d' % bk, 'modraw0', 'xT'], w=['xT'])
        while pending_ada:
            bg()


    LN8 = math.log(0.125)

    def mixer1(g):
        arena_reset()
        hm = carve([128, 8, NT], BF16)
        qk = carve([128, 8, 1024], BF16)
        vv = carve([128, 8, 1024], BF16)
        so = carve([128, 8, 1024], BF16)
        hf = carve([128, 8, 1024], BF16)
        gt = carve([128, 8, 32])
        nlf = carve([128, 8, 16])
        argA = carve([128, 8, 16])
        argK = carve([128, 16])
        fac = carve([128, 8, 48])
        eG = carve([128, 8, 16])
        eGs = carve([128, 8, 8])
        State = carve([128, 2, 4, 128])
        Nst = carve([128, 2, 4])
        Stb = carve([128, 2, 4, 128], BF16)
        Nstb = carve([128, 2, 4], BF16)
        qp = [carve([128, 512], BF16) for _ in range(2)]
        kp = [carve([128, 512], BF16) for _ in range(2)]
        kpp = [carve([128, 512], BF16) for _ in range(2)]
        qz = [carve([128, 8, 128], BF16) for _ in range(2)]
        kT = [carve([128, 4, 128], BF16) for _ in range(2)]
        ST = [carve([128, 8, 128], BF16) for _ in range(2)]
        dn = carve([128, 2, 8])
        hs32 = carve([128, 1024])
        ssq = carve([128, 16])
        hsn = carve([128, 1024], BF16)
        hsq = hsn
        if g == 0:
            amaxT = carve([16, 8])
            GtT = carve([16, 8])
            mw = carve([16, 12, 4])
            emfbc = carve([128, 16, 4])
            rhsD = carve([16, 16, 4])
            Cst = carve([128, 4, 128])
            nst_o = carve([128, 4])
        else:
            em0 = carve([128, 16])
        hsT = hm
        cnt = {'i': 0}

        norm_mod(xT, hm, 'hm', 1, g, 0)

        def tm_mm(bk, col0, wv, wn, n, j):
            for kc in range(8):
                P.add('pe', mm(pb[bk][:, col0:col0 + n], hm[:, kc, j * 128:(j + 1) * 128], wv[:, kc, 0:n], start=(kc == 0), stop=(kc == 7)),
                      r=[wn, 'hm.%d' % (j // 4)], w=['pb%d' % bk])

        for ld in range(12):
            c0 = ld * 256
            wv, wn = wload(w_in_c[:, c0:c0 + 256].rearrange("(kc p) n -> p kc n", p=128), [128, 8, 256])
            for j in range(8):
                bk = nb()
                tm_mm(bk, 0, wv, wn, 256, j)
                if c0 < 1024:
                    evac(qk[:, j, c0:c0 + 256], pb[bk][:, 0:256], r=['pb%d' % bk], w=['qk.%d.%d' % (j, ld)])
                elif c0 < 2048:
                    evac(vv[:, j, c0 - 1024:c0 - 1024 + 256], pb[bk][:, 0:256], r=['pb%d' % bk], w=['vv.%d.%d' % (j, ld)])
                else:
                    sov = so[:, j, c0 - 2048:c0 - 2048 + 256]
                    P.add('act', act(sov, pb[bk][:, 0:256], AF.Sigmoid), r=['pb%d' % bk], w=['so.%d.%d' % (j, ld)])
                    P.add('pool', tt(sov.rearrange("p (h v) -> p h v", v=128), sov.rearrange("p (h v) -> p h v", v=128),
                                     vec[:, V_GMLB:V_GMLB + 128].unsqueeze(1).to_broadcast([128, 2, 128]), ALU.mult),
                          r=['so.%d.%d' % (j, ld), 'vec'], w=['so.%d.%d' % (j, ld)])
        qk_names = lambda j: ['qk.%d.%d' % (j, ld) for ld in range(4)]
        vv_names = lambda j: ['vv.%d.%d' % (j, ld) for ld in range(4, 8)]
        so_names = lambda j: ['so.%d.%d' % (j, ld) for ld in range(8, 12)]
        if m1_stop <= 1:
            return
        wv, wn = wload(w_in_c[:, 3072:3104].rearrange("(kc p) n -> p kc n", p=128), [128, 8, 32])
        for j in range(8):
            bk = nb()
            tm_mm(bk, 0, wv, wn, 32, j)
            P.add('dve', tt(gt[:, j, :], pb[bk][:, 0:32], vec[:, V_BGB:V_BGB + 32], ALU.add), r=['pb%d' % bk, 'vec'], w=['gt.%d' % j])
            g4 = gt[:, j, :].rearrange("p (a b) -> p a b", b=8)
            nl = nlf[:, j, :].rearrange("p (a b) -> p a b", b=8)
            P.add('act', act(nl, g4[:, 1::2, :], AF.Exp, scale=-1.0), r=['gt.%d' % j], w=['nlf.%d' % j])
            P.add('act', act(nl, nl, AF.Ln, bias=1.0), r=['nlf.%d' % j], w=['nlf.%d' % j])
            bc = nb()
            for k_, (co, d_) in enumerate(((C_TUI, 0), (C_TLS, 0), (C_TLI, 1), (C_TUS, 1))):
                P.add('pe', mm(pb[bc][:, k_ * 8:(k_ + 1) * 8], cst[:, co:co + 128], nlf[:, j, d_ * 8:(d_ + 1) * 8]), r=['cst', 'nlf.%d' % j], w=['pb%d' % bc])
            P.add('pe', mm(pb[bc][:, 32:48], onesf, nlf[:, j, :]), r=['cst', 'nlf.%d' % j], w=['pb%d' % bc])
            p4 = pb[bc][:, 0:32].rearrange("p (a b) -> p a b", b=8)
            f6 = fac[:, j, :].rearrange("p (a b) -> p a b", b=8)
            aK = argK[:, :].rearrange("p (a b) -> p a b", b=8)
            aA = argA[:, j, :].rearrange("p (a b) -> p a b", b=8)
            P.add('dve', tt(aK, g4[:, 0::2, :], p4[:, 0::2, :], ALU.add), r=['gt.%d' % j, 'pb%d' % bc], w=['argK'])
            P.add('dve', tt(aA, g4[:, 0::2, :], p4[:, 1::2, :], ALU.subtract), r=['gt.%d' % j, 'pb%d' % bc], w=['argA.%d' % j])
            P.add('act', act(f6[:, 0::3, :], p4[:, 0::2, :], AF.Exp, scale=-1.0), r=['pb%d' % bc], w=['fac.%d' % j])
            P.add('act', act(eG[:, j, :], pb[bc][:, 32:48], AF.Exp, scale=-1.0), r=['pb%d' % bc], w=['eG.%d' % j])
            P.add('act', act(f6[:, 1::3, :], aK, AF.Exp, bias=LN8), r=['argK'], w=['fac.%d' % j])
            P.add('act', act(f6[:, 2::3, :], aA, AF.Exp, bias=LN8), r=['argA.%d' % j], w=['fac.%d' % j])
            eg4 = eG[:, j, :].rearrange("p (d c e) -> p d c e", c=4, e=2)
            es = eGs[:, j, :].rearrange("p (d c) -> p d c", c=4)
            for e_ in range(2):
                pe_ = slice(e_ * 64, (e_ + 1) * 64)
                P.add('dve', cp(es[pe_], eg4[pe_, :, :, e_]), r=['eG.%d' % j], w=['eGs.%d' % j])

        if m1_stop <= 2:
            return
        if g == 0:
            for half in range(2):
                bk = nb()
                for jj in range(4):
                    j = half * 4 + jj
                    P.add('pe', tr(pb[bk][0:16, jj * 128:(jj + 1) * 128], argA[:, j, :], idf), r=['argA.%d' % j, 'cst'], w=['pb%d' % bk])
                P.add('dve', lambda e, bk=bk, half=half: e.tensor_reduce(out=amaxT[0:16, half * 4:(half + 1) * 4],
                                                                        in_=pb[bk][0:16, :].rearrange("p (a b) -> p a b", b=128), axis=AX.X, op=ALU.max),
                      r=['pb%d' % bk], w=['amaxT'])
            bk = nb()
            for j in range(8):
                P.add('pe', mm(pb[bk][0:16, j:j + 1], nlf[:, j, :], onesf[:, 0:1]), r=['nlf.%d' % j, 'cst'], w=['pb%d' % bk])
            P.add('dve', cp(GtT[0:16, :], pb[bk][0:16, 0:8]), r=['pb%d' % bk], w=['GtT'])
            am = amaxT[0:16, :].rearrange("p (s t) -> p s t", t=2)
            gm = GtT[0:16, :].rearrange("p (s t) -> p s t", t=2)
            a0, a1, G0, G1 = am[:, :, 0], am[:, :, 1], gm[:, :, 0], gm[:, :, 1]
            W = lambda i: mw[0:16, i, :]
            mops = [
                ts(W(0), G0, -1.0, ALU.mult), tt(W(1), a0, W(0), ALU.max), tt(W(2), W(1), G1, ALU.subtract), tt(W(3), a1, W(2), ALU.max),
                ts(W(4), G1, -1.0, ALU.mult), tt(W(5), a1, W(4), ALU.max), tt(W(6), W(5), G0, ALU.subtract), tt(W(7), a0, W(6), ALU.max),
                ts(W(8), W(3), cst[0:16, C_SELF:C_SELF + 1], ALU.mult),
                stt(W(9), W(7), cst[0:16, C_SELB:C_SELB + 1], W(8), ALU.mult, ALU.add),
            ]
            for f_ in mops:
                P.add('dve', f_, r=['amaxT', 'GtT', 'mw', 'cst'], w=['mw'])
            P.add('act', act(W(10), W(9), AF.Exp, scale=-1.0), r=['mw'], w=['mw'])
            P.add('sp', dma(o_m.rearrange("s d h -> (d h) s"), W(9), slow=True), r=['mw'], chan='out_m')
            P.add('dve', tt(rhsD, cst[0:16, C_ID:C_ID + 16].unsqueeze(2).to_broadcast([16, 16, 4]),
                            W(10).unsqueeze(1).to_broadcast([16, 16, 4]), ALU.mult), r=['mw', 'cst'], w=['rhsD'])
            bk = nb()
            P.add('pe', mm(pb[bk][:, 0:64], onesf[0:16, :], rhsD.rearrange("p a b -> p (a b)")), r=['rhsD', 'cst'], w=['pb%d' % bk])
            P.add('dve', cp(emfbc.rearrange("p a b -> p (a b)"), pb[bk][:, 0:64]), r=['pb%d' % bk], w=['emfbc'])
        else:
            P.add('act', act(em0, vec[:, V_M0:V_M0 + 16], AF.Exp), r=['vec'], w=['em0'])

        if m1_stop <= 3:
            return
        seqs = [[2 * s_, 2 * s_ + 1] for s_ in range(4)] if g == 0 else [list(range(8))]
        mask4 = carve([128, 2, 4, 128], BF16)
        for i__ in range(2):
            P.add('pool', mset(qz[i__], 0.0), w=['qzz'])
        for d_ in range(2):
            co = C_TUI if d_ == 0 else C_TLI
            for hh in range(4):
                P.add('pool', cp(mask4[:, d_, hh, :], cst[:, co:co + 128]), r=['cst'], w=['mask4'])
        for si_, tiles in enumerate(seqs):
            ntl = len(tiles)
            for d in range(2):
                sn_ = 'State.%d' % d
                if g == 0:
                    P.add('pool', mset(State[:, d], 0.0), w=[sn_])
                    P.add('pool', mset(Nst[:, d, :], 0.0), w=['Nst.%d' % d])
                else:
                    P.add('sp', dma(State[:, d], c0_d[:, d]), w=[sn_], chan='ld_c0_%d' % d)
                    e4 = em0[:, :].rearrange("p (d c e) -> p d c e", c=4, e=2)
                    for e_ in range(2):
                        pe_ = slice(e_ * 64, (e_ + 1) * 64)
                        P.add('dve', tt(State[pe_, d], State[pe_, d], e4[pe_, d, :, e_].unsqueeze(2).to_broadcast([64, 4, 128]), ALU.mult),
                              r=[sn_, 'em0'], w=[sn_])
                        P.add('dve', tt(Nst[pe_, d, :], vec[pe_, V_N0 + d * 4:V_N0 + d * 4 + 4], e4[pe_, d, :, e_], ALU.mult),
                              r=['vec', 'em0'], w=['Nst.%d' % d])
                P.add('act', acp(Stb[:, d], State[:, d]), r=[sn_], w=['Stb.%d' % d])
                P.add('act', acp(Nstb[:, d, :], Nst[:, d, :]), r=['Nst.%d' % d], w=['Nstb.%d' % d])
            def tile_step(d, j, k_, ntl=ntl, si_=si_):
                        i_ = d
                        store_h = (k_ < ntl // 2)
                        sn_ = 'State.%d' % d
                        f6 = fac[:, j, :].rearrange("p (a b) -> p a b", b=8)
                        q3 = qk[:, j, 0:512].rearrange("p (h k) -> p h k", k=64)
                        k3 = qk[:, j, 512:1024].rearrange("p (h k) -> p h k", k=64)
                        bc3 = lambda a: f6[:, a, :].unsqueeze(2).to_broadcast([128, 8, 64])
                        P.add('dve', tt(qp[i_].rearrange("p (h k) -> p h k", k=64), q3, bc3(3 * d), ALU.mult), r=qk_names(j) + ['fac.%d' % j], w=['qp%d' % i_])
                        P.add('pool', tt(kp[i_].rearrange("p (h k) -> p h k", k=64), k3, bc3(3 * d + 1), ALU.mult), r=qk_names(j) + ['fac.%d' % j], w=['kp%d' % i_])
                        P.add('dve', tt(kpp[i_].rearrange("p (h k) -> p h k", k=64), k3, bc3(3 * d + 2), ALU.mult), r=qk_names(j) + ['fac.%d' % j], w=['kpp%d' % i_])
                        bt = nb()
                        for c in range(4):
                            P.add('pe', tr(pbb[bt][:, c * 128:(c + 1) * 128], qp[i_][:, c * 128:(c + 1) * 128], idb), r=['qp%d' % i_, 'cstb'], w=['pb%d' % bt])
                        for c in range(4):
                            P.add('pe', tr(pbb[bt][:, (4 + c) * 128:(5 + c) * 128], kp[i_][:, c * 128:(c + 1) * 128], idb), r=['kp%d' % i_, 'cstb'], w=['pb%d' % bt])
                        pt3 = pbb[bt][:, :].rearrange("p (a b) -> p a b", b=128)
                        P.add('act', acp(qz[i_][0:64, 0::2, :], pt3[0:64, 0:4, :]), r=['pb%d' % bt, 'qzz'], w=['qkT%d' % i_])
                        P.add('act', acp(qz[i_][64:128, 1::2, :], pt3[64:128, 0:4, :]), r=['pb%d' % bt, 'qzz'], w=['qkT%d' % i_])
                        P.add('act', acp(kT[i_][:, :, :], pt3[:, 4:8, :]), r=['pb%d' % bt], w=['qkT%d' % i_])
                        yield
                        if m1_stop <= 4:
                            return
                        bsp = [nb(), nb()]
                        for h in range(8):
                            c, e_ = h // 2, h % 2
                            pe_ = slice(e_ * 64, (e_ + 1) * 64)
                            P.add('pe', mm(pb[bsp[e_]][:, c * 128:(c + 1) * 128], kT[i_][:, c, :], qz[i_][:, h, :]), r=['qkT%d' % i_], w=['pb%d' % bsp[e_]])
                        for e_ in range(2):
                            P.add('dve', tt(ST[i_][:, e_::2, :], pb[bsp[e_]][:].rearrange("p (h l) -> p h l", l=128),
                                            mask4[:, d], ALU.mult), r=['pb%d' % bsp[e_], 'mask4'], w=['ST%d.%d' % (i_, e_)])
                        yield
                        if m1_stop <= 5:
                            return
                        bd = nb()
                        bnum = []
                        for h4 in range(2):
                            bn_ = nb()
                            bnum.append(bn_)
                            for hh in range(4):
                                h = h4 * 4 + hh
                                c, e_ = h // 2, h % 2
                                pe_ = slice(e_ * 64, (e_ + 1) * 64)
                                P.add('pe', mm(pb[bn_][:, hh * 128:(hh + 1) * 128], ST[i_][:, h, :], vv[:, j, h * 128:(h + 1) * 128], start=True, stop=False),
                                      r=['ST%d.%d' % (i_, h % 2)] + vv_names(j), w=['pb%d' % bn_])
                                P.add('pe', mm(pb[bn_][:, hh * 128:(hh + 1) * 128], qz[i_][:, h, :], Stb[:, d, c, :], start=False, stop=True),
                                      r=['qkT%d' % i_, 'Stb.%d' % d], w=['pb%d' % bn_])
                        for h in range(8):
                            c, e_ = h // 2, h % 2
                            pe_ = slice(e_ * 64, (e_ + 1) * 64)
                            P.add('pe', mm(pb[bd][:, h:h + 1], ST[i_][:, h, :], onesb[:, 0:1], start=True, stop=False),
                                  r=['ST%d.%d' % (i_, h % 2), 'cstb'], w=['pb%d' % bd])
                            P.add('pe', mm(pb[bd][:, h:h + 1], qz[i_][:, h, :], Nstb[:, d, c:c + 1], start=False, stop=True),
                                  r=['qkT%d' % i_, 'Nstb.%d' % d], w=['pb%d' % bd])
                        yield
                        P.add('act', act(dn[:, i_, :], pb[bd][:, 0:8], AF.Abs), r=['pb%d' % bd], w=['dn%d' % i_])
                        P.add('dve', ts(dn[:, i_, :], dn[:, i_, :], 1.0, ALU.max), r=['dn%d' % i_], w=['dn%d' % i_])
                        P.add('dve', rcp(dn[:, i_, :], dn[:, i_, :]), r=['dn%d' % i_], w=['dn%d' % i_])
                        for h4 in range(2):
                            num3 = pb[bnum[h4]][:].rearrange("p (h v) -> p h v", v=128)
                            rd = dn[:, i_, h4 * 4:(h4 + 1) * 4].unsqueeze(2).to_broadcast([128, 4, 128])
                            if store_h:
                                P.add('dve', tt(hf[:, j, h4 * 512:(h4 + 1) * 512].rearrange("p (h v) -> p h v", v=128), num3, rd, ALU.mult),
                                      r=['pb%d' % bnum[h4], 'dn%d' % i_], w=['hf.%d.%d' % (j, h4)])
                            else:
                                P.add('dve', tt(hs32[:, h4 * 512:(h4 + 1) * 512].rearrange("p (h v) -> p h v", v=128), num3, rd, ALU.mult),
                                      r=['pb%d' % bnum[h4], 'dn%d' % i_], w=['hs32.%d' % h4])
                                P.add('pool' if h4 == 0 else 'dve', tt(hs32[:, h4 * 512:(h4 + 1) * 512], hs32[:, h4 * 512:(h4 + 1) * 512], hf[:, j, h4 * 512:(h4 + 1) * 512], ALU.add),
                                      r=['hs32.%d' % h4, 'hf.%d.%d' % (j, h4)], w=['hs32.%d' % h4])
                        if m1_stop <= 6:
                            return
                        bkv = []
                        for b_ in range(2):
                            bk = nb()
                            bkv.append(bk)
                            for cc in range(2):
                                c = b_ * 2 + cc
                                P.add('pe', mm(pb[bk][:, cc * 256:(cc + 1) * 256], kpp[i_][:, c * 128:(c + 1) * 128], vv[:, j, c * 256:(c + 1) * 256]),
                                      r=['kpp%d' % i_] + vv_names(j), w=['pb%d' % bk])
                        bkn = nb()
                        for c in range(4):
                            P.add('pe', mm(pb[bkn][:, c:c + 1], kpp[i_][:, c * 128:(c + 1) * 128], onesb[:, 0:1]), r=['kpp%d' % i_, 'cstb'], w=['pb%d' % bkn])
                        es = eGs[:, j, :].rearrange("p (d c) -> p d c", c=4)
                        P.add('dve', tt(State[:, d], State[:, d], es[:, d, :].unsqueeze(2).to_broadcast([128, 4, 128]), ALU.mult),
                              r=[sn_, 'eGs.%d' % j, 'Stb.%d' % d], w=[sn_])
                        for b_ in range(2):
                            for e_ in range(2):
                                pe_ = slice(e_ * 64, (e_ + 1) * 64)
                                kvv = pb[bkv[b_]][pe_, :].rearrange("p (c e v) -> p c e v", e=2, v=128)[:, :, e_, :]
                                P.add('dve', tt(State[pe_, d, b_ * 2:b_ * 2 + 2, :], State[pe_, d, b_ * 2:b_ * 2 + 2, :], kvv, ALU.add),
                                      r=[sn_, 'pb%d' % bkv[b_]], w=[sn_])
                        P.add('pool', tt(Nst[:, d, :], Nst[:, d, :], es[:, d, :], ALU.mult), r=['Nst.%d' % d, 'eGs.%d' % j], w=['Nst.%d' % d])
                        P.add('dve', tt(Nst[:, d, :], Nst[:, d, :], pb[bkn][:, 0:4], ALU.add), r=['Nst.%d' % d, 'pb%d' % bkn], w=['Nst.%d' % d])
                        P.add('act', acp(Stb[:, d], State[:, d]), r=[sn_], w=['Stb.%d' % d])
                        P.add('act', acp(Nstb[:, d, :], Nst[:, d, :]), r=['Nst.%d' % d], w=['Nstb.%d' % d])
                        if m1_stop <= 7:
                            return
                        if not store_h:
                            P.add('act', act(hsq[:], hs32[:], AF.Square), r=['hs32.0', 'hs32.1'], w=['hsn'] + ['hsn.%d' % h_ for h_ in range(8)])
                            P.add('dve', lambda e: e.tensor_reduce(out=ssq[:, 0:8], in_=hsq[:].rearrange("p (h v) -> p h v", v=128), axis=AX.X, op=ALU.add),
                                  r=['hsn'], w=['ssq'])
                            P.add('act', act(ssq[:, 8:16], ssq[:, 0:8], AF.Ln, scale=1.0 / 128, bias=EPS), r=['ssq'], w=['ssq'])
                            P.add('act', act(ssq[:, 8:16], ssq[:, 8:16], AF.Exp, scale=-0.5), r=['ssq'], w=['ssq'])
                            h3 = hs32[:].rearrange("p (h v) -> p h v", v=128)
                            for h_ in range(8):
                                P.add('dve',
                                      stt(hsn[:, h_ * 128:(h_ + 1) * 128], hs32[:, h_ * 128:(h_ + 1) * 128], ssq[:, 8 + h_:9 + h_],
                                          so[:, j, h_ * 128:(h_ + 1) * 128], ALU.mult, ALU.mult),
                                      r=['hs32.0', 'hs32.1', 'ssq'] + so_names(j), w=['hsn.%d' % h_, 'hsnr.%d' % h_])
                            bt = nb()
                            for c in range(8):
                                P.add('pe', tr(pbb[bt][:, c * 128:(c + 1) * 128], hsn[:, c * 128:(c + 1) * 128], idb), r=['hsn.%d' % c, 'cstb'], w=['pb%d' % bt])
                            P.add('act', acp(hsT[:, :, j * 128:(j + 1) * 128], pbb[bt][:, :].rearrange("p (c t) -> p c t", t=128)),
                                  r=['pb%d' % bt], w=['hm.%d' % (j // 4)])
            for k_ in range(ntl):
                gens = [tile_step(0, tiles[k_], k_), tile_step(1, tiles[ntl - 1 - k_], k_)]
                while gens:
                    for gn in list(gens):
                        try:
                            next(gn)
                        except StopIteration:
                            gens.remove(gn)
            if g == 0 and m1_stop > 8:
                for d in range(2):
                    for e_ in range(2):
                        pe_ = slice(e_ * 64, (e_ + 1) * 64)
                        emsel = emfbc[pe_, d * 8 + e_:d * 8 + 8:2, si_]
                        P.add('dve', tt(Cst[pe_], State[pe_, d], emsel.unsqueeze(2).to_broadcast([64, 4, 128]), ALU.mult),
                              r=['State.%d' % d, 'emfbc'], w=['Cst'])
                        P.add('dve', tt(nst_o[pe_, :], Nst[pe_, d, :], emsel, ALU.mult), r=['Nst.%d' % d, 'emfbc'], w=['nst_o'])
                        P.add('sp', dma(o_C[si_, d].rearrange("(c e) k v -> e k c v", e=2)[e_], Cst[pe_]), r=['Cst'], chan='out_C')
                        P.add('sp', dma(o_n[si_, d].rearrange("(c e) k -> e k c", e=2)[e_], nst_o[pe_, :], slow=True), r=['nst_o'], chan='out_n')

        for o2 in range(4):
            wO, wOn = wload(w_out_c[:, o2 * 256:(o2 + 1) * 256].rearrange("(kc p) n -> p kc n", p=128), [128, 8, 256])
            for oo in range(2):
                oc = o2 * 2 + oo
                for t_ in range(2):
                    tok = slice(t_ * 512, (t_ + 1) * 512)
                    bk = nb()
                    for kc in range(8):
                        P.add('pe', mm(pb[bk][:], wO[:, kc, oo * 128:(oo + 1) * 128], hsT[:, kc, tok], start=(kc == 0), stop=(kc == 7)),
                              r=[wOn, 'hm.%d' % t_], w=['pb%d' % bk])
                    P.add('dve', stt(xT[:, oc, tok], pb[bk][:], mcol(1, g, 2, oc), xT[:, oc, tok], ALU.mult, ALU.add),
                          r=['pb%d' % bk, 'modraw1', 'xT'], w=['xT'])

    def load_x(g, reset=True):
        if reset:
            arena_reset()
        xin = [carve([128, D]) for _ in range(2)]
        for j in range(8):
            s_ = j % 2
            P.add('sp', dma(xin[s_], xg[g][j * 128:(j + 1) * 128, :]), w=['xin%d' % s_], chan='xin%d' % s_)
            for half in range(2):
                bk = nb()
                for c4 in range(4):
                    c = half * 4 + c4
                    P.add('pe', tr(pb[bk][:, c4 * 128:(c4 + 1) * 128], xin[s_][:, c * 128:(c + 1) * 128], idf),
                          r=['xin%d' % s_, 'cst'], w=['pb%d' % bk])
                evac(xT[:, half * 4:(half + 1) * 4, j * 128:(j + 1) * 128], pb[bk][:].rearrange("p (c t) -> p c t", c=4),
                     r=['pb%d' % bk], w=['xT'])

    def store_y(g):
        arena_reset()
        yT = carve([128, 8, NT])
        yo = [carve([128, D]) for _ in range(2)]
        norm_mod(xT, yT, 'yT', 0, g, 0, final=True)
        for j in range(8):
            s_ = j % 2
            for half in range(2):
                bk = nb()
                for c4 in range(4):
                    c = half * 4 + c4
                    P.add('pe', tr(pb[bk][:, c4 * 128:(c4 + 1) * 128], yT[:, c, j * 128:(j + 1) * 128], idf),
                          r=['yT.%d' % (j // 4), 'cst'], w=['pb%d' % bk])
                evac(yo[s_][:, half * 512:(half + 1) * 512], pb[bk][:], r=['pb%d' % bk], w=['yo%d' % s_])
            P.add('pool', dma(y_d[g][j * 128:(j + 1) * 128, :], yo[s_]), r=['yo%d' % s_], chan='out_y%d' % s_)

    for gi_, g in enumerate(groups):
        if gi_ == 0:
            load_x(g)
        if g == groups[0]:
            ada_slots0 = [carve([128, 8, 256], BF16) for _ in range(2)]
            for f_ in adaln_closures(0):
                f_(ada_slots0)
            pending_ada.extend(adaln_closures(1))
            if 'm0' not in parts:
                while pending_ada:
                    pending_ada.pop(0)(ada_slots0)
        if 'm0' in parts:
            mixer0(g)
            if dbg:
                dump('xm0_%d' % g, xT[:], [128, 8, NT], ['xT'])
        if 'f0' in parts:
            ffn(0, g)
        if 'm1' in parts:
            mixer1(g)
            if dbg:
                dump('xm1_%d' % g, xT[:], [128, 8, NT], ['xT'])
        if 'f1' in parts:
            ffn(1, g)
        store_y(g)
        if gi_ + 1 < len(groups):
            load_x(groups[gi_ + 1], reset=False)

    stats = P.emit()
    return nc, stats, P


def _prep_inputs(inp, ncores=8):
    f = np.float32
    cst = _consts()
    c32, s32, c128, s128 = _rope_tables()
    rope = np.zeros((4, 128, 1024), f)
    rope[0, 0:32] = c32
    rope[1, 0:32] = s32
    rope[2] = c128
    rope[3] = s128
    w_in_ab = np.ascontiguousarray(inp['w_in_ab'][0])
    perm_kr = 384 + _swap_perm(32, 8)
    perm_dq = 416 + _swap_perm(512, 16)
    perm_dk = 928 + _swap_perm(512, 16)
    w_in_sw = np.ascontiguousarray(w_in_ab[:, np.concatenate([perm_kr, perm_dq, perm_dk])])
    wuq = inp['w_uq'][0].reshape(256, 8, 96)
    nope = wuq[:, :, 0:64].reshape(256, 512)
    rp = wuq[:, :, 64:96].reshape(256, 256)
    rp_sw = rp[:, _swap_perm(256, 8)]
    w_uq = np.ascontiguousarray(np.concatenate([nope, rp, rp_sw], 1))
    wukv = inp['w_ukv'][0].reshape(128, 8, 128)
    w_ukv = np.ascontiguousarray(np.concatenate([wukv[:, :, 0:64].reshape(128, 512), wukv[:, :, 64:128].reshape(128, 512)], 1))
    shared = dict(
        cst=cst, rope=rope,
        w_ada=np.ascontiguousarray(inp['w_ada']), w_ffn_in=np.ascontiguousarray(inp['w_ffn_in']),
        w_ffn_out=np.ascontiguousarray(inp['w_ffn_out']), w_in_ab=w_in_ab, w_in_sw=w_in_sw, w_uq=w_uq, w_ukv=w_ukv,
        w_out_ab=np.ascontiguousarray(inp['w_out_ab'][0]), w_in_c=np.ascontiguousarray(inp['w_in_c'][0]),
        w_out_c=np.ascontiguousarray(inp['w_out_c'][0]))

    def fm(v):
        return np.ascontiguousarray(v.reshape(-1, 128).T)

    maps = []
    for i in range(ncores):
        vec = np.zeros((128, NVEC), f)
        vec[:, V_COND:V_COND + 8] = fm(inp['c_ctx'])
        vec[:, V_COND + 8:V_COND + 16] = fm(inp['c'][i])
        for l in range(2):
            vec[:, V_BADA + l * 48:V_BADA + (l + 1) * 48] = fm(inp['b_ada'][l])
        gl = [inp['g_mix'][0], inp['g_ffn'][0], inp['g_mix'][1], inp['g_ffn'][1], inp['g_final']]
        for k, gv in enumerate(gl):
            vec[:, V_G + k * 8:V_G + (k + 1) * 8] = fm(gv)
        vec[:, V_GQL:V_GQL + 2] = fm(inp['g_q_lora'][0])
        vec[:, V_GKV:V_GKV + 1] = fm(inp['g_kv_lora'][0])
        vec[:, V_GSUB:V_GSUB + 1] = fm(inp['g_diff_subln'][0])
        vec[:, V_GKVB:V_GKVB + 128] = inp['g_kv_lora'][0][None, :]
        vec[:, V_GMLB:V_GMLB + 128] = inp['g_mlstm'][0][None, :]
        vec[:, V_BGB:V_BGB + 32] = inp['b_gate_c'][0][None, :]
        vec[:, V_LAMB:V_LAMB + 256] = inp['diff_lambda'][0].reshape(1, 256)
        n0 = inp['state_mlstm_n'][i, 0]
        vec[:, V_N0:V_N0 + 8] = n0.reshape(2, 4, 2, 64).transpose(2, 3, 0, 1).reshape(128, 8)
        vec[:, V_M0:V_M0 + 16] = inp['state_mlstm_m'][i, 0].reshape(1, 16)
        c0 = inp['state_mlstm_C'][i, 0]
        c0 = np.ascontiguousarray(c0.reshape(2, 4, 2, 64, 128).transpose(2, 3, 0, 1, 4).reshape(128, 2, 4, 128))
        m = dict(shared)
        m.update(
            xg=np.ascontiguousarray(np.stack([inp['x_prompt'][4 * i:4 * i + 4].reshape(NT, D), inp['x_sample'][i]])),
            vec=vec, c0=c0,
            ckv_c=np.ascontiguousarray(inp['cache_mla_ckv'][i, 0]), kr_c=np.ascontiguousarray(inp['cache_mla_krope'][i, 0]),
            dk_c=np.ascontiguousarray(inp['cache_diff_k'][i, 0]), dv_c=np.ascontiguousarray(inp['cache_diff_v'][i, 0]))
        maps.append(m)
    return maps


_CACHE = {}


def kernel(**inputs):
    inp = {k: np.asarray(v, dtype=np.float32) for k, v in inputs.items()}
    if 'nc' not in _CACHE:
        _CACHE['nc'] = build()[0]
    nc = _CACHE['nc']
    maps = _prep_inputs(inp, 8)
    res = run_bass_kernel_spmd(nc, maps, core_ids=list(range(8)))
    R = res.results
    y_prompt = np.concatenate([r['y'][0].reshape(4, 256, D) for r in R], 0)
    y_sample = np.stack([r['y'][1] for r in R], 0)
    new_ckv = np.concatenate([r['o_ckv'].reshape(4, 1, 256, 128) for r in R], 0)
    new_kr = np.concatenate([r['o_kr'].reshape(4, 1, 256, 32) for r in R], 0)
    new_dk = np.concatenate([r['o_dk'].reshape(4, 1, 4, 256, 128) for r in R], 0)
    new_dv = np.concatenate([r['o_dv'].reshape(4, 1, 4, 256, 128) for r in R], 0)
    new_C = np.concatenate([r['o_C'].reshape(4, 1, 2, 8, 64, 128) for r in R], 0)
    new_n = np.concatenate([r['o_n'].reshape(4, 1, 2, 8, 64) for r in R], 0)
    new_m = np.concatenate([r['o_m'].reshape(4, 1, 2, 8) for r in R], 0)
    return (y_prompt, y_sample, new_ckv, new_kr, new_dk, new_dv, new_C, new_n, new_m)
```
